# Optimizing a Trainium2 kernel written in Bass

```python
import jax, jax.numpy as jnp
from jax import lax
import numpy as np

D_MODEL = 1024
BATCH = 4
SEQ = 4096
DEPTH = 1

CHUNK = 64
Q_BLOCK = 128
ML_HEADS = 4
ML_DQK = D_MODEL // 8
ML_DV = D_MODEL // 4
ML_QK = ML_HEADS * ML_DQK
ML_V = ML_HEADS * ML_DV
FOX_HEADS = 8
FOX_DH = D_MODEL // 8
FOX_W = FOX_HEADS * FOX_DH
D_FF = 2816
CONV_W = 3
GATE_CAP = 15.0
EPS = 1e-6
SPLITS = (ML_QK, ML_QK, ML_V, ML_HEADS, ML_HEADS, ML_V,
          FOX_W, FOX_W, FOX_W, FOX_HEADS, D_MODEL, D_MODEL)
D_IN = 2 * ML_QK + 2 * ML_V + 2 * ML_HEADS + 3 * FOX_W + FOX_HEADS + 2 * D_MODEL

kernel_name = "hybrid_mlstm_fox_convffn"


def rmsnorm(x, w):
    xf = x.astype(jnp.float32)
    y = xf * lax.rsqrt(jnp.mean(xf * xf, axis=-1, keepdims=True) + EPS)
    return (y * w.astype(jnp.float32)).astype(x.dtype)


def softcap(z):
    return GATE_CAP * jnp.tanh(z / GATE_CAP)


def causal_dwconv(u, w, b):
    S = u.shape[1]
    up = jnp.pad(u, ((0, 0), (CONV_W - 1, 0), (0, 0)))
    y = b
    for j in range(CONV_W):
        y = y + w[j] * up[:, j:j + S]
    return y


def mlstm_chunkwise(q, k, v, log_i, log_f):
    B, S, H, DK = q.shape
    DV = v.shape[-1]
    NC = S // CHUNK
    q = q.astype(jnp.float32) * (DK ** -0.5)
    k = k.astype(jnp.float32)
    v = v.astype(jnp.float32)

    def to_chunks(t):
        t = t.reshape((B, NC, CHUNK, H) + t.shape[3:])
        return jnp.moveaxis(t, (1, 3), (0, 2))

    causal = jnp.tril(jnp.ones((CHUNK, CHUNK), dtype=bool))

    def step(carry, xs):
        C, n, m = carry
        qc, kc, vc, li, lf = xs
        b = jnp.cumsum(lf, axis=-1)
        dmat = b[..., :, None] - b[..., None, :] + li[..., None, :]
        dmat = jnp.where(causal, dmat, -jnp.inf)
        inter = b + m[..., None]
        m_t = jnp.maximum(inter, jnp.max(dmat, axis=-1))
        w_intra = jnp.exp(dmat - m_t[..., None])
        w_inter = jnp.exp(inter - m_t)
        s = jnp.einsum('bhtd,bhsd->bhts', qc, kc) * w_intra
        num = (jnp.einsum('bhts,bhsv->bhtv', s, vc)
               + w_inter[..., None] * jnp.einsum('bhvd,bhtd->bhtv', C, qc))
        den = jnp.sum(s, axis=-1) + w_inter * jnp.einsum('bhd,bhtd->bht', n, qc)
        h = num / jnp.maximum(jnp.abs(den), jnp.exp(-m_t))[..., None]
        b_last = b[..., -1]
        g = b_last[..., None] - b + li
        m_new = jnp.maximum(b_last + m, jnp.max(g, axis=-1))
        a = jnp.exp(b_last + m - m_new)
        wk = jnp.exp(g - m_new[..., None])
        C_new = a[..., None, None] * C + jnp.einsum('bhs,bhsv,bhsd->bhvd', wk, vc, kc)
        n_new = a[..., None] * n + jnp.einsum('bhs,bhsd->bhd', wk, kc)
        return (C_new, n_new, m_new), h

    init = (jnp.zeros((B, H, DV, DK), jnp.float32),
            jnp.zeros((B, H, DK), jnp.float32),
            jnp.zeros((B, H), jnp.float32))
    xs = (to_chunks(q), to_chunks(k), to_chunks(v), to_chunks(log_i), to_chunks(log_f))
    _, hs = lax.scan(step, init, xs)
    return jnp.moveaxis(hs, (0, 2), (1, 3)).reshape(B, S, H, DV)


def forgetting_attention(q, k, v, log_f):
    B, S, H, D = q.shape
    nb = S // Q_BLOCK
    F = jnp.moveaxis(jnp.cumsum(log_f, axis=1), 1, 2)
    qb = q.reshape(B, nb, Q_BLOCK, H, D).transpose(1, 0, 3, 2, 4)
    Fb = F.reshape(B, H, nb, Q_BLOCK).transpose(2, 0, 1, 3)
    kpos = jnp.arange(S)
    scale = D ** -0.5

    def block(args):
        qi, Fi, i = args
        logits = (jnp.einsum('bhqd,bshd->bhqs', qi, k).astype(jnp.float32) * scale
                  + Fi[..., None] - F[:, :, None, :])
        qpos = i * Q_BLOCK + jnp.arange(Q_BLOCK)
        logits = jnp.where(kpos[None, :] <= qpos[:, None], logits, -jnp.inf)
        p = jax.nn.softmax(logits, axis=-1)
        return jnp.einsum('bhqs,bshd->bqhd', p.astype(v.dtype), v)

    out = lax.map(block, (qb, Fb, jnp.arange(nb)))
    return out.transpose(1, 0, 2, 3, 4).reshape(B, S, H, D)


def setup_inputs(seed: int = 0) -> dict:
    key = jax.random.key(seed)
    ks = jax.random.split(key, 19)
    f32 = jnp.float32

    def dense(k, fan_in, shape):
        return jax.random.normal(k, (DEPTH,) + shape, f32) * fan_in ** -0.5

    def around(k, n, center):
        return center + 0.05 * jax.random.normal(k, (DEPTH, n), f32)

    return {
        "x": jax.random.normal(ks[0], (BATCH, SEQ, D_MODEL), f32),
        "norm_mix_pre": around(ks[1], D_MODEL, 1.0),
        "w_in": dense(ks[2], D_MODEL, (D_MODEL, D_IN)),
        "b_ml_i": around(ks[3], ML_HEADS, 0.0),
        "b_ml_f": around(ks[4], ML_HEADS, jnp.linspace(3.0, 6.0, ML_HEADS)),
        "ml_head_norm": around(ks[5], ML_V, 1.0),
        "b_fox_f": around(ks[6], FOX_HEADS, 2.0),
        "b_gate_a": around(ks[7], D_MODEL, 0.0),
        "b_gate_b": around(ks[8], D_MODEL, 0.0),
        "w_branch_a": dense(ks[9], ML_V, (ML_V, D_MODEL)),
        "w_branch_b": dense(ks[10], FOX_W, (FOX_W, D_MODEL)),
        "w_out": dense(ks[11], D_MODEL, (D_MODEL, D_MODEL)),
        "norm_mix_post": around(ks[12], D_MODEL, 1.0),
        "norm_ffn_pre": around(ks[13], D_MODEL, 1.0),
        "w_up": dense(ks[14], D_MODEL, (D_MODEL, 2 * D_FF)),
        "conv_w": dense(ks[15], CONV_W, (CONV_W, 2 * D_FF)),
        "conv_b": around(ks[16], 2 * D_FF, 0.0),
        "w_down": dense(ks[17], D_FF, (D_FF, D_MODEL)),
        "norm_ffn_post": around(ks[18], D_MODEL, 1.0),
    }


def reference(x, norm_mix_pre, w_in, b_ml_i, b_ml_f, ml_head_norm, b_fox_f, b_gate_a, b_gate_b,
              w_branch_a, w_branch_b, w_out, norm_mix_post, norm_ffn_pre, w_up, conv_w, conv_b,
              w_down, norm_ffn_post):
    B, S, _ = x.shape
    f32 = jnp.float32
    offsets = np.cumsum(SPLITS)[:-1].tolist()
    for l in range(DEPTH):
        h = rmsnorm(x, norm_mix_pre[l])
        proj = h @ w_in[l]
        (q_m, k_m, v_m, i_m, f_m, o_m,
         q_f, k_f, v_f, f_f, g_a, g_b) = jnp.split(proj, offsets, axis=-1)

        log_i = softcap(i_m.astype(f32) + b_ml_i[l])
        log_f = jax.nn.log_sigmoid(softcap(f_m.astype(f32) + b_ml_f[l]))
        h_a = mlstm_chunkwise(q_m.reshape(B, S, ML_HEADS, ML_DQK),
                              k_m.reshape(B, S, ML_HEADS, ML_DQK),
                              v_m.reshape(B, S, ML_HEADS, ML_DV), log_i, log_f)
        h_a = rmsnorm(h_a, ml_head_norm[l].reshape(ML_HEADS, ML_DV))
        h_a = h_a.reshape(B, S, ML_V).astype(x.dtype) * jax.nn.sigmoid(o_m)
        y_a = h_a @ w_branch_a[l]

        log_fg = jax.nn.log_sigmoid(f_f.astype(f32) + b_fox_f[l])
        h_b = forgetting_attention(q_f.reshape(B, S, FOX_HEADS, FOX_DH),
                                   k_f.reshape(B, S, FOX_HEADS, FOX_DH),
                                   v_f.reshape(B, S, FOX_HEADS, FOX_DH), log_fg)
        y_b = h_b.reshape(B, S, FOX_W) @ w_branch_b[l]

        merged = jax.nn.sigmoid(g_a + b_gate_a[l]) * y_a + jax.nn.sigmoid(g_b + b_gate_b[l]) * y_b
        x = x + rmsnorm(merged @ w_out[l], norm_mix_post[l])

        h = rmsnorm(x, norm_ffn_pre[l])
        u = causal_dwconv(h @ w_up[l], conv_w[l], conv_b[l])
        a, g = jnp.split(u, 2, axis=-1)
        x = x + rmsnorm((jax.nn.gelu(g) * a) @ w_down[l], norm_ffn_post[l])
    return x
```

```python
from contextlib import ExitStack

import numpy as np
import concourse.bass as bass
import concourse.mybir as mybir
from concourse.bass_utils import run_bass_kernel_spmd

F32 = mybir.dt.float32
BF16 = mybir.dt.bfloat16
AF = mybir.ActivationFunctionType
ALU = mybir.AluOpType

P = 128
D = 1024
KC = 8
NB = 32
NPRE = 15
NEXT = 17
TPRE = NPRE * P
TEXT = NEXT * P
D_IN = 8208
D_FF = 2816
NFC = 22
EPS = 1e-6
OFF_QM, OFF_KM, OFF_VM, OFF_IM, OFF_FM, OFF_OM = 0, 512, 1024, 2048, 2052, 2056
OFF_QF, OFF_KF, OFF_VF, OFF_FF, OFF_GA, OFF_GB = 3080, 4104, 5128, 6152, 6160, 7184
MASKNEG = -30000.0
EXT_TILES = [(0, 1), (1, 4), (5, 4), (9, 4), (13, 4)]
ALL = "__all__"
ARENA_KIB = 192


class Unit:
    __slots__ = ("name", "lo", "hi", "w", "r", "acc", "accw", "aliases")

    def __init__(self, name, lo=None, hi=None):
        self.name, self.lo, self.hi = name, lo, hi
        self.w, self.r, self.acc, self.accw, self.aliases = {}, {}, {}, {}, []


class Op:
    __slots__ = ("eng", "fn", "deps", "semkey", "order", "signals", "value", "dma", "name")


class Prog:
    ENGS = ("pe", "act", "dve", "pool", "sp")

    def __init__(self):
        self.eng_ops = {e: [] for e in self.ENGS}
        self.units = []
        self.dma_cnt = {}
        self.nops = 0

    def unit(self, name, lo=None, hi=None):
        u = Unit(name, lo, hi)
        if lo is not None:
            for o in self.units:
                if o.lo is not None and o.lo < hi and lo < o.hi:
                    o.aliases.append(u)
                    u.aliases.append(o)
        self.units.append(u)
        return u

    @staticmethod
    def _norm(lst):
        out = []
        for x in lst:
            if isinstance(x, Unit):
                out.append((x, ALL))
            elif isinstance(x, tuple):
                out.append(x)
            else:
                out.append((x.u, ALL))
        return out

    def add(self, eng, fn, reads=(), writes=(), dma=None, name=""):
        op = Op()
        op.eng, op.fn, op.name, op.dma = eng, fn, name, dma
        op.signals, op.value = False, None
        if dma is None:
            op.semkey = ("E", eng)
            op.order = len(self.eng_ops[eng])
        else:
            op.semkey = ("D", dma)
            self.dma_cnt[dma] = self.dma_cnt.get(dma, 0) + 1
            op.order = self.dma_cnt[dma]
        deps = {}

        def need(o):
            if o is None:
                return
            c = deps.get(o.semkey)
            if c is None or o.order > c.order:
                deps[o.semkey] = o

        reads, writes = self._norm(reads), self._norm(writes)
        for (u, sk) in reads:
            if sk is ALL:
                for o in u.w.values():
                    need(o)
            else:
                need(u.w.get(sk))
                need(u.w.get(ALL))
            for a in u.aliases:
                for o in a.accw.values():
                    need(o)
        for (u, sk) in writes:
            if sk is ALL:
                for o in u.w.values():
                    need(o)
                for d in u.r.values():
                    for o in d.values():
                        need(o)
            else:
                need(u.w.get(sk))
                need(u.w.get(ALL))
                for o in u.r.get(sk, {}).values():
                    need(o)
                for o in u.r.get(ALL, {}).values():
                    need(o)
            for a in u.aliases:
                for o in a.acc.values():
                    need(o)
        if eng == "pe" and dma is None:
            deps.pop(("E", "pe"), None)
        op.deps = []
        for sk_, o in deps.items():
            if o is op:
                continue
            o.signals = True
            val = None
            if sk_[0] == "D":
                val = 16 * (self.dma_cnt[sk_[1]] - (1 if sk_ == op.semkey else 0))
            op.deps.append((sk_, o, val))
        for (u, sk) in reads:
            u.r.setdefault(sk, {})[op.semkey] = op
            u.acc[op.semkey] = op
        for (u, sk) in writes:
            if sk is ALL:
                u.w = {ALL: op}
                u.r = {}
            else:
                u.w[sk] = op
                u.r[sk] = {}
            u.acc[op.semkey] = op
            u.accw[op.semkey] = op
        self.eng_ops[eng].append(op)
        self.nops += 1
        return op

    def emit(self, nc, stack):
        sems = {}

        def sem(key):
            if key not in sems:
                sems[key] = stack.enter_context(nc.semaphore("s_%s_%s" % (key[0], key[1])))
            return sems[key]

        for e in self.ENGS:
            n = 0
            for op in self.eng_ops[e]:
                if op.dma is None and op.signals:
                    n += 1
                    op.value = n
        engobj = {"pe": "tensor", "act": "scalar", "dve": "vector", "pool": "gpsimd", "sp": "sync"}
        with nc.Block() as block:
            for e in self.ENGS:
                ops = self.eng_ops[e]
                if not ops:
                    continue

                def body(eng, ops=ops):
                    waited = {}
                    for op in ops:
                        for (sk_, o, val) in op.deps:
                            v = val if sk_[0] == "D" else o.value
                            if waited.get(sk_, 0) >= v:
                                continue
                            eng.wait_ge(sem(sk_), v)
                            waited[sk_] = v
                        ins = op.fn(eng)
                        if op.dma is not None:
                            ins.then_inc(sem(op.semkey), 16)
                        elif op.signals:
                            ins.then_inc(sem(op.semkey), 1)

                getattr(block, engobj[e])(body)
        return len(sems)


def build_program(stop_after="E", dumps=()):
    nc = bass.Bass("TRN2", target_bir_lowering=False)
    pg = Prog()
    PH = ["A", "B0", "B", "C", "D1", "D2", "E"]
    nph = PH.index(stop_after)

    def dram(name, shape, kind="ExternalInput"):
        return nc.dram_tensor(name, list(shape), F32, kind=kind).ap()

    x_d = dram("x", [NB * P, D])
    kmask_d = dram("kmask", [P, NB])
    w_in_d = dram("w_in", [D, D_IN])
    w_a_d = dram("w_branch_a", [D, D])
    w_b_d = dram("w_branch_b", [D, D])
    w_out_d = dram("w_out", [D, D])
    w_up_d = dram("w_up", [D, 2 * D_FF])
    w_dn_d = dram("w_down", [D_FF, D])
    vec = {}
    for nm, n in (("norm_mix_pre", D), ("b_ml_i", 4), ("b_ml_f", 4), ("ml_head_norm", D),
                  ("b_fox_f", 8), ("norm_mix_post", D), ("norm_ffn_pre", D), ("norm_ffn_post", D)):
        vec[nm] = dram(nm, [1, n])
    bg_d = dram("b_gate_ab", [16, P])
    convw_d = dram("conv_w", [3 * 44, P])
    convb_d = dram("conv_b", [44, P])
    out_d = dram("out", [16 * P, D], kind="ExternalOutput")
    NG = NFC // 2
    wup_scr = nc.dram_tensor("wup_scr", [P, NG, KC * 2 * 256], BF16, kind="Internal").ap()
    dbg_d = {}

    stack = ExitStack()
    with stack:
        arena = stack.enter_context(nc.sbuf_tensor("arena", [P, ARENA_KIB * 512], BF16))
        psum = stack.enter_context(nc.psum_tensor("psum", [P, 4096], F32))

        class Buf:
            def __init__(self, name, lo, free, dt):
                n = int(np.prod(free))
                esz = 4 if dt == F32 else 2
                lo = (lo + 63) // 64 * 64
                self.lo, self.hi = lo, lo + n * esz
                assert self.hi <= ARENA_KIB * 1024, (name, self.hi)
                a = arena[:, lo // 2:self.hi // 2]
                if dt == F32:
                    a = a.bitcast(F32)
                if len(free) == 2:
                    a = a.rearrange("p (a b) -> p a b", a=free[0])
                elif len(free) == 3:
                    a = a.rearrange("p (a b c) -> p a b c", a=free[0], b=free[1])
                self.ap = a
                self.u = pg.unit(name, self.lo, self.hi)

        class Carver:
            def __init__(self, lo_kib, hi_kib):
                self.pos, self.hi = int(lo_kib * 1024), int(hi_kib * 1024)

            def __call__(self, name, free, dt):
                b = Buf(name, self.pos, free, dt)
                self.pos = b.hi
                assert self.pos <= self.hi, (name, self.pos, self.hi)
                return b

        class SB:
            def __init__(self, name, free, dt):
                self.t = stack.enter_context(nc.sbuf_tensor("sb_" + name, [P] + list(free), dt))
                self.ap = self.t[:]
                self.u = pg.unit(name)

        def bank(i, n=1):
            return psum[:, i * 512:(i + n) * 512]

        bank_u = [pg.unit("bank%d" % i) for i in range(8)]
        out_u = pg.unit("out_dram")

        def add_dumps(phase):
            for (ph, name, apf, shape) in dumps:
                if ph != phase:
                    continue
                ap, u = apf(locals_ref[0])
                dd = nc.dram_tensor(name, list(shape), ap.dtype, kind="ExternalOutput").ap()
                dbg_d[name] = dd
                pg.add("sp", lambda e, dd=dd, ap=ap: e.dma_start(out=dd, in_=ap), reads=[u], writes=[(out_u, name)],
                       dma="dbg_" + name)

        locals_ref = [None]

        ident_f = SB("ident_f", [P], F32)
        ident_b = SB("ident_b", [P], BF16)
        U_f = SB("U_f", [P], F32)
        M_b = SB("M_b", [P], BF16)
        ones_f = SB("ones_f", [P], F32)
        neghalf = SB("neghalf", [1], F32)
        ssq = SB("ssq", [96], F32)
        ms = SB("ms", [96], F32)
        rstd = SB("rstd", [96], F32)
        LF = SB("LF", [NB, 12], F32)
        LFh = SB("LFh", [NB, 12], BF16)
        LFl = SB("LFl", [NB, 12], BF16)
        es_t = SB("es_t", [NB, 4], F32)
        wk_t = SB("wk_t", [NB, 4], F32)
        a_t = SB("a_t", [NB, 4], F32)
        bias_all = SB("bias_all", [5, NB, 8], F32)
        kmask = SB("kmask", [NB], F32)
        bg_pp = SB("bg_pp", [16], F32)
        cw_pp = SB("cw_pp", [3, 44], F32)
        cb_pp = SB("cb_pp", [44], F32)
        halo_u = SB("halo_u", [2, 44, 2], F32)

        pg.add("pool", lambda e: e.memset(ones_f.ap, 1.0), writes=[ones_f])
        pg.add("pool", lambda e: e.memset(U_f.ap, 1.0), writes=[U_f])
        pg.add("pool", lambda e: e.memset(ident_f.ap, 1.0), writes=[ident_f])
        pg.add("pool", lambda e: e.memset(neghalf.ap, -0.5), writes=[neghalf])
        pg.add("pool", lambda e: e.memset(halo_u.ap, 0.0), writes=[halo_u])
        pg.add("pool", lambda e: e.affine_select(out=U_f.ap, in_=U_f.ap, pattern=[[1, P]],
                                                 compare_op=ALU.is_ge, fill=0.0, base=0,
                                                 channel_multiplier=-1),
               reads=[U_f], writes=[U_f])
        pg.add("pool", lambda e: e.affine_select(out=ident_f.ap, in_=ident_f.ap, pattern=[[-1, P]],
                                                 compare_op=ALU.is_equal, fill=0.0, base=0,
                                                 channel_multiplier=1),
               reads=[ident_f], writes=[ident_f])
        pg.add("dve", lambda e: e.tensor_copy(out=ident_b.ap, in_=ident_f.ap), reads=[ident_f], writes=[ident_b])
        pg.add("dve", lambda e: e.tensor_copy(out=M_b.ap, in_=U_f.ap), reads=[U_f], writes=[M_b])
        pg.add("sp", lambda e: e.dma_start(out=kmask.ap, in_=kmask_d[:, :]), writes=[kmask], dma="c_kmask")

        xnT_own = Carver(0, 34)("xnT_own", [KC, TEXT], BF16)
        xnT_pre = Carver(34, 64)("xnT_pre", [KC, TPRE], BF16)
        haT = Carver(64, 98)("haT", [KC, TEXT], BF16)
        hbT = Carver(98, 132)("hbT", [KC, TEXT], BF16)

        def xnT(tok0, n):
            if tok0 < TPRE:
                assert tok0 + n <= TPRE
                return xnT_pre, xnT_pre.ap[:, :, tok0:tok0 + n]
            return xnT_own, xnT_own.ap[:, :, tok0 - TPRE:tok0 - TPRE + n]

        def bcast_load(buf, name):
            pg.add("sp", lambda e: e.dma_start(out=buf.ap, in_=vec[name].partition_broadcast(P)),
                   writes=[buf], dma="c_" + name)

        def wload(buf_ap, buf_dep, src_ap, slot):
            pg.add("pool", lambda e: e.dma_start(out=buf_ap, in_=src_ap), writes=[buf_dep], dma=slot)

        def wsrc(w_d, c0, n):
            return w_d[:, c0:c0 + n].rearrange("(kc p) n -> p kc n", p=P)

        def rms_rstd(col, inv_n):
            pg.add("dve", lambda e: e.tensor_scalar(out=ms.ap[:, col:col + 1], in0=ssq.ap[:, col:col + 1],
                                                    scalar1=inv_n, scalar2=EPS, op0=ALU.mult, op1=ALU.add),
                   reads=[(ssq.u, col)], writes=[(ms.u, col)])
            pg.add("pool", lambda e: e.tensor_tensor(out=rstd.ap[:, col:col + 1], in0=ms.ap[:, col:col + 1],
                                                     in1=neghalf.ap, op=ALU.pow),
                   reads=[(ms.u, col), neghalf], writes=[(rstd.u, col)])

        def transpose8(src_ap, src_dep, bnk, dst_ap, dst_dep, n=KC, evac="act"):
            pb = bank(bnk).bitcast(BF16)

            def f(e):
                ins = None
                for kc in range(n):
                    ins = e.transpose(pb[:, kc * P:(kc + 1) * P], src_ap[:, kc * P:(kc + 1) * P], ident_b.ap)
                return ins
            pg.add("pe", f, reads=[src_dep, ident_b], writes=[bank_u[bnk]])
            pv = pb[:, 0:n * P]
            if n > 1:
                pv = pv.rearrange("p (a b) -> p a b", a=n)
            if evac == "act":
                pg.add("act", lambda e: e.activation(out=dst_ap, in_=pv, func=AF.Copy),
                       reads=[bank_u[bnk]], writes=[dst_dep])
            else:
                pg.add("dve", lambda e: e.tensor_copy(out=dst_ap, in_=pv), reads=[bank_u[bnk]], writes=[dst_dep])

        def mm_group(out_ap, pairs, bnk_dep, reads):
            def f(e):
                ins = None
                n = len(pairs)
                for i, (l, r) in enumerate(pairs):
                    ins = e.matmul(out_ap, lhsT=l, rhs=r, start=(i == 0), stop=(i == n - 1))
                return ins
            pg.add("pe", f, reads=reads, writes=[bnk_dep])

        cw = Carver(132, ARENA_KIB)
        wpre_bc = cw("wpre_bc", [D], F32)
        xin = [cw("xin%d" % i, [2, D], F32) for i in range(5)]
        xs = [cw("xs%d" % i, [D], BF16) for i in range(2)]
        jk = [cw("jk%d" % i, [D], BF16) for i in range(2)]
        Wg = cw("Wg", [KC, 16], BF16)
        bias16 = cw("bias16", [16], F32)
        bcast_load(wpre_bc, "norm_mix_pre")
        GB = 7
        if nph >= 1:
            wload(Wg.ap[:, :, 0:8], Wg.u, wsrc(w_in_d, OFF_IM, 8), "wg")
            wload(Wg.ap[:, :, 8:16], Wg.u, wsrc(w_in_d, OFF_FF, 8), "wg")
            for (nm, c0, n) in (("b_ml_i", 0, 4), ("b_ml_f", 4, 4), ("b_fox_f", 8, 8)):
                pg.add("sp", lambda e, nm=nm, c0=c0, n=n: e.dma_start(
                    out=bias16.ap[:, c0:c0 + n], in_=vec[nm].partition_broadcast(P)),
                    writes=[bias16], dma="c_bias16")

        def A_stats(b):
            if b >= NB:
                return
            g_ = b // 2
            xg = xin[g_ % 5]
            if b % 2 == 0:
                pg.add("sp", lambda e: e.dma_start(
                    out=xg.ap, in_=x_d[g_ * 2 * P:(g_ + 1) * 2 * P, :].rearrange("(k p) d -> p k d", p=P)),
                    writes=[xg], dma="xin%d" % (g_ % 5))
            pg.add("act", lambda e: e.activation(out=jk[b % 2].ap, in_=xg.ap[:, b % 2, :], func=AF.Square,
                                                 accum_out=ssq.ap[:, b:b + 1]),
                   reads=[xg], writes=[(ssq.u, b), jk[b % 2]])
            rms_rstd(b, 1.0 / D)

        def A_scale_T(b):
            xg, xsb = xin[(b // 2) % 5], xs[b % 2]
            pg.add("dve", lambda e: e.scalar_tensor_tensor(
                out=xsb.ap, in0=xg.ap[:, b % 2, :], scalar=rstd.ap[:, b:b + 1], in1=wpre_bc.ap, op0=ALU.mult, op1=ALU.mult),
                reads=[xg, (rstd.u, b), wpre_bc], writes=[xsb])
            pb = bank(b % 2).bitcast(BF16)

            def f(e):
                ins = None
                for kc in range(KC):
                    ins = e.transpose(pb[:, kc * P:(kc + 1) * P], xsb.ap[:, kc * P:(kc + 1) * P], ident_b.ap)
                return ins
            pg.add("pe", f, reads=[xsb, ident_b], writes=[bank_u[b % 2]])

        def A_evac(b):
            if b < 0:
                return
            buf, dst = xnT(b * P, P)
            pv = bank(b % 2).bitcast(BF16).rearrange("p (a b) -> p a b", a=KC)
            if b % 2 == 0:
                pg.add("act", lambda e: e.activation(out=dst, in_=pv, func=AF.Copy),
                       reads=[bank_u[b % 2]], writes=[(buf.u, b)])
            else:
                pg.add("dve", lambda e: e.tensor_copy(out=dst, in_=pv), reads=[bank_u[b % 2]], writes=[(buf.u, b)])
            if nph >= 1:
                mm_group(bank(GB)[:, b * 16:(b + 1) * 16],
                         [(dst[:, kc, :], Wg.ap[:, kc, :]) for kc in range(KC)],
                         (bank_u[GB], b), [(buf.u, b), Wg])

        for b_ in range(6):
            A_stats(b_)
        for b in range(NB):
            A_stats(b + 6)
            A_scale_T(b)
            A_evac(b - 1)
        A_evac(NB - 1)
        locals_ref[0] = locals()
        add_dumps("A")

        if nph >= 1:
            cw = Carver(160, ARENA_KIB)
            gates = cw("gates", [NB, 16], F32)
            cap8 = cw("cap8", [NB, 8], F32)
            e1 = cw("e1", [NB, 12], F32)
            spl = cw("spl", [NB, 12], F32)
            loc = cw("loc", [NB, 12], F32)
            tot = cw("tot", [NB, 12], F32)
            dbias = cw("dbias", [NB, 4], F32)
            wkarg = cw("wkarg", [NB, 4], F32)
            incl = cw("incl", [NB, 8], F32)
            carry = cw("carry", [NB, 8], F32)
            Gt = cw("Gt", [NB, 8], F32)
            negGm = cw("negGm", [NB, 8], F32)
            gps = bank(GB).rearrange("p (a b) -> p a b", a=NB)
            pg.add("dve", lambda e: e.tensor_tensor(out=gates.ap, in0=gps,
                                                    in1=bias16.ap.unsqueeze(1).to_broadcast([P, NB, 16]),
                                                    op=ALU.add),
                   reads=[bank_u[GB], bias16], writes=[gates])
            pg.add("act", lambda e: e.activation(out=cap8.ap, in_=gates.ap[:, :, 0:8], func=AF.Tanh, scale=1.0 / 15.0),
                   reads=[gates], writes=[cap8])
            pg.add("dve", lambda e: e.tensor_scalar(out=gates.ap[:, :, 0:8], in0=cap8.ap, scalar1=15.0, scalar2=None,
                                                    op0=ALU.mult),
                   reads=[cap8], writes=[gates])
            pg.add("act", lambda e: e.activation(out=e1.ap, in_=gates.ap[:, :, 4:16], func=AF.Exp, scale=-1.0),
                   reads=[gates], writes=[e1])
            pg.add("act", lambda e: e.activation(out=spl.ap, in_=e1.ap, func=AF.Ln, bias=1.0, scale=1.0),
                   reads=[e1], writes=[spl])
            pg.add("dve", lambda e: e.tensor_scalar(out=LF.ap, in0=spl.ap, scalar1=-1.0, scalar2=None, op0=ALU.mult),
                   reads=[spl], writes=[LF])
            pg.add("dve", lambda e: e.tensor_copy(out=LFh.ap, in_=LF.ap), reads=[LF], writes=[LFh])
            pg.add("dve", lambda e: e.tensor_tensor(out=LFl.ap, in0=LF.ap, in1=LFh.ap, op=ALU.subtract),
                   reads=[LF, LFh], writes=[LFl])
            LF2 = LF.ap.rearrange("p a b -> p (a b)")
            mm_group(bank(GB)[:, 0:384], [(U_f.ap, LF2)], bank_u[GB], [U_f, LF])
            pg.add("act", lambda e: e.activation(out=loc.ap.rearrange("p a b -> p (a b)"), in_=bank(GB)[:, 0:384],
                                                 func=AF.Copy),
                   reads=[bank_u[GB]], writes=[loc])
            mm_group(bank(GB)[:, 0:384], [(ones_f.ap, LF2)], bank_u[GB], [ones_f, LF])
            pg.add("act", lambda e: e.activation(out=tot.ap.rearrange("p a b -> p (a b)"), in_=bank(GB)[:, 0:384],
                                                 func=AF.Copy),
                   reads=[bank_u[GB]], writes=[tot])
            pg.add("dve", lambda e: e.tensor_tensor(out=dbias.ap, in0=gates.ap[:, :, 0:4], in1=loc.ap[:, :, 0:4],
                                                    op=ALU.subtract),
                   reads=[gates, loc], writes=[dbias])
            pg.add("act", lambda e: e.activation(out=es_t.ap, in_=dbias.ap, func=AF.Exp), reads=[dbias], writes=[es_t])
            pg.add("dve", lambda e: e.tensor_tensor(out=wkarg.ap, in0=dbias.ap, in1=tot.ap[:, :, 0:4], op=ALU.add),
                   reads=[dbias, tot], writes=[wkarg])
            pg.add("act", lambda e: e.activation(out=wk_t.ap, in_=wkarg.ap, func=AF.Exp), reads=[wkarg], writes=[wk_t])
            pg.add("act", lambda e: e.activation(out=a_t.ap, in_=tot.ap[:, :, 0:4], func=AF.Exp), reads=[tot], writes=[a_t])
            for h in range(8):
                pg.add("dve", lambda e, h=h: e.tensor_tensor_scan(
                    out=incl.ap[:, :, h], data0=ones_f.ap[:, 0:NB], data1=tot.ap[:, :, 4 + h], initial=0.0,
                    op0=ALU.mult, op1=ALU.add),
                    reads=[tot, ones_f], writes=[(incl.u, h)])
            pg.add("dve", lambda e: e.tensor_tensor(out=carry.ap, in0=incl.ap, in1=tot.ap[:, :, 4:12], op=ALU.subtract),
                   reads=[incl, tot], writes=[carry])
            pg.add("dve", lambda e: e.tensor_tensor(out=Gt.ap, in0=loc.ap[:, :, 4:12], in1=carry.ap, op=ALU.add),
                   reads=[loc, carry], writes=[Gt])
            pg.add("dve", lambda e: e.scalar_tensor_tensor(
                out=negGm.ap, in0=Gt.ap, scalar=-1.0, in1=kmask.ap.unsqueeze(2).to_broadcast([P, NB, 8]),
                op0=ALU.mult, op1=ALU.add),
                reads=[Gt, kmask], writes=[negGm])
            for ti, (eb0, nb_) in enumerate(EXT_TILES):
                r = NPRE + eb0 + (2 if nb_ == 4 else 0)
                pg.add("dve", lambda e, ti=ti, r=r: e.tensor_tensor(
                    out=bias_all.ap[:, ti, :, :], in0=negGm.ap,
                    in1=carry.ap[:, r:r + 1, :].to_broadcast([P, NB, 8]), op=ALU.add),
                    reads=[negGm, carry], writes=[(bias_all.u, ti)])
            locals_ref[0] = locals()
            add_dumps("B0")

        if nph >= 2:
            cw = Carver(98, ARENA_KIB)
            Wm = cw("Wm", [KC, 3072], BF16)
            halfw = cw("halfw", [D], F32)
            qkT = cw("qkT", [2, 4, 512], BF16)
            ksc = [cw("ksc%d" % i, [4, P], BF16) for i in range(2)]
            vext = [cw("vext%d" % i, [4, 257], BF16) for i in range(2)]
            Gw = [cw("Gw%d" % i, [D], F32) for i in range(2)]
            EB = [cw("EB0", [4, P], F32)] * 2
            EBM = [cw("EBM0", [4, P], F32)] * 2
            ST = [cw("ST%d" % i, [4, P], BF16) for i in range(2)]
            qs = [cw("qs%d" % i, [4, P], BF16) for i in range(2)]
            hablk = [cw("hablk%d" % i, [D], BF16) for i in range(2)]
            CT = cw("CT", [4, 257], F32)
            CTb = cw("CTb", [4, 257], BF16)
            sc = SB("sc", [2, 4, 8], F32)
            for i, (c0, n) in enumerate(((0, 512), (512, 512), (1024, 512), (1536, 512))):
                wload(Wm.ap[:, :, c0:c0 + n], (Wm.u, i), wsrc(w_in_d, c0, n), "wm%d" % i)
            for i in range(2):
                wload(Wm.ap[:, :, 2048 + i * 512:2048 + (i + 1) * 512], (Wm.u, 4 + i),
                      wsrc(w_in_d, OFF_OM + i * 512, 512), "wm%d" % (4 + i))
            bcast_load(halfw, "ml_head_norm")
            pg.add("dve", lambda e: e.tensor_scalar(out=halfw.ap, in0=halfw.ap, scalar1=0.5, scalar2=None, op0=ALU.mult),
                   reads=[halfw], writes=[halfw])
            pg.add("pool", lambda e: e.memset(CT.ap, 0.0), writes=[CT])
            pg.add("pool", lambda e: e.memset(CTb.ap, 0.0), writes=[CTb])
            for i in range(2):
                pg.add("pool", lambda e, i=i: e.memset(vext[i].ap[:, :, 256:257], 1.0), writes=[(vext[i].u, "ones")])
            PB = [0, 1, 2]
            pbi = [0]

            def nextbank():
                b_ = PB[pbi[0] % 3]
                pbi[0] += 1
                return b_
            SB0, SB1, RB0, RB1, TB = 3, 4, 5, 6, 7
            tile_of = {}
            for (eb0, nb_) in EXT_TILES:
                for j in range(nb_):
                    tile_of[NPRE + eb0 + j] = (eb0, nb_)

            def P_parts(c):
                own = c >= NPRE
                buf, xv = xnT(c * P, P)
                xdep = (buf.u, c)
                ks, ve = ksc[c % 2], vext[c % 2]

                def part_qk():
                    eb0, nb_ = tile_of[c]
                    if NPRE + eb0 != c:
                        return
                    w_ = nb_ * P
                    tb, tv = xnT(c * P, w_)
                    for qk in range(2):
                        for h in range(4):
                            bk = nextbank()
                            col = qk * 512 + h * P
                            mm_group(bank(bk)[:, 0:w_],
                                     [(Wm.ap[:, kc, col:col + P], tv[:, kc, :]) for kc in range(KC)],
                                     bank_u[bk], [(Wm.u, qk)] + [(tb.u, c + j) for j in range(nb_)])
                            sc_ = (P ** -0.5) if qk == 0 else 1.0
                            pg.add("act", lambda e, bk=bk, qk=qk, h=h, w_=w_, sc_=sc_: e.activation(
                                out=qkT.ap[:, qk, h, 0:w_], in_=bank(bk)[:, 0:w_], func=AF.Copy, scale=sc_),
                                reads=[bank_u[bk]], writes=[(qkT.u, (qk, h))])

                def part_k():
                    if own:
                        part_qk()
                    bk = nextbank()
                    if own:
                        eb0_, _ = tile_of[c]
                        lc_ = (c - NPRE - eb0_) * P
                        pbk = bank(bk).bitcast(BF16)

                        def f(e):
                            ins = None
                            for h in range(4):
                                ins = e.transpose(pbk[:, h * P:(h + 1) * P], qkT.ap[:, 1, h, lc_:lc_ + P], ident_b.ap)
                            return ins
                        pg.add("pe", f, reads=[(qkT.u, (1, h)) for h in range(4)] + [ident_b], writes=[bank_u[bk]])
                        ksrc = pbk
                    else:
                        mm_group(bank(bk), [(xv[:, kc, :], Wm.ap[:, kc, 512:1024]) for kc in range(KC)],
                                 bank_u[bk], [xdep, (Wm.u, 1)])
                        ksrc = bank(bk)
                    for h in range(4):
                        pg.add("dve", lambda e, h=h: e.tensor_scalar(
                            out=ks.ap[:, h, :], in0=ksrc[:, h * P:(h + 1) * P], scalar1=wk_t.ap[:, c, h:h + 1],
                            scalar2=None, op0=ALU.mult),
                            reads=[bank_u[bk], wk_t], writes=[(ks.u, h)])

                def part_v():
                    for hv in range(2):
                        bk = nextbank()
                        mm_group(bank(bk), [(xv[:, kc, :], Wm.ap[:, kc, 1024 + hv * 512:1536 + hv * 512]) for kc in range(KC)],
                                 bank_u[bk], [xdep, (Wm.u, 2 + hv)])
                        pg.add("act", lambda e, bk=bk, hv=hv: e.activation(
                            out=ve.ap[:, 2 * hv:2 * hv + 2, 0:256], in_=bank(bk).rearrange("p (a b) -> p a b", a=2),
                            func=AF.Copy),
                            reads=[bank_u[bk]], writes=[(ve.u, hv)])

                def part_o():
                    if not own:
                        return
                    gw = Gw[c % 2]
                    for ho in range(2):
                        bk = nextbank()
                        mm_group(bank(bk), [(xv[:, kc, :], Wm.ap[:, kc, 2048 + ho * 512:2560 + ho * 512]) for kc in range(KC)],
                                 bank_u[bk], [xdep, (Wm.u, 4 + ho)])
                        sl = slice(ho * 512, (ho + 1) * 512)
                        pg.add("act", lambda e, bk=bk, sl=sl: e.activation(out=gw.ap[:, sl], in_=bank(bk), func=AF.Tanh, scale=0.5),
                               reads=[bank_u[bk]], writes=[(gw.u, ho)])
                        pg.add("dve", lambda e, sl=sl: e.scalar_tensor_tensor(
                            out=gw.ap[:, sl], in0=gw.ap[:, sl], scalar=1.0, in1=halfw.ap[:, sl], op0=ALU.add, op1=ALU.mult),
                            reads=[(gw.u, ho), halfw], writes=[(gw.u, ho)])
                return [part_k, part_v, part_o]

            def I_stage(c):
                if c >= NB or c < NPRE:
                    return
                eb0, nb_ = tile_of[c]
                lc = (c - NPRE - eb0) * P
                eb_, ebm_, st_, qs_ = EB[c % 2], EBM[c % 2], ST[c % 2], qs[c % 2]
                for h in range(4):
                    mm_group(bank(SB0)[:, h * P:(h + 1) * P],
                             [(LFh.ap[:, c, h:h + 1].to_broadcast([P, P]), M_b.ap),
                              (LFl.ap[:, c, h:h + 1].to_broadcast([P, P]), M_b.ap)],
                             (bank_u[SB0], h), [LFh, LFl, M_b])
                for h in range(4):
                    mm_group(bank(SB1)[:, h * P:(h + 1) * P],
                             [(qkT.ap[:, 1, h, lc:lc + P], qkT.ap[:, 0, h, lc:lc + P])],
                             (bank_u[SB1], h), [(qkT.u, (0, h)), (qkT.u, (1, h))])
                for h in range(4):
                    pg.add("act", lambda e, h=h: e.activation(out=eb_.ap[:, h, :], in_=bank(SB0)[:, h * P:(h + 1) * P],
                                                              func=AF.Exp),
                           reads=[bank_u[SB0]], writes=[(eb_.u, h)])
                for h in range(4):
                    pg.add("dve", lambda e, h=h: e.tensor_tensor(out=ebm_.ap[:, h, :], in0=eb_.ap[:, h, :], in1=U_f.ap,
                                                                 op=ALU.mult),
                           reads=[(eb_.u, h), U_f], writes=[(ebm_.u, h)])
                    pg.add("dve", lambda e, h=h: e.tensor_tensor(
                        out=qs_.ap[:, h, :], in0=qkT.ap[:, 0, h, lc:lc + P], in1=eb_.ap[:, h, :], op=ALU.mult),
                        reads=[(eb_.u, h), (qkT.u, (0, h))], writes=[(qs_.u, h)])
                for h in range(4):
                    pg.add("dve", lambda e, h=h: e.scalar_tensor_tensor(
                        out=st_.ap[:, h, :], in0=bank(SB1)[:, h * P:(h + 1) * P], scalar=es_t.ap[:, c, h:h + 1],
                        in1=ebm_.ap[:, h, :], op0=ALU.mult, op1=ALU.mult),
                        reads=[bank_u[SB1], es_t, (ebm_.u, h)], writes=[(st_.u, h)])

            def R_num(c, h):
                ve = vext[c % 2]
                st_, qs_ = ST[c % 2], qs[c % 2]
                gw, hb_ = Gw[c % 2], hablk[c % 2]
                rb = RB0 if h % 2 == 0 else RB1
                nps = bank(rb)[:, 0:257]
                mm_group(nps, [(st_.ap[:, h, :], ve.ap[:, h, :]), (qs_.ap[:, h, :], CTb.ap[:, h, :])],
                         bank_u[rb], [(st_.u, h), (ve.u, h // 2), (ve.u, "ones"), (qs_.u, h), (CTb.u, h)])
                s0 = sc.ap[:, c % 2, h, :]
                su = (sc.u, (c % 2, h))
                pg.add("act", lambda e: e.activation(out=hb_.ap[:, h * 256:(h + 1) * 256], in_=nps[:, 0:256], func=AF.Square,
                                                     accum_out=s0[:, 0:1]),
                       reads=[bank_u[rb]], writes=[su, (hb_.u, h)])
                pg.add("act", lambda e: e.activation(out=s0[:, 1:2], in_=nps[:, 256:257], func=AF.Square),
                       reads=[bank_u[rb]], writes=[su])
                pg.add("dve", lambda e: e.tensor_scalar(out=s0[:, 2:3], in0=s0[:, 1:2], scalar1=1.0, scalar2=EPS,
                                                        op0=ALU.max, op1=ALU.mult),
                       reads=[su], writes=[su])
                pg.add("dve", lambda e: e.scalar_tensor_tensor(out=s0[:, 3:4], in0=s0[:, 0:1], scalar=1.0 / 256.0,
                                                               in1=s0[:, 2:3], op0=ALU.mult, op1=ALU.add),
                       reads=[su], writes=[su])
                pg.add("pool", lambda e: e.tensor_tensor(out=s0[:, 4:5], in0=s0[:, 3:4], in1=neghalf.ap, op=ALU.pow),
                       reads=[su, neghalf], writes=[su])
                pg.add("dve", lambda e: e.scalar_tensor_tensor(
                    out=hb_.ap[:, h * 256:(h + 1) * 256], in0=nps[:, 0:256], scalar=s0[:, 4:5],
                    in1=gw.ap[:, h * 256:(h + 1) * 256], op0=ALU.mult, op1=ALU.mult),
                    reads=[bank_u[rb], su, (gw.u, h // 2)], writes=[(hb_.u, h)])

            def R_upd(c, h):
                ks, ve = ksc[c % 2], vext[c % 2]
                rb = RB0 if h % 2 == 0 else RB1
                ups = bank(rb)[:, 0:257]
                mm_group(ups, [(ks.ap[:, h, :], ve.ap[:, h, :])], bank_u[rb],
                         [(ks.u, h), (ve.u, h // 2), (ve.u, "ones")])
                pg.add("dve", lambda e: e.scalar_tensor_tensor(
                    out=CT.ap[:, h, :], in0=CT.ap[:, h, :], scalar=a_t.ap[:, c, h:h + 1], in1=ups,
                    op0=ALU.mult, op1=ALU.add),
                    reads=[bank_u[rb], a_t, (CT.u, h)], writes=[(CT.u, h)])
                pg.add("act", lambda e: e.activation(out=CTb.ap[:, h, :], in_=CT.ap[:, h, :], func=AF.Copy),
                       reads=[(CT.u, h)], writes=[(CTb.u, h)])

            def T_stage(c):
                if c < NPRE or c >= NB:
                    return
                ec = c - NPRE
                hb_ = hablk[c % 2]
                transpose8(hb_.ap, hb_.u, TB, haT.ap[:, :, ec * P:(ec + 1) * P], (haT.u, ec))

            scr_u = pg.unit("wup_scr")
            scr_v = wup_scr.rearrange("p g (kc ag n) -> p g kc ag n", kc=KC, ag=2)
            conv_jobs = []
            for g_ in range(NG):
                def cj(g_=g_):
                    for ag_ in range(2):
                        pg.add("pool", lambda e, g_=g_, ag_=ag_: e.dma_start(
                            out=scr_v[:, g_, :, ag_, :], in_=wsrc(w_up_d, ag_ * D_FF + g_ * 256, 256)),
                            writes=[(scr_u, g_)], dma="scr%d" % (g_ % 4))
                conv_jobs.append(cj)
            for p_ in P_parts(0):
                p_()
            I_stage(0)
            for c in range(NB):
                own = c >= NPRE
                nxt = P_parts(c + 1) if c + 1 < NB else [lambda: None] * 3
                if own:
                    R_num(c, 0)
                    R_num(c, 1)
                    nxt[0]()
                    I_stage(c + 1)
                    R_num(c, 2)
                    R_num(c, 3)
                    nxt[1]()
                    R_upd(c, 0)
                    R_upd(c, 1)
                    nxt[2]()
                    R_upd(c, 2)
                    R_upd(c, 3)
                else:
                    R_upd(c, 0)
                    R_upd(c, 1)
                    nxt[0]()
                    I_stage(c + 1)
                    R_upd(c, 2)
                    R_upd(c, 3)
                    nxt[1]()
                    nxt[2]()
                T_stage(c - 1)
                if c % 2 == 0 and c // 2 < NG:
                    conv_jobs[c // 2]()
            T_stage(NB - 1)
            locals_ref[0] = locals()
            add_dumps("B")

        if nph >= 3:
            cw = Carver(132, 190)
            kT = [cw("kT%d" % i, [NB * P], BF16) for i in range(2)]
            vxp = cw("vxp", [NB, 2, 129], BF16)
            qT = [cw("qT%d" % i, [TEXT], BF16) for i in range(2)]
            Wf = [cw("Wf%d" % i, [KC, 256], BF16) for i in range(2)]
            Wfv = cw("Wfv", [KC, 256], BF16)
            pT = [cw("pT%d" % i, [512], BF16) for i in range(4)]
            fsc = cw("fsc", [4, 2], F32)
            pg.add("pool", lambda e: e.memset(vxp.ap[:, :, :, 128:129], 1.0), writes=[(vxp.u, "ones")])
            LG = [0, 1, 7]
            ACC = [2, 3, 4, 5]
            PBK = [6, 0, 1, 7]
            pbc = [0]

            def nextb():
                b_ = PBK[pbc[0] % 4]
                pbc[0] += 1
                return b_
            def evac(out_ap, in_ap, reads, writes):
                pg.add("dve", lambda e: e.tensor_copy(out=out_ap, in_=in_ap), reads=reads, writes=writes)
            WIN_TILES = [(0, 512), (512, 512), (1024, 512), (1536, 384)] + \
                        [(TPRE + eb0 * P, nb_ * P) for (eb0, nb_) in EXT_TILES]
            hbtok = [cw("hbtk%d" % i, [P], BF16) for i in range(4)]

            def proj_groups(h):
                s_ = h % 2
                wf = Wf[s_]
                out = []

                def ld():
                    for i, off in enumerate((OFF_QF, OFF_KF)):
                        wload(wf.ap[:, :, i * P:(i + 1) * P], wf.u, wsrc(w_in_d, off + h * P, P), "wf%d" % s_)
                out.append(ld)
                gidx = 0
                for (t0, w_) in WIN_TILES:
                    for which in ((1, 0) if t0 >= TPRE else (1,)):
                        gbk = PBK[gidx % 4] if h == 0 else 6
                        gidx += 1
                        for piece in range(4):
                            def pc(t0=t0, w_=w_, which=which, piece=piece, gbk=gbk):
                                tb, tv = xnT(t0, w_)
                                blks = list(range(t0 // P, (t0 + w_) // P))
                                bk = gbk

                                def f(e):
                                    ins = None
                                    for kc in (2 * piece, 2 * piece + 1):
                                        ins = e.matmul(bank(bk)[:, 0:w_], lhsT=wf.ap[:, kc, which * P:(which + 1) * P],
                                                       rhs=tv[:, kc, :], start=(kc == 0), stop=(kc == KC - 1))
                                    return ins
                                pg.add("pe", f, reads=[wf] + [(tb.u, j) for j in blks], writes=[bank_u[bk]])
                                if piece == 3:
                                    if which == 1:
                                        evac(kT[s_].ap[:, t0:t0 + w_], bank(bk)[:, 0:w_], [bank_u[bk]],
                                             [(kT[s_].u, j) for j in blks])
                                    else:
                                        evac(qT[s_].ap[:, t0 - TPRE:t0 - TPRE + w_], bank(bk)[:, 0:w_], [bank_u[bk]],
                                             [(qT[s_].u, j) for j in blks])
                            pc.closes = (piece == 3)
                            out.append(pc)
                return out

            def vpair_load(p_):
                if p_ >= 4:
                    return
                wload(Wfv.ap, Wfv.u, wsrc(w_in_d, OFF_VF + p_ * 2 * P, 2 * P), "wfv")

            def vpair_proj(p_):
                for g in range(NB // 2):
                    bk = nextb()
                    for jj in range(2):
                        j = g * 2 + jj
                        tb, tv = xnT(j * P, P)
                        mm_group(bank(bk)[:, jj * 256:(jj + 1) * 256],
                                 [(tv[:, kc, :], Wfv.ap[:, kc, :]) for kc in range(KC)],
                                 (bank_u[bk], jj), [Wfv, (tb.u, j)])
                    evac(vxp.ap[:, g * 2:(g + 1) * 2, :, 0:P],
                         bank(bk).rearrange("p (a b c) -> p a b c", a=2, b=2),
                         [bank_u[bk]], [(vxp.u, g)])
            vpair_load(0)
            pgs = {h_: proj_groups(h_) for h_ in range(8)}
            lds = {h_: pgs[h_].pop(0) for h_ in range(8)}
            lds[0]()
            for f_ in pgs[0]:
                f_()
            lds[1]()
            gstep = [0]
            for h in range(8):
                s_ = h % 2
                pend = pgs[h + 1] if h + 1 < 8 else []
                if h + 2 < 8:
                    lds[h + 2]()
                if h % 2 == 0:
                    vpair_proj(h // 2)
                    vpair_load(h // 2 + 1)
                npend = len(pend)
                HOLD = 0
                steps = []
                for ti, (eb0, nb_) in enumerate(EXT_TILES):
                    b0 = NPRE + eb0
                    for j in range(b0 + nb_):
                        steps.append((ti, eb0, nb_, b0, j))
                nst = len(steps)

                def qk(n):
                    ti, eb0, nb_, b0, j = steps[n]
                    w_ = nb_ * P
                    c0 = max(0, j - b0) * P
                    lgb = LG[(gstep[0] + n) % 3]
                    mm_group(bank(lgb)[:, 0:w_ - c0],
                             [(kT[s_].ap[:, j * P:(j + 1) * P], qT[s_].ap[:, eb0 * P + c0:eb0 * P + w_])],
                             bank_u[lgb], [(kT[s_].u, j)] + [(qT[s_].u, b0 + i) for i in range(nb_)])
                qk(0)
                if nst > 1:
                    qk(1)
                deferred = []
                grp_open = [False]
                for n in range(nst):
                    ti, eb0, nb_, b0, j = steps[n]
                    w_ = nb_ * P
                    c0 = max(0, j - b0) * P
                    if n + 2 < nst:
                        qk(n + 2)
                    want = (max(0, n + 1 - HOLD) * npend + (nst - HOLD) - 1) // (nst - HOLD)
                    while npend - len(pend) < want and pend:
                        f_ = pend.pop(0)
                        f_()
                        grp_open[0] = not getattr(f_, "closes", True)
                    lgb = LG[(gstep[0] + n) % 3]
                    pt = pT[(gstep[0] + n) % 4]
                    pg.add("act", lambda e, pt=pt, lgb=lgb, c0=c0, w_=w_, ti=ti, j=j, h=h: e.activation(
                        out=pt.ap[:, c0:w_], in_=bank(lgb)[:, 0:w_ - c0], func=AF.Exp, scale=float(P) ** -0.5,
                        bias=bias_all.ap[:, ti, j, h:h + 1]),
                        reads=[bank_u[lgb], (bias_all.u, ti)], writes=[pt])
                    if j >= b0:
                        pg.add("pool", lambda e, pt=pt, c0=c0: e.tensor_tensor(
                            out=pt.ap[:, c0:c0 + P], in0=pt.ap[:, c0:c0 + P], in1=M_b.ap, op=ALU.mult),
                            reads=[pt, M_b], writes=[pt])
                    act_i = [i for i in range(nb_) if i * P >= c0]

                    def pv(e, pt=pt, j=j, b0=b0, act_i=act_i, hp=h % 2):
                        ins = None
                        for i in act_i:
                            ins = e.matmul(bank(ACC[i])[:, 0:129], lhsT=pt.ap[:, i * P:(i + 1) * P], rhs=vxp.ap[:, j, hp, :],
                                           start=(j == 0), stop=(j == b0 + i))
                        return ins
                    pg.add("pe", pv, reads=[pt, (vxp.u, j // 2), (vxp.u, "ones")],
                           writes=[bank_u[ACC[i]] for i in act_i])
                    while deferred and deferred[0][0] <= n and not grp_open[0]:
                        deferred.pop(0)[1]()
                    if j >= b0:
                        i = j - b0
                        ab = ACC[i]
                        hbk = hbtok[(gstep[0] + n) % 4]
                        pg.add("dve", lambda e, ab=ab, i=i: e.tensor_scalar(
                            out=fsc.ap[:, i, 0:1], in0=bank(ab)[:, 128:129], scalar1=1e-30, scalar2=None, op0=ALU.max),
                            reads=[bank_u[ab]], writes=[(fsc.u, i)])
                        pg.add("dve", lambda e, i=i: e.reciprocal(out=fsc.ap[:, i, 1:2], in_=fsc.ap[:, i, 0:1]),
                               reads=[(fsc.u, i)], writes=[(fsc.u, i)])
                        pg.add("dve", lambda e, ab=ab, i=i, hbk=hbk: e.tensor_scalar(
                            out=hbk.ap, in0=bank(ab)[:, 0:P], scalar1=fsc.ap[:, i, 1:2], scalar2=None, op0=ALU.mult),
                            reads=[bank_u[ab], (fsc.u, i)], writes=[hbk])

                        def tr(hbk=hbk, h=h, col=(eb0 + i) * P, key=(h, eb0 + i)):
                            tbk = 6
                            transpose8(hbk.ap, hbk.u, tbk, hbT.ap[:, h, col:col + P], (hbT.u, key), n=1, evac="dve")
                        deferred.append((n + 2, tr))
                for f_ in pend:
                    f_()
                for (_, f_) in deferred:
                    f_()
                gstep[0] += nst
            locals_ref[0] = locals()
            add_dumps("C")

        if nph >= 4:
            cw = Carver(132, 190)
            mergedT = cw("mergedT", [KC, TEXT], BF16)
            Wd = [cw("Wd%d" % i, [4, KC, P], BF16) for i in range(2)]
            cw = Carver(34, 64)
            sga = cw("sga", [512], F32)
            sgb = cw("sgb", [512], F32)
            m1 = cw("m1", [512], F32)
            m2 = cw("m2", [512], F32)
            stg = cw("stg", [5, P], F32)
            for i, (src, n) in enumerate(((bg_d, 16), (convw_d[0:44, :], 44), (convw_d[44:88, :], 44),
                                          (convw_d[88:132, :], 44), (convb_d, 44))):
                pg.add("sp", lambda e, i=i, src=src, n=n: e.dma_start(out=stg.ap[0:n, i, :], in_=src),
                       writes=[(stg.u, i)], dma="c_stg")
            for i, (dst, n) in enumerate(((bg_pp.ap, 16), (cw_pp.ap[:, 0, :], 44), (cw_pp.ap[:, 1, :], 44),
                                          (cw_pp.ap[:, 2, :], 44), (cb_pp.ap, 44))):
                pg.add("pe", lambda e, i=i, n=n: e.transpose(bank(7)[:, i * 64:i * 64 + n], stg.ap[0:n, i, :],
                                                             ident_f.ap[0:n, 0:n]),
                       reads=[(stg.u, i), ident_f], writes=[(bank_u[7], i)])
                pg.add("dve", lambda e, i=i, n=n, dst=dst: e.tensor_copy(out=dst, in_=bank(7)[:, i * 64:i * 64 + n]),
                       reads=[bank_u[7]], writes=[bg_pp if i == 0 else (cb_pp if i == 4 else (cw_pp.u, i))])

            dstep = 0

            wstg = cw("wstg", [4, KC, P], F32)

            def wd_load(cc_):
                if cc_ >= KC:
                    return
                wd_ = Wd[cc_ % 2]
                for i_, (wdr, off) in enumerate(((w_a_d, cc_ * P), (w_b_d, cc_ * P), (w_in_d, OFF_GA + cc_ * P),
                                                 (w_in_d, OFF_GB + cc_ * P))):
                    pg.add("sp", lambda e, i_=i_, wdr=wdr, off=off: e.dma_start(out=wstg.ap[:, i_, :, :], in_=wsrc(wdr, off, P)),
                           writes=[(wstg.u, i_)], dma="wstg%d" % i_)
                    eng_ = "pool" if i_ % 2 == 0 else "dve"
                    pg.add(eng_, lambda e, i_=i_: e.tensor_copy(out=wd_.ap[:, i_, :, :], in_=wstg.ap[:, i_, :, :]),
                           reads=[(wstg.u, i_)], writes=[(wd_.u, i_)])
            wd_load(0)
            for cc in range(KC):
                wd = Wd[cc % 2]
                wd_load(cc + 1)
                for (eb0, nb_) in EXT_TILES:
                    w_ = nb_ * P
                    t0 = eb0 * P
                    bs = [0, 1, 2, 3] if dstep % 2 == 0 else [4, 5, 6, 7]
                    dstep += 1
                    srcs = (haT, hbT, xnT_own, xnT_own)
                    for i in range(4):
                        src = srcs[i]
                        if src is xnT_own:
                            rd = [(src.u, NPRE + eb0 + j) for j in range(nb_)]
                        elif src is haT:
                            rd = [(src.u, eb0 + j) for j in range(nb_)]
                        else:
                            rd = [src]
                        mm_group(bank(bs[i])[:, 0:w_],
                                 [(wd.ap[:, i, kc, :], src.ap[:, kc, t0:t0 + w_]) for kc in range(KC)],
                                 bank_u[bs[i]], [(wd.u, i)] + rd)
                    pg.add("act", lambda e, bs=bs, w_=w_, cc=cc: e.activation(
                        out=sga.ap[:, 0:w_], in_=bank(bs[2])[:, 0:w_], func=AF.Sigmoid, bias=bg_pp.ap[:, cc:cc + 1]),
                        reads=[bank_u[bs[2]], bg_pp], writes=[sga])
                    pg.add("act", lambda e, bs=bs, w_=w_, cc=cc: e.activation(
                        out=sgb.ap[:, 0:w_], in_=bank(bs[3])[:, 0:w_], func=AF.Sigmoid, bias=bg_pp.ap[:, 8 + cc:9 + cc]),
                        reads=[bank_u[bs[3]], bg_pp], writes=[sgb])
                    pg.add("dve", lambda e, bs=bs, w_=w_: e.tensor_tensor(out=m1.ap[:, 0:w_], in0=bank(bs[0])[:, 0:w_],
                                                                          in1=sga.ap[:, 0:w_], op=ALU.mult),
                           reads=[bank_u[bs[0]], sga], writes=[m1])
                    pg.add("dve", lambda e, bs=bs, w_=w_: e.tensor_tensor(out=m2.ap[:, 0:w_], in0=bank(bs[1])[:, 0:w_],
                                                                          in1=sgb.ap[:, 0:w_], op=ALU.mult),
                           reads=[bank_u[bs[1]], sgb], writes=[m2])
                    pg.add("pool", lambda e, w_=w_, cc=cc, t0=t0: e.tensor_tensor(
                        out=mergedT.ap[:, cc, t0:t0 + w_], in0=m1.ap[:, 0:w_], in1=m2.ap[:, 0:w_], op=ALU.add),
                        reads=[m1, m2], writes=[(mergedT.u, (cc, eb0))])
            locals_ref[0] = locals()
            add_dumps("D1")

        if nph >= 5:
            x1 = Carver(0, 68)("x1", [NEXT, D], F32)
            cw = Carver(68, 132)
            wpost_bc = cw("wpost_bc", [D], F32)
            tmpd = [cw("tmpd%d" % i, [D], F32) for i in range(2)]
            xr = [cw("xr%d" % i, [2, D], F32) for i in range(3)]
            sqd = cw("sqd", [D], BF16)
            cw = Carver(166, 190)
            Wout = cw("Wout", [KC, D], BF16)
            bcast_load(wpost_bc, "norm_mix_post")
            for i in range(2):
                wload(Wout.ap[:, :, i * 512:(i + 1) * 512], (Wout.u, i), wsrc(w_out_d, i * 512, 512), "wout%d" % i)
            for eb in range(NEXT):
                bs = [0, 1] if eb % 2 == 0 else [2, 3]
                for hh in range(2):
                    mm_group(bank(bs[hh]),
                             [(mergedT.ap[:, kc, eb * P:(eb + 1) * P], Wout.ap[:, kc, hh * 512:(hh + 1) * 512])
                              for kc in range(KC)],
                             bank_u[bs[hh]], [mergedT, (Wout.u, hh)])
                    c_ = 32 + 2 * eb + hh
                    pg.add("act", lambda e, b_=bs[hh], c_=c_, hh=hh, tm=tmpd[eb % 2]: e.activation(
                        out=tm.ap[:, hh * 512:(hh + 1) * 512], in_=bank(b_), func=AF.Square, accum_out=ssq.ap[:, c_:c_ + 1]),
                        reads=[bank_u[bs[hh]]], writes=[(ssq.u, c_), (tmpd[eb % 2].u, hh)])
                cs = 66 + eb
                pg.add("dve", lambda e, eb=eb, cs=cs: e.tensor_tensor(
                    out=ssq.ap[:, cs:cs + 1], in0=ssq.ap[:, 32 + 2 * eb:33 + 2 * eb], in1=ssq.ap[:, 33 + 2 * eb:34 + 2 * eb],
                    op=ALU.add),
                    reads=[(ssq.u, 32 + 2 * eb), (ssq.u, 33 + 2 * eb)], writes=[(ssq.u, cs)])
                rms_rstd(cs, 1.0 / D)
                grp = (eb + 1) // 2
                xrg, tm = xr[grp % 3], tmpd[eb % 2]
                kk = (eb + 1) % 2
                if eb == 0:
                    pg.add("sp", lambda e, xrg=xrg: e.dma_start(out=xrg.ap[:, 1, :], in_=x_d[NPRE * P:(NPRE + 1) * P, :]),
                           writes=[xrg], dma="xr%d" % (grp % 3))
                elif kk == 0:
                    pg.add("sp", lambda e, xrg=xrg, eb=eb: e.dma_start(
                        out=xrg.ap, in_=x_d[(NPRE + eb) * P:(NPRE + eb + 2) * P, :].rearrange("(k p) d -> p k d", p=P)),
                        writes=[xrg], dma="xr%d" % (grp % 3))
                xrb_ap = xrg.ap[:, kk, :]
                for hh in range(2):
                    pg.add("dve", lambda e, b_=bs[hh], hh=hh, tm=tm, cs=cs: e.scalar_tensor_tensor(
                        out=tm.ap[:, hh * 512:(hh + 1) * 512], in0=bank(b_), scalar=rstd.ap[:, cs:cs + 1],
                        in1=wpost_bc.ap[:, hh * 512:(hh + 1) * 512], op0=ALU.mult, op1=ALU.mult),
                        reads=[bank_u[bs[hh]], (rstd.u, cs), wpost_bc], writes=[(tm.u, hh)])
                pg.add("pool", lambda e, tm=tm, xrb_ap=xrb_ap, eb=eb: e.tensor_tensor(
                    out=x1.ap[:, eb, :], in0=tm.ap, in1=xrb_ap, op=ALU.add),
                    reads=[tm, xrg], writes=[(x1.u, eb)])
                pg.add("act", lambda e, eb=eb: e.activation(out=sqd.ap, in_=x1.ap[:, eb, :], func=AF.Square,
                                                            accum_out=ssq.ap[:, eb:eb + 1]),
                       reads=[(x1.u, eb)], writes=[(ssq.u, eb), sqd])
                rms_rstd(eb, 1.0 / D)
            locals_ref[0] = locals()
            add_dumps("D2")

        if nph >= 6:
            cw = Carver(68, ARENA_KIB)
            Wdn = cw("Wdn", [NFC, D], BF16)
            actb = cw("actb", [NFC, 512], BF16)
            xn2T = cw("xn2T", [KC, 512], BF16)
            Wup = [cw("Wup%d" % i, [KC, 2, 256], BF16) for i in range(2)]
            yag = [cw("yag%d" % i, [2, 512], F32) for i in range(3)]
            glb = [cw("glb%d" % i, [512], F32) for i in range(2)]
            wfpre_bc = cw("wfpre_bc", [D], F32)
            wfpost_bc = cw("wfpost_bc", [D], F32)
            xs2 = [cw("xs2%d" % i, [D], BF16) for i in range(2)]
            tmpe = [cw("tmpe%d" % i, [D], F32) for i in range(1)]
            bcast_load(wfpre_bc, "norm_ffn_pre")
            bcast_load(wfpost_bc, "norm_ffn_post")
            wdn_src = w_dn_d.rearrange("(fc p) n -> p fc n", p=P)
            for i, (f0, f1) in enumerate(((0, 6), (6, 12), (12, 18), (18, 22))):
                wload(Wdn.ap[:, f0:f1, :], (Wdn.u, i), wdn_src[:, f0:f1, :], "wdn%d" % i)
            Wdn_deps = [(Wdn.u, i) for i in range(4)]
            USETS = [(0, 1), (2, 3), (4, 5)]
            OSETS = [(6, 7), (0, 1)]
            TBKS = [6, 7]
            sqj = cw("sqj", [D], BF16)
            hA = SB("hA", [44, 2], F32)
            hB = SB("hB", [44], F32)
            GROUPS = [(ti_, fg_) for ti_ in range(len(EXT_TILES)) for fg_ in range(NFC // 2)]

            def load_group(k):
                if k >= len(GROUPS):
                    return
                fg_ = GROUPS[k][1]
                wu_ = Wup[k % 2]
                pg.add("sp", lambda e, wu_=wu_, fg_=fg_: e.dma_start(
                    out=wu_.ap.rearrange("p a b c -> p (a b c)"), in_=wup_scr[:, fg_, :]),
                    reads=[(scr_u, fg_)], writes=[wu_], dma="wup%d" % (k % 2))
            load_group(0)
            gn = 0

            def E1(ti_):
                if ti_ >= len(EXT_TILES):
                    return
                eb0_, nbt_ = EXT_TILES[ti_]
                for bl in range(nbt_):
                    eb = eb0_ + bl
                    xsb = xs2[eb % 2]
                    pg.add("dve", lambda e, eb=eb, xsb=xsb: e.scalar_tensor_tensor(
                        out=xsb.ap, in0=x1.ap[:, eb, :], scalar=rstd.ap[:, eb:eb + 1], in1=wfpre_bc.ap,
                        op0=ALU.mult, op1=ALU.mult),
                        reads=[(x1.u, eb), (rstd.u, eb), wfpre_bc], writes=[xsb])
                    transpose8(xsb.ap, xsb.u, TBKS[eb % 2], xn2T.ap[:, :, bl * P:(bl + 1) * P], (xn2T.u, bl), evac="act")
            E1(0)
            ocnt = 0
            for ti, (eb0, nb_) in enumerate(EXT_TILES):
                w_ = nb_ * P
                xn_deps = [(xn2T.u, bl) for bl in range(nb_)]
                first = ti == 0
                last = ti + 1 == len(EXT_TILES)
                if not first:
                    hup = halo_u.ap[:, (ti - 1) % 2, :, :]
                    pg.add("dve", lambda e, hup=hup: e.tensor_tensor(
                        out=hA.ap, in0=hup, in1=cw_pp.ap[:, 0, :].unsqueeze(2).to_broadcast([P, 44, 2]), op=ALU.mult),
                        reads=[halo_u, cw_pp], writes=[hA])
                    pg.add("dve", lambda e, hup=hup: e.tensor_tensor(
                        out=hB.ap, in0=hup[:, :, 1], in1=cw_pp.ap[:, 1, :], op=ALU.mult),
                        reads=[halo_u, cw_pp], writes=[hB])
                    pg.add("dve", lambda e: e.tensor_tensor(out=hA.ap[:, :, 0], in0=hA.ap[:, :, 0], in1=hB.ap, op=ALU.add),
                           reads=[hA, hB], writes=[hA])

                def S0(fi, gi):
                    fg, fl = fi // 2, fi % 2
                    k = ti * (NFC // 2) + fg
                    wu = Wup[k % 2]
                    if fl == 0:
                        load_group(k + 1)
                    bs = USETS[gi % 3]
                    c_lo = w_ - 2 if first else 0
                    for ag in range(2):
                        mm_group(bank(bs[ag])[:, c_lo:w_],
                                 [(wu.ap[:, kc, ag, fl * P:(fl + 1) * P], xn2T.ap[:, kc, c_lo:w_]) for kc in range(KC)],
                                 bank_u[bs[ag]], [wu] + xn_deps)

                def S1(fi, gi):
                    bs = USETS[gi % 3]
                    yb = yag[gi % 3]
                    for ag in range(2):
                        cidx = ag * NFC + fi
                        ups = bank(bs[ag])
                        if not last:
                            pg.add("act", lambda e, ups=ups, cidx=cidx, ti=ti, w_=w_: e.activation(
                                out=halo_u.ap[:, ti % 2, cidx, :], in_=ups[:, w_ - 2:w_], func=AF.Copy),
                                reads=[bank_u[bs[ag]]], writes=[(halo_u.u, (ti % 2, cidx))])
                        if first:
                            continue
                        pg.add("act", lambda e, ups=ups, yb=yb, cidx=cidx, ag=ag, w_=w_: e.activation(
                            out=yb.ap[:, ag, 0:w_], in_=ups[:, 0:w_], func=AF.Identity, scale=cw_pp.ap[:, 2, cidx:cidx + 1],
                            bias=cb_pp.ap[:, cidx:cidx + 1]),
                            reads=[bank_u[bs[ag]], cw_pp, cb_pp], writes=[(yb.u, ag)])
                    if not first:
                        hv = hA.ap.rearrange("p (a f) c -> p a f c", a=2)[:, :, fi, :]
                        pg.add("pool", lambda e, yb=yb, hv=hv: e.tensor_tensor(
                            out=yb.ap[:, :, 0:2], in0=yb.ap[:, :, 0:2], in1=hv, op=ALU.add),
                            reads=[hA, (yb.u, 0), (yb.u, 1)], writes=[(yb.u, 0), (yb.u, 1)])

                def S2(fi, gi):
                    if first:
                        return
                    bs = USETS[gi % 3]
                    yb = yag[gi % 3]
                    for ag in range(2):
                        cidx = ag * NFC + fi
                        ups = bank(bs[ag])
                        for (sh, jw) in ((1, 1), (2, 0)):
                            pg.add("dve", lambda e, ups=ups, yb=yb, cidx=cidx, sh=sh, jw=jw, ag=ag, w_=w_: e.scalar_tensor_tensor(
                                out=yb.ap[:, ag, sh:w_], in0=ups[:, 0:w_ - sh], scalar=cw_pp.ap[:, jw, cidx:cidx + 1],
                                in1=yb.ap[:, ag, sh:w_], op0=ALU.mult, op1=ALU.add),
                                reads=[bank_u[bs[ag]], cw_pp, (yb.u, ag)], writes=[(yb.u, ag)])

                def S3(fi, gi):
                    if first:
                        return
                    yb = yag[gi % 3]
                    gl = glb[gi % 2]
                    pg.add("act", lambda e, yb=yb, gl=gl, w_=w_: e.activation(out=gl.ap[:, 0:w_], in_=yb.ap[:, 1, 0:w_],
                                                                       func=AF.Gelu_apprx_tanh),
                           reads=[(yb.u, 1)], writes=[gl])
                    pg.add("pool", lambda e, yb=yb, gl=gl, fi=fi, w_=w_: e.tensor_tensor(
                        out=actb.ap[:, fi, 0:w_], in0=gl.ap[:, 0:w_], in1=yb.ap[:, 0, 0:w_], op=ALU.mult),
                        reads=[gl, (yb.u, 0)], writes=[(actb.u, fi)])

                for w in range(NFC + 2):
                    if w < NFC:
                        S0(w, gn + w)
                    if w == NFC - 1:
                        E1(ti + 1)
                    if 0 <= w - 1 < NFC:
                        S1(w - 1, gn + w - 1)
                    if 0 <= w - 2 < NFC:
                        S2(w - 2, gn + w - 2)
                        S3(w - 2, gn + w - 2)
                gn += NFC
                if first:
                    continue
                osets = [USETS[gn % 3], (6, 7)]
                SPLIT = 16

                def e3_mm(bl, hh, bs, f0, f1):
                    def f(e):
                        ins = None
                        for fc in range(f0, f1):
                            ins = e.matmul(bank(bs[hh]), lhsT=actb.ap[:, fc, bl * P:(bl + 1) * P],
                                           rhs=Wdn.ap[:, fc, hh * 512:(hh + 1) * 512], start=(fc == 0), stop=(fc == NFC - 1))
                        return ins
                    pg.add("pe", f, reads=[(actb.u, fc) for fc in range(f0, f1)] + Wdn_deps, writes=[bank_u[bs[hh]]])
                for bl in range(nb_):
                    eb = eb0 + bl
                    bs = osets[bl % 2]
                    tme = tmpe[0]
                    if bl == 0:
                        for hh in range(2):
                            e3_mm(bl, hh, bs, 0, SPLIT)
                        for hh in range(2):
                            e3_mm(bl, hh, bs, SPLIT, NFC)
                    else:
                        for hh in range(2):
                            e3_mm(bl, hh, bs, 0, NFC)
                    for hh in range(2):
                        c_ = 32 + 2 * eb + hh
                        pg.add("act", lambda e, b_=bs[hh], c_=c_, hh=hh, tme=tme: e.activation(
                            out=tme.ap[:, hh * 512:(hh + 1) * 512], in_=bank(b_), func=AF.Square,
                            accum_out=ssq.ap[:, c_:c_ + 1]),
                            reads=[bank_u[bs[hh]]], writes=[(ssq.u, c_), (tme.u, hh)])
                    cs = 66 + eb
                    pg.add("dve", lambda e, eb=eb, cs=cs: e.tensor_tensor(
                        out=ssq.ap[:, cs:cs + 1], in0=ssq.ap[:, 32 + 2 * eb:33 + 2 * eb],
                        in1=ssq.ap[:, 33 + 2 * eb:34 + 2 * eb], op=ALU.add),
                        reads=[(ssq.u, 32 + 2 * eb), (ssq.u, 33 + 2 * eb)], writes=[(ssq.u, cs)])
                    rms_rstd(cs, 1.0 / D)
                    for hh in range(2):
                        pg.add("dve", lambda e, b_=bs[hh], hh=hh, cs=cs, tme=tme: e.scalar_tensor_tensor(
                            out=tme.ap[:, hh * 512:(hh + 1) * 512], in0=bank(b_), scalar=rstd.ap[:, cs:cs + 1],
                            in1=wfpost_bc.ap[:, hh * 512:(hh + 1) * 512], op0=ALU.mult, op1=ALU.mult),
                            reads=[bank_u[bs[hh]], (rstd.u, cs), wfpost_bc], writes=[(tme.u, hh)])
                    pg.add("pool", lambda e, eb=eb, tme=tme: e.tensor_tensor(out=x1.ap[:, eb, :], in0=x1.ap[:, eb, :],
                                                                             in1=tme.ap, op=ALU.add),
                           reads=[tme, (x1.u, eb)], writes=[(x1.u, eb)])
                    pg.add("sp", lambda e, eb=eb: e.dma_start(out=out_d[(eb - 1) * P:eb * P, :], in_=x1.ap[:, eb, :]),
                           reads=[(x1.u, eb)], writes=[(out_u, "o%d" % eb)], dma="out%d" % (eb % 4))
            locals_ref[0] = locals()
            add_dumps("E")

        pg.add("sp", lambda e: e.nop(), reads=[out_u], name="final")
        nsem = pg.emit(nc, stack)
    return nc, dbg_d, nsem, pg.nops


def make_in_maps(inputs, cores=range(8)):
    x = np.asarray(inputs["x"], dtype=np.float32)
    g = lambda k: np.ascontiguousarray(np.asarray(inputs[k], dtype=np.float32)[0])
    shared = {
        "w_in": g("w_in"), "w_branch_a": g("w_branch_a"), "w_branch_b": g("w_branch_b"),
        "w_out": g("w_out"), "w_up": g("w_up"), "w_down": g("w_down"),
        "b_gate_ab": np.ascontiguousarray(np.concatenate([g("b_gate_a").reshape(8, P), g("b_gate_b").reshape(8, P)], 0)),
        "conv_w": np.ascontiguousarray(g("conv_w").reshape(3 * 44, P)),
        "conv_b": np.ascontiguousarray(g("conv_b").reshape(44, P)),
    }
    for nm in ("norm_mix_pre", "b_ml_i", "b_ml_f", "ml_head_norm", "b_fox_f", "norm_mix_post",
               "norm_ffn_pre", "norm_ffn_post"):
        shared[nm] = np.ascontiguousarray(g(nm).reshape(1, -1))
    maps = []
    for core in cores:
        b, hf = core // 2, core % 2
        km = np.zeros((P, NB), np.float32)
        if hf == 1:
            xa = np.ascontiguousarray(x[b])
        else:
            xa = np.concatenate([np.zeros((2048, D), np.float32), x[b, :2048]], 0)
            km[:, :16] = MASKNEG
        m = dict(shared)
        m["x"] = np.ascontiguousarray(xa)
        m["kmask"] = km
        maps.append(m)
    return maps


_CACHE = {}


def kernel(**inputs):
    if "nc" not in _CACHE:
        _CACHE["nc"] = build_program()[0]
    nc = _CACHE["nc"]
    maps = make_in_maps(inputs)
    res = run_bass_kernel_spmd(nc, maps, core_ids=list(range(8)))
    out = np.zeros((4, 4096, D), np.float32)
    for core in range(8):
        b, hf = core // 2, core % 2
        out[b, hf * 2048:(hf + 1) * 2048] = res.results[core]["out"]
    return out
```

```python
from contextlib import ExitStack

import numpy as np
import concourse.bass as bass
import concourse.mybir as mybir
from concourse.bass_utils import run_bass_kernel_spmd

F32 = mybir.dt.float32
BF16 = mybir.dt.bfloat16
AF = mybir.ActivationFunctionType
ALU = mybir.AluOpType

P = 128
D = 1024
KC = 8
NB = 32
NPRE = 15
NEXT = 17
TPRE = NPRE * P
TEXT = NEXT * P
D_IN = 8208
D_FF = 2816
NFC = 22
EPS = 1e-6
OFF_QM, OFF_KM, OFF_VM, OFF_IM, OFF_FM, OFF_OM = 0, 512, 1024, 2048, 2052, 2056
OFF_QF, OFF_KF, OFF_VF, OFF_FF, OFF_GA, OFF_GB = 3080, 4104, 5128, 6152, 6160, 7184
MASKNEG = -30000.0
EXT_TILES = [(0, 1), (1, 4), (5, 4), (9, 4), (13, 4)]
ALL = "__all__"
ARENA_KIB = 192


class Unit:
    __slots__ = ("name", "lo", "hi", "w", "r", "acc", "accw", "aliases")

    def __init__(self, name, lo=None, hi=None):
        self.name, self.lo, self.hi = name, lo, hi
        self.w, self.r, self.acc, self.accw, self.aliases = {}, {}, {}, {}, []


class Op:
    __slots__ = ("eng", "fn", "deps", "semkey", "order", "signals", "value", "dma", "name")


class Prog:
    ENGS = ("pe", "act", "dve", "pool", "sp")

    def __init__(self):
        self.eng_ops = {e: [] for e in self.ENGS}
        self.units = []
        self.dma_cnt = {}
        self.nops = 0

    def unit(self, name, lo=None, hi=None):
        u = Unit(name, lo, hi)
        if lo is not None:
            for o in self.units:
                if o.lo is not None and o.lo < hi and lo < o.hi:
                    o.aliases.append(u)
                    u.aliases.append(o)
        self.units.append(u)
        return u

    @staticmethod
    def _norm(lst):
        out = []
        for x in lst:
            if isinstance(x, Unit):
                out.append((x, ALL))
            elif isinstance(x, tuple):
                out.append(x)
            else:
                out.append((x.u, ALL))
        return out

    def add(self, eng, fn, reads=(), writes=(), dma=None, name=""):
        op = Op()
        op.eng, op.fn, op.name, op.dma = eng, fn, name, dma
        op.signals, op.value = False, None
        if dma is None:
            op.semkey = ("E", eng)
            op.order = len(self.eng_ops[eng])
        else:
            op.semkey = ("D", dma)
            self.dma_cnt[dma] = self.dma_cnt.get(dma, 0) + 1
            op.order = self.dma_cnt[dma]
        deps = {}

        def need(o):
            if o is None:
                return
            c = deps.get(o.semkey)
            if c is None or o.order > c.order:
                deps[o.semkey] = o

        reads, writes = self._norm(reads), self._norm(writes)
        for (u, sk) in reads:
            if sk is ALL:
                for o in u.w.values():
                    need(o)
            else:
                need(u.w.get(sk))
                need(u.w.get(ALL))
            for a in u.aliases:
                for o in a.accw.values():
                    need(o)
        for (u, sk) in writes:
            if sk is ALL:
                for o in u.w.values():
                    need(o)
                for d in u.r.values():
                    for o in d.values():
                        need(o)
            else:
                need(u.w.get(sk))
                need(u.w.get(ALL))
                for o in u.r.get(sk, {}).values():
                    need(o)
                for o in u.r.get(ALL, {}).values():
                    need(o)
            for a in u.aliases:
                for o in a.acc.values():
                    need(o)
        if eng == "pe" and dma is None:
            deps.pop(("E", "pe"), None)
        op.deps = []
        for sk_, o in deps.items():
            if o is op:
                continue
            o.signals = True
            val = None
            if sk_[0] == "D":
                val = 16 * (self.dma_cnt[sk_[1]] - (1 if sk_ == op.semkey else 0))
            op.deps.append((sk_, o, val))
        for (u, sk) in reads:
            u.r.setdefault(sk, {})[op.semkey] = op
            u.acc[op.semkey] = op
        for (u, sk) in writes:
            if sk is ALL:
                u.w = {ALL: op}
                u.r = {}
            else:
                u.w[sk] = op
                u.r[sk] = {}
            u.acc[op.semkey] = op
            u.accw[op.semkey] = op
        self.eng_ops[eng].append(op)
        self.nops += 1
        return op

    def emit(self, nc, stack):
        sems = {}

        def sem(key):
            if key not in sems:
                sems[key] = stack.enter_context(nc.semaphore("s_%s_%s" % (key[0], key[1])))
            return sems[key]

        for e in self.ENGS:
            n = 0
            for op in self.eng_ops[e]:
                if op.dma is None and op.signals:
                    n += 1
                    op.value = n
        engobj = {"pe": "tensor", "act": "scalar", "dve": "vector", "pool": "gpsimd", "sp": "sync"}
        with nc.Block() as block:
            for e in self.ENGS:
                ops = self.eng_ops[e]
                if not ops:
                    continue

                def body(eng, ops=ops):
                    waited = {}
                    for op in ops:
                        for (sk_, o, val) in op.deps:
                            v = val if sk_[0] == "D" else o.value
                            if waited.get(sk_, 0) >= v:
                                continue
                            eng.wait_ge(sem(sk_), v)
                            waited[sk_] = v
                        ins = op.fn(eng)
                        if op.dma is not None:
                            ins.then_inc(sem(op.semkey), 16)
                        elif op.signals:
                            ins.then_inc(sem(op.semkey), 1)

                getattr(block, engobj[e])(body)
        return len(sems)


def build_program(stop_after="E", dumps=()):
    nc = bass.Bass("TRN2", target_bir_lowering=False)
    pg = Prog()
    PH = ["A", "B0", "B", "C", "D1", "D2", "E"]
    nph = PH.index(stop_after)

    def dram(name, shape, kind="ExternalInput"):
        return nc.dram_tensor(name, list(shape), F32, kind=kind).ap()

    x_d = dram("x", [NB * P, D])
    kmask_d = dram("kmask", [P, NB])
    w_in_d = dram("w_in", [D, D_IN])
    w_a_d = dram("w_branch_a", [D, D])
    w_b_d = dram("w_branch_b", [D, D])
    w_out_d = dram("w_out", [D, D])
    w_up_d = dram("w_up", [D, 2 * D_FF])
    w_dn_d = dram("w_down", [D_FF, D])
    vec = {}
    for nm, n in (("norm_mix_pre", D), ("b_ml_i", 4), ("b_ml_f", 4), ("ml_head_norm", D),
                  ("b_fox_f", 8), ("norm_mix_post", D), ("norm_ffn_pre", D), ("norm_ffn_post", D)):
        vec[nm] = dram(nm, [1, n])
    bg_d = dram("b_gate_ab", [16, P])
    convw_d = dram("conv_w", [3 * 44, P])
    convb_d = dram("conv_b", [44, P])
    out_d = dram("out", [16 * P, D], kind="ExternalOutput")
    NG = NFC // 2
    wup_scr = nc.dram_tensor("wup_scr", [P, NG, KC * 2 * 256], BF16, kind="Internal").ap()
    dbg_d = {}

    stack = ExitStack()
    with stack:
        arena = stack.enter_context(nc.sbuf_tensor("arena", [P, ARENA_KIB * 512], BF16))
        psum = stack.enter_context(nc.psum_tensor("psum", [P, 4096], F32))

        class Buf:
            def __init__(self, name, lo, free, dt):
                n = int(np.prod(free))
                esz = 4 if dt == F32 else 2
                lo = (lo + 63) // 64 * 64
                self.lo, self.hi = lo, lo + n * esz
                assert self.hi <= ARENA_KIB * 1024, (name, self.hi)
                a = arena[:, lo // 2:self.hi // 2]
                if dt == F32:
                    a = a.bitcast(F32)
                if len(free) == 2:
                    a = a.rearrange("p (a b) -> p a b", a=free[0])
                elif len(free) == 3:
                    a = a.rearrange("p (a b c) -> p a b c", a=free[0], b=free[1])
                self.ap = a
                self.u = pg.unit(name, self.lo, self.hi)

        class Carver:
            def __init__(self, lo_kib, hi_kib):
                self.pos, self.hi = int(lo_kib * 1024), int(hi_kib * 1024)

            def __call__(self, name, free, dt):
                b = Buf(name, self.pos, free, dt)
                self.pos = b.hi
                assert self.pos <= self.hi, (name, self.pos, self.hi)
                return b

        class SB:
            def __init__(self, name, free, dt):
                self.t = stack.enter_context(nc.sbuf_tensor("sb_" + name, [P] + list(free), dt))
                self.ap = self.t[:]
                self.u = pg.unit(name)

        def bank(i, n=1):
            return psum[:, i * 512:(i + n) * 512]

        bank_u = [pg.unit("bank%d" % i) for i in range(8)]
        out_u = pg.unit("out_dram")

        def add_dumps(phase):
            for (ph, name, apf, shape) in dumps:
                if ph != phase:
                    continue
                ap, u = apf(locals_ref[0])
                dd = nc.dram_tensor(name, list(shape), ap.dtype, kind="ExternalOutput").ap()
                dbg_d[name] = dd
                pg.add("sp", lambda e, dd=dd, ap=ap: e.dma_start(out=dd, in_=ap), reads=[u], writes=[(out_u, name)],
                       dma="dbg_" + name)

        locals_ref = [None]

        ident_f = SB("ident_f", [P], F32)
        ident_b = SB("ident_b", [P], BF16)
        U_f = SB("U_f", [P], F32)
        M_b = SB("M_b", [P], BF16)
        ones_f = SB("ones_f", [P], F32)
        neghalf = SB("neghalf", [1], F32)
        ssq = SB("ssq", [96], F32)
        ms = SB("ms", [96], F32)
        rstd = SB("rstd", [96], F32)
        LF = SB("LF", [NB, 12], F32)
        LFh = SB("LFh", [NB, 12], BF16)
        LFl = SB("LFl", [NB, 12], BF16)
        es_t = SB("es_t", [NB, 4], F32)
        wk_t = SB("wk_t", [NB, 4], F32)
        a_t = SB("a_t", [NB, 4], F32)
        bias_all = SB("bias_all", [5, NB, 8], F32)
        kmask = SB("kmask", [NB], F32)
        bg_pp = SB("bg_pp", [16], F32)
        cw_pp = SB("cw_pp", [3, 44], F32)
        cb_pp = SB("cb_pp", [44], F32)
        halo_u = SB("halo_u", [2, 44, 2], F32)

        pg.add("pool", lambda e: e.memset(ones_f.ap, 1.0), writes=[ones_f])
        pg.add("pool", lambda e: e.memset(U_f.ap, 1.0), writes=[U_f])
        pg.add("pool", lambda e: e.memset(ident_f.ap, 1.0), writes=[ident_f])
        pg.add("pool", lambda e: e.memset(neghalf.ap, -0.5), writes=[neghalf])
        pg.add("pool", lambda e: e.memset(halo_u.ap, 0.0), writes=[halo_u])
        pg.add("pool", lambda e: e.affine_select(out=U_f.ap, in_=U_f.ap, pattern=[[1, P]],
                                                 compare_op=ALU.is_ge, fill=0.0, base=0,
                                                 channel_multiplier=-1),
               reads=[U_f], writes=[U_f])
        pg.add("pool", lambda e: e.affine_select(out=ident_f.ap, in_=ident_f.ap, pattern=[[-1, P]],
                                                 compare_op=ALU.is_equal, fill=0.0, base=0,
                                                 channel_multiplier=1),
               reads=[ident_f], writes=[ident_f])
        pg.add("dve", lambda e: e.tensor_copy(out=ident_b.ap, in_=ident_f.ap), reads=[ident_f], writes=[ident_b])
        pg.add("dve", lambda e: e.tensor_copy(out=M_b.ap, in_=U_f.ap), reads=[U_f], writes=[M_b])
        pg.add("sp", lambda e: e.dma_start(out=kmask.ap, in_=kmask_d[:, :]), writes=[kmask], dma="c_kmask")

        xnT_own = Carver(0, 34)("xnT_own", [KC, TEXT], BF16)
        xnT_pre = Carver(34, 64)("xnT_pre", [KC, TPRE], BF16)
        haT = Carver(64, 98)("haT", [KC, TEXT], BF16)
        hbT = Carver(98, 132)("hbT", [KC, TEXT], BF16)

        def xnT(tok0, n):
            if tok0 < TPRE:
                assert tok0 + n <= TPRE
                return xnT_pre, xnT_pre.ap[:, :, tok0:tok0 + n]
            return xnT_own, xnT_own.ap[:, :, tok0 - TPRE:tok0 - TPRE + n]

        def bcast_load(buf, name):
            pg.add("sp", lambda e: e.dma_start(out=buf.ap, in_=vec[name].partition_broadcast(P)),
                   writes=[buf], dma="c_" + name)

        def wload(buf_ap, buf_dep, src_ap, slot):
            pg.add("pool", lambda e: e.dma_start(out=buf_ap, in_=src_ap), writes=[buf_dep], dma=slot)

        def wsrc(w_d, c0, n):
            return w_d[:, c0:c0 + n].rearrange("(kc p) n -> p kc n", p=P)

        def rms_rstd(col, inv_n):
            pg.add("dve", lambda e: e.tensor_scalar(out=ms.ap[:, col:col + 1], in0=ssq.ap[:, col:col + 1],
                                                    scalar1=inv_n, scalar2=EPS, op0=ALU.mult, op1=ALU.add),
                   reads=[(ssq.u, col)], writes=[(ms.u, col)])
            pg.add("pool", lambda e: e.tensor_tensor(out=rstd.ap[:, col:col + 1], in0=ms.ap[:, col:col + 1],
                                                     in1=neghalf.ap, op=ALU.pow),
                   reads=[(ms.u, col), neghalf], writes=[(rstd.u, col)])

        def transpose8(src_ap, src_dep, bnk, dst_ap, dst_dep, n=KC, evac="act"):
            pb = bank(bnk).bitcast(BF16)

            def f(e):
                ins = None
                for kc in range(n):
                    ins = e.transpose(pb[:, kc * P:(kc + 1) * P], src_ap[:, kc * P:(kc + 1) * P], ident_b.ap)
                return ins
            pg.add("pe", f, reads=[src_dep, ident_b], writes=[bank_u[bnk]])
            pv = pb[:, 0:n * P]
            if n > 1:
                pv = pv.rearrange("p (a b) -> p a b", a=n)
            if evac == "act":
                pg.add("act", lambda e: e.activation(out=dst_ap, in_=pv, func=AF.Copy),
                       reads=[bank_u[bnk]], writes=[dst_dep])
            else:
                pg.add("dve", lambda e: e.tensor_copy(out=dst_ap, in_=pv), reads=[bank_u[bnk]], writes=[dst_dep])

        def mm_group(out_ap, pairs, bnk_dep, reads):
            def f(e):
                ins = None
                n = len(pairs)
                for i, (l, r) in enumerate(pairs):
                    ins = e.matmul(out_ap, lhsT=l, rhs=r, start=(i == 0), stop=(i == n - 1))
                return ins
            pg.add("pe", f, reads=reads, writes=[bnk_dep])

        cw = Carver(132, ARENA_KIB)
        wpre_bc = cw("wpre_bc", [D], F32)
        xin = [cw("xin%d" % i, [2, D], F32) for i in range(5)]
        xs = [cw("xs%d" % i, [D], BF16) for i in range(2)]
        jk = [cw("jk%d" % i, [D], BF16) for i in range(2)]
        Wg = cw("Wg", [KC, 16], BF16)
        bias16 = cw("bias16", [16], F32)
        bcast_load(wpre_bc, "norm_mix_pre")
        GB = 7
        if nph >= 1:
            wload(Wg.ap[:, :, 0:8], Wg.u, wsrc(w_in_d, OFF_IM, 8), "wg")
            wload(Wg.ap[:, :, 8:16], Wg.u, wsrc(w_in_d, OFF_FF, 8), "wg")
            for (nm, c0, n) in (("b_ml_i", 0, 4), ("b_ml_f", 4, 4), ("b_fox_f", 8, 8)):
                pg.add("sp", lambda e, nm=nm, c0=c0, n=n: e.dma_start(
                    out=bias16.ap[:, c0:c0 + n], in_=vec[nm].partition_broadcast(P)),
                    writes=[bias16], dma="c_bias16")

        def A_stats(b):
            if b >= NB:
                return
            g_ = b // 2
            xg = xin[g_ % 5]
            if b % 2 == 0:
                pg.add("sp", lambda e: e.dma_start(
                    out=xg.ap, in_=x_d[g_ * 2 * P:(g_ + 1) * 2 * P, :].rearrange("(k p) d -> p k d", p=P)),
                    writes=[xg], dma="xin%d" % (g_ % 5))
            pg.add("act", lambda e: e.activation(out=jk[b % 2].ap, in_=xg.ap[:, b % 2, :], func=AF.Square,
                                                 accum_out=ssq.ap[:, b:b + 1]),
                   reads=[xg], writes=[(ssq.u, b), jk[b % 2]])
            rms_rstd(b, 1.0 / D)

        def A_scale_T(b):
            xg, xsb = xin[(b // 2) % 5], xs[b % 2]
            pg.add("dve", lambda e: e.scalar_tensor_tensor(
                out=xsb.ap, in0=xg.ap[:, b % 2, :], scalar=rstd.ap[:, b:b + 1], in1=wpre_bc.ap, op0=ALU.mult, op1=ALU.mult),
                reads=[xg, (rstd.u, b), wpre_bc], writes=[xsb])
            pb = bank(b % 2).bitcast(BF16)

            def f(e):
                ins = None
                for kc in range(KC):
                    ins = e.transpose(pb[:, kc * P:(kc + 1) * P], xsb.ap[:, kc * P:(kc + 1) * P], ident_b.ap)
                return ins
            pg.add("pe", f, reads=[xsb, ident_b], writes=[bank_u[b % 2]])

        def A_evac(b):
            if b < 0:
                return
            buf, dst = xnT(b * P, P)
            pv = bank(b % 2).bitcast(BF16).rearrange("p (a b) -> p a b", a=KC)
            if b % 2 == 0:
                pg.add("act", lambda e: e.activation(out=dst, in_=pv, func=AF.Copy),
                       reads=[bank_u[b % 2]], writes=[(buf.u, b)])
            else:
                pg.add("dve", lambda e: e.tensor_copy(out=dst, in_=pv), reads=[bank_u[b % 2]], writes=[(buf.u, b)])
            if nph >= 1:
                mm_group(bank(GB)[:, b * 16:(b + 1) * 16],
                         [(dst[:, kc, :], Wg.ap[:, kc, :]) for kc in range(KC)],
                         (bank_u[GB], b), [(buf.u, b), Wg])

        for b_ in range(6):
            A_stats(b_)
        for b in range(NB):
            A_stats(b + 6)
            A_scale_T(b)
            A_evac(b - 1)
        A_evac(NB - 1)
        locals_ref[0] = locals()
        add_dumps("A")

        if nph >= 1:
            cw = Carver(160, ARENA_KIB)
            gates = cw("gates", [NB, 16], F32)
            cap8 = cw("cap8", [NB, 8], F32)
            e1 = cw("e1", [NB, 12], F32)
            spl = cw("spl", [NB, 12], F32)
            loc = cw("loc", [NB, 12], F32)
            tot = cw("tot", [NB, 12], F32)
            dbias = cw("dbias", [NB, 4], F32)
            wkarg = cw("wkarg", [NB, 4], F32)
            incl = cw("incl", [NB, 8], F32)
            carry = cw("carry", [NB, 8], F32)
            Gt = cw("Gt", [NB, 8], F32)
            negGm = cw("negGm", [NB, 8], F32)
            gps = bank(GB).rearrange("p (a b) -> p a b", a=NB)
            pg.add("dve", lambda e: e.tensor_tensor(out=gates.ap, in0=gps,
                                                    in1=bias16.ap.unsqueeze(1).to_broadcast([P, NB, 16]),
                                                    op=ALU.add),
                   reads=[bank_u[GB], bias16], writes=[gates])
            pg.add("act", lambda e: e.activation(out=cap8.ap, in_=gates.ap[:, :, 0:8], func=AF.Tanh, scale=1.0 / 15.0),
                   reads=[gates], writes=[cap8])
            pg.add("dve", lambda e: e.tensor_scalar(out=gates.ap[:, :, 0:8], in0=cap8.ap, scalar1=15.0, scalar2=None,
                                                    op0=ALU.mult),
                   reads=[cap8], writes=[gates])
            pg.add("act", lambda e: e.activation(out=e1.ap, in_=gates.ap[:, :, 4:16], func=AF.Exp, scale=-1.0),
                   reads=[gates], writes=[e1])
            pg.add("act", lambda e: e.activation(out=spl.ap, in_=e1.ap, func=AF.Ln, bias=1.0, scale=1.0),
                   reads=[e1], writes=[spl])
            pg.add("dve", lambda e: e.tensor_scalar(out=LF.ap, in0=spl.ap, scalar1=-1.0, scalar2=None, op0=ALU.mult),
                   reads=[spl], writes=[LF])
            pg.add("dve", lambda e: e.tensor_copy(out=LFh.ap, in_=LF.ap), reads=[LF], writes=[LFh])
            pg.add("dve", lambda e: e.tensor_tensor(out=LFl.ap, in0=LF.ap, in1=LFh.ap, op=ALU.subtract),
                   reads=[LF, LFh], writes=[LFl])
            LF2 = LF.ap.rearrange("p a b -> p (a b)")
            mm_group(bank(GB)[:, 0:384], [(U_f.ap, LF2)], bank_u[GB], [U_f, LF])
            pg.add("act", lambda e: e.activation(out=loc.ap.rearrange("p a b -> p (a b)"), in_=bank(GB)[:, 0:384],
                                                 func=AF.Copy),
                   reads=[bank_u[GB]], writes=[loc])
            mm_group(bank(GB)[:, 0:384], [(ones_f.ap, LF2)], bank_u[GB], [ones_f, LF])
            pg.add("act", lambda e: e.activation(out=tot.ap.rearrange("p a b -> p (a b)"), in_=bank(GB)[:, 0:384],
                                                 func=AF.Copy),
                   reads=[bank_u[GB]], writes=[tot])
            pg.add("dve", lambda e: e.tensor_tensor(out=dbias.ap, in0=gates.ap[:, :, 0:4], in1=loc.ap[:, :, 0:4],
                                                    op=ALU.subtract),
                   reads=[gates, loc], writes=[dbias])
            pg.add("act", lambda e: e.activation(out=es_t.ap, in_=dbias.ap, func=AF.Exp), reads=[dbias], writes=[es_t])
            pg.add("dve", lambda e: e.tensor_tensor(out=wkarg.ap, in0=dbias.ap, in1=tot.ap[:, :, 0:4], op=ALU.add),
                   reads=[dbias, tot], writes=[wkarg])
            pg.add("act", lambda e: e.activation(out=wk_t.ap, in_=wkarg.ap, func=AF.Exp), reads=[wkarg], writes=[wk_t])
            pg.add("act", lambda e: e.activation(out=a_t.ap, in_=tot.ap[:, :, 0:4], func=AF.Exp), reads=[tot], writes=[a_t])
            for h in range(8):
                pg.add("dve", lambda e, h=h: e.tensor_tensor_scan(
                    out=incl.ap[:, :, h], data0=ones_f.ap[:, 0:NB], data1=tot.ap[:, :, 4 + h], initial=0.0,
                    op0=ALU.mult, op1=ALU.add),
                    reads=[tot, ones_f], writes=[(incl.u, h)])
            pg.add("dve", lambda e: e.tensor_tensor(out=carry.ap, in0=incl.ap, in1=tot.ap[:, :, 4:12], op=ALU.subtract),
                   reads=[incl, tot], writes=[carry])
            pg.add("dve", lambda e: e.tensor_tensor(out=Gt.ap, in0=loc.ap[:, :, 4:12], in1=carry.ap, op=ALU.add),
                   reads=[loc, carry], writes=[Gt])
            pg.add("dve", lambda e: e.scalar_tensor_tensor(
                out=negGm.ap, in0=Gt.ap, scalar=-1.0, in1=kmask.ap.unsqueeze(2).to_broadcast([P, NB, 8]),
                op0=ALU.mult, op1=ALU.add),
                reads=[Gt, kmask], writes=[negGm])
            for ti, (eb0, nb_) in enumerate(EXT_TILES):
                r = NPRE + eb0 + (2 if nb_ == 4 else 0)
                pg.add("dve", lambda e, ti=ti, r=r: e.tensor_tensor(
                    out=bias_all.ap[:, ti, :, :], in0=negGm.ap,
                    in1=carry.ap[:, r:r + 1, :].to_broadcast([P, NB, 8]), op=ALU.add),
                    reads=[negGm, carry], writes=[(bias_all.u, ti)])
            locals_ref[0] = locals()
            add_dumps("B0")

        if nph >= 2:
            cw = Carver(98, ARENA_KIB)
            Wm = cw("Wm", [KC, 3072], BF16)
            halfw = cw("halfw", [D], F32)
            qkT = cw("qkT", [2, 4, 512], BF16)
            ksc = [cw("ksc%d" % i, [4, P], BF16) for i in range(2)]
            vext = [cw("vext%d" % i, [4, 257], BF16) for i in range(2)]
            Gw = [cw("Gw%d" % i, [D], F32) for i in range(2)]
            EB = [cw("EB0", [4, P], F32)] * 2
            EBM = [cw("EBM0", [4, P], F32)] * 2
            ST = [cw("ST%d" % i, [4, P], BF16) for i in range(2)]
            qs = [cw("qs%d" % i, [4, P], BF16) for i in range(2)]
            hablk = [cw("hablk%d" % i, [D], BF16) for i in range(2)]
            CT = cw("CT", [4, 257], F32)
            CTb = cw("CTb", [4, 257], BF16)
            sc = SB("sc", [2, 4, 8], F32)
            for i, (c0, n) in enumerate(((0, 512), (512, 512), (1024, 512), (1536, 512))):
                wload(Wm.ap[:, :, c0:c0 + n], (Wm.u, i), wsrc(w_in_d, c0, n), "wm%d" % i)
            for i in range(2):
                wload(Wm.ap[:, :, 2048 + i * 512:2048 + (i + 1) * 512], (Wm.u, 4 + i),
                      wsrc(w_in_d, OFF_OM + i * 512, 512), "wm%d" % (4 + i))
            bcast_load(halfw, "ml_head_norm")
            pg.add("dve", lambda e: e.tensor_scalar(out=halfw.ap, in0=halfw.ap, scalar1=0.5, scalar2=None, op0=ALU.mult),
                   reads=[halfw], writes=[halfw])
            pg.add("pool", lambda e: e.memset(CT.ap, 0.0), writes=[CT])
            pg.add("pool", lambda e: e.memset(CTb.ap, 0.0), writes=[CTb])
            for i in range(2):
                pg.add("pool", lambda e, i=i: e.memset(vext[i].ap[:, :, 256:257], 1.0), writes=[(vext[i].u, "ones")])
            PB = [0, 1, 2]
            pbi = [0]

            def nextbank():
                b_ = PB[pbi[0] % 3]
                pbi[0] += 1
                return b_
            SB0, SB1, RB0, RB1, TB = 3, 4, 5, 6, 7
            tile_of = {}
            for (eb0, nb_) in EXT_TILES:
                for j in range(nb_):
                    tile_of[NPRE + eb0 + j] = (eb0, nb_)

            def P_parts(c):
                own = c >= NPRE
                buf, xv = xnT(c * P, P)
                xdep = (buf.u, c)
                ks, ve = ksc[c % 2], vext[c % 2]

                def part_qk():
                    eb0, nb_ = tile_of[c]
                    if NPRE + eb0 != c:
                        return
                    w_ = nb_ * P
                    tb, tv = xnT(c * P, w_)
                    for qk in range(2):
                        for h in range(4):
                            bk = nextbank()
                            col = qk * 512 + h * P
                            mm_group(bank(bk)[:, 0:w_],
                                     [(Wm.ap[:, kc, col:col + P], tv[:, kc, :]) for kc in range(KC)],
                                     bank_u[bk], [(Wm.u, qk)] + [(tb.u, c + j) for j in range(nb_)])
                            sc_ = (P ** -0.5) if qk == 0 else 1.0
                            pg.add("act", lambda e, bk=bk, qk=qk, h=h, w_=w_, sc_=sc_: e.activation(
                                out=qkT.ap[:, qk, h, 0:w_], in_=bank(bk)[:, 0:w_], func=AF.Copy, scale=sc_),
                                reads=[bank_u[bk]], writes=[(qkT.u, (qk, h))])

                def part_k():
                    if own:
                        part_qk()
                    bk = nextbank()
                    if own:
                        eb0_, _ = tile_of[c]
                        lc_ = (c - NPRE - eb0_) * P
                        pbk = bank(bk).bitcast(BF16)

                        def f(e):
                            ins = None
                            for h in range(4):
                                ins = e.transpose(pbk[:, h * P:(h + 1) * P], qkT.ap[:, 1, h, lc_:lc_ + P], ident_b.ap)
                            return ins
                        pg.add("pe", f, reads=[(qkT.u, (1, h)) for h in range(4)] + [ident_b], writes=[bank_u[bk]])
                        ksrc = pbk
                    else:
                        mm_group(bank(bk), [(xv[:, kc, :], Wm.ap[:, kc, 512:1024]) for kc in range(KC)],
                                 bank_u[bk], [xdep, (Wm.u, 1)])
                        ksrc = bank(bk)
                    for h in range(4):
                        pg.add("dve", lambda e, h=h: e.tensor_scalar(
                            out=ks.ap[:, h, :], in0=ksrc[:, h * P:(h + 1) * P], scalar1=wk_t.ap[:, c, h:h + 1],
                            scalar2=None, op0=ALU.mult),
                            reads=[bank_u[bk], wk_t], writes=[(ks.u, h)])

                def part_v():
                    for hv in range(2):
                        bk = nextbank()
                        mm_group(bank(bk), [(xv[:, kc, :], Wm.ap[:, kc, 1024 + hv * 512:1536 + hv * 512]) for kc in range(KC)],
                                 bank_u[bk], [xdep, (Wm.u, 2 + hv)])
                        pg.add("act", lambda e, bk=bk, hv=hv: e.activation(
                            out=ve.ap[:, 2 * hv:2 * hv + 2, 0:256], in_=bank(bk).rearrange("p (a b) -> p a b", a=2),
                            func=AF.Copy),
                            reads=[bank_u[bk]], writes=[(ve.u, hv)])

                def part_o():
                    if not own:
                        return
                    gw = Gw[c % 2]
                    for ho in range(2):
                        bk = nextbank()
                        mm_group(bank(bk), [(xv[:, kc, :], Wm.ap[:, kc, 2048 + ho * 512:2560 + ho * 512]) for kc in range(KC)],
                                 bank_u[bk], [xdep, (Wm.u, 4 + ho)])
                        sl = slice(ho * 512, (ho + 1) * 512)
                        pg.add("act", lambda e, bk=bk, sl=sl: e.activation(out=gw.ap[:, sl], in_=bank(bk), func=AF.Tanh, scale=0.5),
                               reads=[bank_u[bk]], writes=[(gw.u, ho)])
                        pg.add("dve", lambda e, sl=sl: e.scalar_tensor_tensor(
                            out=gw.ap[:, sl], in0=gw.ap[:, sl], scalar=1.0, in1=halfw.ap[:, sl], op0=ALU.add, op1=ALU.mult),
                            reads=[(gw.u, ho), halfw], writes=[(gw.u, ho)])
                return [part_k, part_v, part_o]

            def I_stage(c):
                if c >= NB or c < NPRE:
                    return
                eb0, nb_ = tile_of[c]
                lc = (c - NPRE - eb0) * P
                eb_, ebm_, st_, qs_ = EB[c % 2], EBM[c % 2], ST[c % 2], qs[c % 2]
                for h in range(4):
                    mm_group(bank(SB0)[:, h * P:(h + 1) * P],
                             [(LFh.ap[:, c, h:h + 1].to_broadcast([P, P]), M_b.ap),
                              (LFl.ap[:, c, h:h + 1].to_broadcast([P, P]), M_b.ap)],
                             (bank_u[SB0], h), [LFh, LFl, M_b])
                for h in range(4):
                    mm_group(bank(SB1)[:, h * P:(h + 1) * P],
                             [(qkT.ap[:, 1, h, lc:lc + P], qkT.ap[:, 0, h, lc:lc + P])],
                             (bank_u[SB1], h), [(qkT.u, (0, h)), (qkT.u, (1, h))])
                for h in range(4):
                    pg.add("act", lambda e, h=h: e.activation(out=eb_.ap[:, h, :], in_=bank(SB0)[:, h * P:(h + 1) * P],
                                                              func=AF.Exp),
                           reads=[bank_u[SB0]], writes=[(eb_.u, h)])
                for h in range(4):
                    pg.add("dve", lambda e, h=h: e.tensor_tensor(out=ebm_.ap[:, h, :], in0=eb_.ap[:, h, :], in1=U_f.ap,
                                                                 op=ALU.mult),
                           reads=[(eb_.u, h), U_f], writes=[(ebm_.u, h)])
                    pg.add("dve", lambda e, h=h: e.tensor_tensor(
                        out=qs_.ap[:, h, :], in0=qkT.ap[:, 0, h, lc:lc + P], in1=eb_.ap[:, h, :], op=ALU.mult),
                        reads=[(eb_.u, h), (qkT.u, (0, h))], writes=[(qs_.u, h)])
                for h in range(4):
                    pg.add("dve", lambda e, h=h: e.scalar_tensor_tensor(
                        out=st_.ap[:, h, :], in0=bank(SB1)[:, h * P:(h + 1) * P], scalar=es_t.ap[:, c, h:h + 1],
                        in1=ebm_.ap[:, h, :], op0=ALU.mult, op1=ALU.mult),
                        reads=[bank_u[SB1], es_t, (ebm_.u, h)], writes=[(st_.u, h)])

            def R_num(c, h):
                ve = vext[c % 2]
                st_, qs_ = ST[c % 2], qs[c % 2]
                gw, hb_ = Gw[c % 2], hablk[c % 2]
                rb = RB0 if h % 2 == 0 else RB1
                nps = bank(rb)[:, 0:257]
                mm_group(nps, [(st_.ap[:, h, :], ve.ap[:, h, :]), (qs_.ap[:, h, :], CTb.ap[:, h, :])],
                         bank_u[rb], [(st_.u, h), (ve.u, h // 2), (ve.u, "ones"), (qs_.u, h), (CTb.u, h)])
                s0 = sc.ap[:, c % 2, h, :]
                su = (sc.u, (c % 2, h))
                pg.add("act", lambda e: e.activation(out=hb_.ap[:, h * 256:(h + 1) * 256], in_=nps[:, 0:256], func=AF.Square,
                                                     accum_out=s0[:, 0:1]),
                       reads=[bank_u[rb]], writes=[su, (hb_.u, h)])
                pg.add("act", lambda e: e.activation(out=s0[:, 1:2], in_=nps[:, 256:257], func=AF.Square),
                       reads=[bank_u[rb]], writes=[su])
                pg.add("dve", lambda e: e.tensor_scalar(out=s0[:, 2:3], in0=s0[:, 1:2], scalar1=1.0, scalar2=EPS,
                                                        op0=ALU.max, op1=ALU.mult),
                       reads=[su], writes=[su])
                pg.add("dve", lambda e: e.scalar_tensor_tensor(out=s0[:, 3:4], in0=s0[:, 0:1], scalar=1.0 / 256.0,
                                                               in1=s0[:, 2:3], op0=ALU.mult, op1=ALU.add),
                       reads=[su], writes=[su])
                pg.add("pool", lambda e: e.tensor_tensor(out=s0[:, 4:5], in0=s0[:, 3:4], in1=neghalf.ap, op=ALU.pow),
                       reads=[su, neghalf], writes=[su])
                pg.add("dve", lambda e: e.scalar_tensor_tensor(
                    out=hb_.ap[:, h * 256:(h + 1) * 256], in0=nps[:, 0:256], scalar=s0[:, 4:5],
                    in1=gw.ap[:, h * 256:(h + 1) * 256], op0=ALU.mult, op1=ALU.mult),
                    reads=[bank_u[rb], su, (gw.u, h // 2)], writes=[(hb_.u, h)])

            def R_upd(c, h):
                ks, ve = ksc[c % 2], vext[c % 2]
                rb = RB0 if h % 2 == 0 else RB1
                ups = bank(rb)[:, 0:257]
                mm_group(ups, [(ks.ap[:, h, :], ve.ap[:, h, :])], bank_u[rb],
                         [(ks.u, h), (ve.u, h // 2), (ve.u, "ones")])
                pg.add("dve", lambda e: e.scalar_tensor_tensor(
                    out=CT.ap[:, h, :], in0=CT.ap[:, h, :], scalar=a_t.ap[:, c, h:h + 1], in1=ups,
                    op0=ALU.mult, op1=ALU.add),
                    reads=[bank_u[rb], a_t, (CT.u, h)], writes=[(CT.u, h)])
                pg.add("act", lambda e: e.activation(out=CTb.ap[:, h, :], in_=CT.ap[:, h, :], func=AF.Copy),
                       reads=[(CT.u, h)], writes=[(CTb.u, h)])

            def T_stage(c):
                if c < NPRE or c >= NB:
                    return
                ec = c - NPRE
                hb_ = hablk[c % 2]
                transpose8(hb_.ap, hb_.u, TB, haT.ap[:, :, ec * P:(ec + 1) * P], (haT.u, ec))

            scr_u = pg.unit("wup_scr")
            scr_v = wup_scr.rearrange("p g (kc ag n) -> p g kc ag n", kc=KC, ag=2)
            conv_jobs = []
            for g_ in range(NG):
                def cj(g_=g_):
                    for ag_ in range(2):
                        pg.add("pool", lambda e, g_=g_, ag_=ag_: e.dma_start(
                            out=scr_v[:, g_, :, ag_, :], in_=wsrc(w_up_d, ag_ * D_FF + g_ * 256, 256)),
                            writes=[(scr_u, g_)], dma="scr%d" % (g_ % 4))
                conv_jobs.append(cj)
            for p_ in P_parts(0):
                p_()
            I_stage(0)
            for c in range(NB):
                own = c >= NPRE
                nxt = P_parts(c + 1) if c + 1 < NB else [lambda: None] * 3
                if own:
                    R_num(c, 0)
                    R_num(c, 1)
                    nxt[0]()
                    I_stage(c + 1)
                    R_num(c, 2)
                    R_num(c, 3)
                    nxt[1]()
                    R_upd(c, 0)
                    R_upd(c, 1)
                    nxt[2]()
                    R_upd(c, 2)
                    R_upd(c, 3)
                else:
                    R_upd(c, 0)
                    R_upd(c, 1)
                    nxt[0]()
                    I_stage(c + 1)
                    R_upd(c, 2)
                    R_upd(c, 3)
                    nxt[1]()
                    nxt[2]()
                T_stage(c - 1)
                if c % 2 == 0 and c // 2 < NG:
                    conv_jobs[c // 2]()
            T_stage(NB - 1)
            locals_ref[0] = locals()
            add_dumps("B")

        if nph >= 3:
            cw = Carver(132, 190)
            kT = [cw("kT%d" % i, [NB * P], BF16) for i in range(2)]
            vxp = cw("vxp", [NB, 2, 129], BF16)
            qT = [cw("qT%d" % i, [TEXT], BF16) for i in range(2)]
            Wf = [cw("Wf%d" % i, [KC, 256], BF16) for i in range(2)]
            Wfv = cw("Wfv", [KC, 256], BF16)
            pT = [cw("pT%d" % i, [512], BF16) for i in range(4)]
            fsc = cw("fsc", [4, 2], F32)
            pg.add("pool", lambda e: e.memset(vxp.ap[:, :, :, 128:129], 1.0), writes=[(vxp.u, "ones")])
            LG = [0, 1, 7]
            ACC = [2, 3, 4, 5]
            PBK = [6, 0, 1, 7]
            pbc = [0]

            def nextb():
                b_ = PBK[pbc[0] % 4]
                pbc[0] += 1
                return b_
            def evac(out_ap, in_ap, reads, writes):
                pg.add("dve", lambda e: e.tensor_copy(out=out_ap, in_=in_ap), reads=reads, writes=writes)
            WIN_TILES = [(0, 512), (512, 512), (1024, 512), (1536, 384)] + \
                        [(TPRE + eb0 * P, nb_ * P) for (eb0, nb_) in EXT_TILES]
            hbtok = [cw("hbtk%d" % i, [P], BF16) for i in range(4)]

            def proj_groups(h):
                s_ = h % 2
                wf = Wf[s_]
                out = []

                def ld():
                    for i, off in enumerate((OFF_QF, OFF_KF)):
                        wload(wf.ap[:, :, i * P:(i + 1) * P], wf.u, wsrc(w_in_d, off + h * P, P), "wf%d" % s_)
                out.append(ld)
                gidx = 0
                for (t0, w_) in WIN_TILES:
                    for which in ((1, 0) if t0 >= TPRE else (1,)):
                        gbk = PBK[gidx % 4] if h == 0 else 6
                        gidx += 1
                        for piece in range(4):
                            def pc(t0=t0, w_=w_, which=which, piece=piece, gbk=gbk):
                                tb, tv = xnT(t0, w_)
                                blks = list(range(t0 // P, (t0 + w_) // P))
                                bk = gbk

                                def f(e):
                                    ins = None
                                    for kc in (2 * piece, 2 * piece + 1):
                                        ins = e.matmul(bank(bk)[:, 0:w_], lhsT=wf.ap[:, kc, which * P:(which + 1) * P],
                                                       rhs=tv[:, kc, :], start=(kc == 0), stop=(kc == KC - 1))
                                    return ins
                                pg.add("pe", f, reads=[wf] + [(tb.u, j) for j in blks], writes=[bank_u[bk]])
                                if piece == 3:
                                    if which == 1:
                                        evac(kT[s_].ap[:, t0:t0 + w_], bank(bk)[:, 0:w_], [bank_u[bk]],
                                             [(kT[s_].u, j) for j in blks])
                                    else:
                                        evac(qT[s_].ap[:, t0 - TPRE:t0 - TPRE + w_], bank(bk)[:, 0:w_], [bank_u[bk]],
                                             [(qT[s_].u, j) for j in blks])
                            pc.closes = (piece == 3)
                            out.append(pc)
                return out

            def vpair_load(p_):
                if p_ >= 4:
                    return
                wload(Wfv.ap, Wfv.u, wsrc(w_in_d, OFF_VF + p_ * 2 * P, 2 * P), "wfv")

            def vpair_proj(p_):
                for g in range(NB // 2):
                    bk = nextb()
                    for jj in range(2):
                        j = g * 2 + jj
                        tb, tv = xnT(j * P, P)
                        mm_group(bank(bk)[:, jj * 256:(jj + 1) * 256],
                                 [(tv[:, kc, :], Wfv.ap[:, kc, :]) for kc in range(KC)],
                                 (bank_u[bk], jj), [Wfv, (tb.u, j)])
                    evac(vxp.ap[:, g * 2:(g + 1) * 2, :, 0:P],
                         bank(bk).rearrange("p (a b c) -> p a b c", a=2, b=2),
                         [bank_u[bk]], [(vxp.u, g)])
            vpair_load(0)
            pgs = {h_: proj_groups(h_) for h_ in range(8)}
            lds = {h_: pgs[h_].pop(0) for h_ in range(8)}
            lds[0]()
            for f_ in pgs[0]:
                f_()
            lds[1]()
            gstep = [0]
            for h in range(8):
                s_ = h % 2
                pend = pgs[h + 1] if h + 1 < 8 else []
                if h + 2 < 8:
                    lds[h + 2]()
                if h % 2 == 0:
                    vpair_proj(h // 2)
                    vpair_load(h // 2 + 1)
                npend = len(pend)
                HOLD = 0
                steps = []
                for ti, (eb0, nb_) in enumerate(EXT_TILES):
                    b0 = NPRE + eb0
                    for j in range(b0 + nb_):
                        steps.append((ti, eb0, nb_, b0, j))
                nst = len(steps)

                def qk(n):
                    ti, eb0, nb_, b0, j = steps[n]
                    w_ = nb_ * P
                    c0 = max(0, j - b0) * P
                    lgb = LG[(gstep[0] + n) % 3]
                    mm_group(bank(lgb)[:, 0:w_ - c0],
                             [(kT[s_].ap[:, j * P:(j + 1) * P], qT[s_].ap[:, eb0 * P + c0:eb0 * P + w_])],
                             bank_u[lgb], [(kT[s_].u, j)] + [(qT[s_].u, b0 + i) for i in range(nb_)])
                qk(0)
                if nst > 1:
                    qk(1)
                deferred = []
                lagged = []
                grp_open = [False]
                for n in range(nst):
                    ti, eb0, nb_, b0, j = steps[n]
                    w_ = nb_ * P
                    c0 = max(0, j - b0) * P
                    if n + 2 < nst:
                        qk(n + 2)
                    want = (max(0, n + 1 - HOLD) * npend + (nst - HOLD) - 1) // (nst - HOLD)
                    while npend - len(pend) < want and pend:
                        f_ = pend.pop(0)
                        f_()
                        grp_open[0] = not getattr(f_, "closes", True)
                    if lagged:
                        lagged.pop(0)()
                    lgb = LG[(gstep[0] + n) % 3]
                    pt = pT[(gstep[0] + n) % 4]
                    pg.add("act", lambda e, pt=pt, lgb=lgb, c0=c0, w_=w_, ti=ti, j=j, h=h: e.activation(
                        out=pt.ap[:, c0:w_], in_=bank(lgb)[:, 0:w_ - c0], func=AF.Exp, scale=float(P) ** -0.5,
                        bias=bias_all.ap[:, ti, j, h:h + 1]),
                        reads=[bank_u[lgb], (bias_all.u, ti)], writes=[pt])
                    if j >= b0:
                        pg.add("pool", lambda e, pt=pt, c0=c0: e.tensor_tensor(
                            out=pt.ap[:, c0:c0 + P], in0=pt.ap[:, c0:c0 + P], in1=M_b.ap, op=ALU.mult),
                            reads=[pt, M_b], writes=[pt])
                    def pv_fin(n=n, pt=pt, j=j, b0=b0, eb0=eb0, nb_=nb_, c0=c0, h=h):
                        act_i = [i for i in range(nb_) if i * P >= c0]

                        def pv(e, pt=pt, j=j, b0=b0, act_i=act_i, hp=h % 2):
                            ins = None
                            for i in act_i:
                                ins = e.matmul(bank(ACC[i])[:, 0:129], lhsT=pt.ap[:, i * P:(i + 1) * P], rhs=vxp.ap[:, j, hp, :],
                                               start=(j == 0), stop=(j == b0 + i))
                            return ins
                        pg.add("pe", pv, reads=[pt, (vxp.u, j // 2), (vxp.u, "ones")],
                               writes=[bank_u[ACC[i]] for i in act_i])
                        while deferred and deferred[0][0] <= n and not grp_open[0]:
                            deferred.pop(0)[1]()
                        if j >= b0:
                            i = j - b0
                            ab = ACC[i]
                            hbk = hbtok[(gstep[0] + n) % 4]
                            pg.add("dve", lambda e, ab=ab, i=i: e.tensor_scalar(
                                out=fsc.ap[:, i, 0:1], in0=bank(ab)[:, 128:129], scalar1=1e-30, scalar2=None, op0=ALU.max),
                                reads=[bank_u[ab]], writes=[(fsc.u, i)])
                            pg.add("dve", lambda e, i=i: e.reciprocal(out=fsc.ap[:, i, 1:2], in_=fsc.ap[:, i, 0:1]),
                                   reads=[(fsc.u, i)], writes=[(fsc.u, i)])
                            pg.add("dve", lambda e, ab=ab, i=i, hbk=hbk: e.tensor_scalar(
                                out=hbk.ap, in0=bank(ab)[:, 0:P], scalar1=fsc.ap[:, i, 1:2], scalar2=None, op0=ALU.mult),
                                reads=[bank_u[ab], (fsc.u, i)], writes=[hbk])

                            def tr(hbk=hbk, h=h, col=(eb0 + i) * P, key=(h, eb0 + i)):
                                tbk = 6
                                transpose8(hbk.ap, hbk.u, tbk, hbT.ap[:, h, col:col + P], (hbT.u, key), n=1, evac="dve")
                            deferred.append((n + 2, tr))
                    lagged.append(pv_fin)
                for f_ in lagged:
                    f_()
                for f_ in pend:
                    f_()
                for (_, f_) in deferred:
                    f_()
                gstep[0] += nst
            locals_ref[0] = locals()
            add_dumps("C")

        if nph >= 4:
            cw = Carver(132, 190)
            mergedT = cw("mergedT", [KC, TEXT], BF16)
            Wd = [cw("Wd%d" % i, [4, KC, P], BF16) for i in range(2)]
            cw = Carver(34, 64)
            sga = cw("sga", [512], F32)
            sgb = cw("sgb", [512], F32)
            m1 = cw("m1", [512], F32)
            m2 = cw("m2", [512], F32)
            stg = cw("stg", [5, P], F32)
            for i, (src, n) in enumerate(((bg_d, 16), (convw_d[0:44, :], 44), (convw_d[44:88, :], 44),
                                          (convw_d[88:132, :], 44), (convb_d, 44))):
                pg.add("sp", lambda e, i=i, src=src, n=n: e.dma_start(out=stg.ap[0:n, i, :], in_=src),
                       writes=[(stg.u, i)], dma="c_stg")
            for i, (dst, n) in enumerate(((bg_pp.ap, 16), (cw_pp.ap[:, 0, :], 44), (cw_pp.ap[:, 1, :], 44),
                                          (cw_pp.ap[:, 2, :], 44), (cb_pp.ap, 44))):
                pg.add("pe", lambda e, i=i, n=n: e.transpose(bank(7)[:, i * 64:i * 64 + n], stg.ap[0:n, i, :],
                                                             ident_f.ap[0:n, 0:n]),
                       reads=[(stg.u, i), ident_f], writes=[(bank_u[7], i)])
                pg.add("dve", lambda e, i=i, n=n, dst=dst: e.tensor_copy(out=dst, in_=bank(7)[:, i * 64:i * 64 + n]),
                       reads=[bank_u[7]], writes=[bg_pp if i == 0 else (cb_pp if i == 4 else (cw_pp.u, i))])

            dstep = 0

            wstg = cw("wstg", [4, KC, P], F32)

            def wd_load(cc_):
                if cc_ >= KC:
                    return
                wd_ = Wd[cc_ % 2]
                for i_, (wdr, off) in enumerate(((w_a_d, cc_ * P), (w_b_d, cc_ * P), (w_in_d, OFF_GA + cc_ * P),
                                                 (w_in_d, OFF_GB + cc_ * P))):
                    pg.add("sp", lambda e, i_=i_, wdr=wdr, off=off: e.dma_start(out=wstg.ap[:, i_, :, :], in_=wsrc(wdr, off, P)),
                           writes=[(wstg.u, i_)], dma="wstg%d" % i_)
                    eng_ = "pool" if i_ % 2 == 0 else "dve"
                    pg.add(eng_, lambda e, i_=i_: e.tensor_copy(out=wd_.ap[:, i_, :, :], in_=wstg.ap[:, i_, :, :]),
                           reads=[(wstg.u, i_)], writes=[(wd_.u, i_)])
            wd_load(0)
            for cc in range(KC):
                wd = Wd[cc % 2]
                wd_load(cc + 1)
                for (eb0, nb_) in EXT_TILES:
                    w_ = nb_ * P
                    t0 = eb0 * P
                    bs = [0, 1, 2, 3] if dstep % 2 == 0 else [4, 5, 6, 7]
                    dstep += 1
                    srcs = (haT, hbT, xnT_own, xnT_own)
                    for i in range(4):
                        src = srcs[i]
                        if src is xnT_own:
                            rd = [(src.u, NPRE + eb0 + j) for j in range(nb_)]
                        elif src is haT:
                            rd = [(src.u, eb0 + j) for j in range(nb_)]
                        else:
                            rd = [src]
                        mm_group(bank(bs[i])[:, 0:w_],
                                 [(wd.ap[:, i, kc, :], src.ap[:, kc, t0:t0 + w_]) for kc in range(KC)],
                                 bank_u[bs[i]], [(wd.u, i)] + rd)
                    pg.add("act", lambda e, bs=bs, w_=w_, cc=cc: e.activation(
                        out=sga.ap[:, 0:w_], in_=bank(bs[2])[:, 0:w_], func=AF.Sigmoid, bias=bg_pp.ap[:, cc:cc + 1]),
                        reads=[bank_u[bs[2]], bg_pp], writes=[sga])
                    pg.add("act", lambda e, bs=bs, w_=w_, cc=cc: e.activation(
                        out=sgb.ap[:, 0:w_], in_=bank(bs[3])[:, 0:w_], func=AF.Sigmoid, bias=bg_pp.ap[:, 8 + cc:9 + cc]),
                        reads=[bank_u[bs[3]], bg_pp], writes=[sgb])
                    pg.add("dve", lambda e, bs=bs, w_=w_: e.tensor_tensor(out=m1.ap[:, 0:w_], in0=bank(bs[0])[:, 0:w_],
                                                                          in1=sga.ap[:, 0:w_], op=ALU.mult),
                           reads=[bank_u[bs[0]], sga], writes=[m1])
                    pg.add("dve", lambda e, bs=bs, w_=w_: e.tensor_tensor(out=m2.ap[:, 0:w_], in0=bank(bs[1])[:, 0:w_],
                                                                          in1=sgb.ap[:, 0:w_], op=ALU.mult),
                           reads=[bank_u[bs[1]], sgb], writes=[m2])
                    pg.add("pool", lambda e, w_=w_, cc=cc, t0=t0: e.tensor_tensor(
                        out=mergedT.ap[:, cc, t0:t0 + w_], in0=m1.ap[:, 0:w_], in1=m2.ap[:, 0:w_], op=ALU.add),
                        reads=[m1, m2], writes=[(mergedT.u, (cc, eb0))])
            locals_ref[0] = locals()
            add_dumps("D1")

        if nph >= 5:
            x1 = Carver(0, 68)("x1", [NEXT, D], F32)
            cw = Carver(68, 132)
            wpost_bc = cw("wpost_bc", [D], F32)
            tmpd = [cw("tmpd%d" % i, [D], F32) for i in range(2)]
            xr = [cw("xr%d" % i, [2, D], F32) for i in range(3)]
            sqd = cw("sqd", [D], BF16)
            cw = Carver(166, 190)
            Wout = cw("Wout", [KC, D], BF16)
            bcast_load(wpost_bc, "norm_mix_post")
            for i in range(2):
                wload(Wout.ap[:, :, i * 512:(i + 1) * 512], (Wout.u, i), wsrc(w_out_d, i * 512, 512), "wout%d" % i)
            for eb in range(NEXT):
                bs = [0, 1] if eb % 2 == 0 else [2, 3]
                for hh in range(2):
                    mm_group(bank(bs[hh]),
                             [(mergedT.ap[:, kc, eb * P:(eb + 1) * P], Wout.ap[:, kc, hh * 512:(hh + 1) * 512])
                              for kc in range(KC)],
                             bank_u[bs[hh]], [mergedT, (Wout.u, hh)])
                    c_ = 32 + 2 * eb + hh
                    pg.add("act", lambda e, b_=bs[hh], c_=c_, hh=hh, tm=tmpd[eb % 2]: e.activation(
                        out=tm.ap[:, hh * 512:(hh + 1) * 512], in_=bank(b_), func=AF.Square, accum_out=ssq.ap[:, c_:c_ + 1]),
                        reads=[bank_u[bs[hh]]], writes=[(ssq.u, c_), (tmpd[eb % 2].u, hh)])
                cs = 66 + eb
                pg.add("dve", lambda e, eb=eb, cs=cs: e.tensor_tensor(
                    out=ssq.ap[:, cs:cs + 1], in0=ssq.ap[:, 32 + 2 * eb:33 + 2 * eb], in1=ssq.ap[:, 33 + 2 * eb:34 + 2 * eb],
                    op=ALU.add),
                    reads=[(ssq.u, 32 + 2 * eb), (ssq.u, 33 + 2 * eb)], writes=[(ssq.u, cs)])
                rms_rstd(cs, 1.0 / D)
                grp = (eb + 1) // 2
                xrg, tm = xr[grp % 3], tmpd[eb % 2]
                kk = (eb + 1) % 2
                if eb == 0:
                    pg.add("sp", lambda e, xrg=xrg: e.dma_start(out=xrg.ap[:, 1, :], in_=x_d[NPRE * P:(NPRE + 1) * P, :]),
                           writes=[xrg], dma="xr%d" % (grp % 3))
                elif kk == 0:
                    pg.add("sp", lambda e, xrg=xrg, eb=eb: e.dma_start(
                        out=xrg.ap, in_=x_d[(NPRE + eb) * P:(NPRE + eb + 2) * P, :].rearrange("(k p) d -> p k d", p=P)),
                        writes=[xrg], dma="xr%d" % (grp % 3))
                xrb_ap = xrg.ap[:, kk, :]
                for hh in range(2):
                    pg.add("dve", lambda e, b_=bs[hh], hh=hh, tm=tm, cs=cs: e.scalar_tensor_tensor(
                        out=tm.ap[:, hh * 512:(hh + 1) * 512], in0=bank(b_), scalar=rstd.ap[:, cs:cs + 1],
                        in1=wpost_bc.ap[:, hh * 512:(hh + 1) * 512], op0=ALU.mult, op1=ALU.mult),
                        reads=[bank_u[bs[hh]], (rstd.u, cs), wpost_bc], writes=[(tm.u, hh)])
                pg.add("pool", lambda e, tm=tm, xrb_ap=xrb_ap, eb=eb: e.tensor_tensor(
                    out=x1.ap[:, eb, :], in0=tm.ap, in1=xrb_ap, op=ALU.add),
                    reads=[tm, xrg], writes=[(x1.u, eb)])
                pg.add("act", lambda e, eb=eb: e.activation(out=sqd.ap, in_=x1.ap[:, eb, :], func=AF.Square,
                                                            accum_out=ssq.ap[:, eb:eb + 1]),
                       reads=[(x1.u, eb)], writes=[(ssq.u, eb), sqd])
                rms_rstd(eb, 1.0 / D)
            locals_ref[0] = locals()
            add_dumps("D2")

        if nph >= 6:
            cw = Carver(68, ARENA_KIB)
            Wdn = cw("Wdn", [NFC, D], BF16)
            actb = cw("actb", [NFC, 512], BF16)
            xn2T = cw("xn2T", [KC, 512], BF16)
            Wup = [cw("Wup%d" % i, [KC, 2, 256], BF16) for i in range(2)]
            yag = [cw("yag%d" % i, [2, 512], F32) for i in range(3)]
            glb = [cw("glb%d" % i, [512], F32) for i in range(2)]
            wfpre_bc = cw("wfpre_bc", [D], F32)
            wfpost_bc = cw("wfpost_bc", [D], F32)
            xs2 = [cw("xs2%d" % i, [D], BF16) for i in range(2)]
            tmpe = [cw("tmpe%d" % i, [D], F32) for i in range(1)]
            bcast_load(wfpre_bc, "norm_ffn_pre")
            bcast_load(wfpost_bc, "norm_ffn_post")
            wdn_src = w_dn_d.rearrange("(fc p) n -> p fc n", p=P)
            for i, (f0, f1) in enumerate(((0, 6), (6, 12), (12, 18), (18, 22))):
                wload(Wdn.ap[:, f0:f1, :], (Wdn.u, i), wdn_src[:, f0:f1, :], "wdn%d" % i)
            Wdn_deps = [(Wdn.u, i) for i in range(4)]
            USETS = [(0, 1), (2, 3), (4, 5)]
            OSETS = [(6, 7), (0, 1)]
            TBKS = [6, 7]
            sqj = cw("sqj", [D], BF16)
            hA = SB("hA", [44, 2], F32)
            hB = SB("hB", [44], F32)
            GROUPS = [(ti_, fg_) for ti_ in range(len(EXT_TILES)) for fg_ in range(NFC // 2)]

            def load_group(k):
                if k >= len(GROUPS):
                    return
                fg_ = GROUPS[k][1]
                wu_ = Wup[k % 2]
                pg.add("sp", lambda e, wu_=wu_, fg_=fg_: e.dma_start(
                    out=wu_.ap.rearrange("p a b c -> p (a b c)"), in_=wup_scr[:, fg_, :]),
                    reads=[(scr_u, fg_)], writes=[wu_], dma="wup%d" % (k % 2))
            load_group(0)
            gn = 0

            def E1(ti_):
                if ti_ >= len(EXT_TILES):
                    return
                eb0_, nbt_ = EXT_TILES[ti_]
                for bl in range(nbt_):
                    eb = eb0_ + bl
                    xsb = xs2[eb % 2]
                    pg.add("dve", lambda e, eb=eb, xsb=xsb: e.scalar_tensor_tensor(
                        out=xsb.ap, in0=x1.ap[:, eb, :], scalar=rstd.ap[:, eb:eb + 1], in1=wfpre_bc.ap,
                        op0=ALU.mult, op1=ALU.mult),
                        reads=[(x1.u, eb), (rstd.u, eb), wfpre_bc], writes=[xsb])
                    transpose8(xsb.ap, xsb.u, TBKS[eb % 2], xn2T.ap[:, :, bl * P:(bl + 1) * P], (xn2T.u, bl), evac="act")
            E1(0)
            ocnt = 0
            for ti, (eb0, nb_) in enumerate(EXT_TILES):
                w_ = nb_ * P
                xn_deps = [(xn2T.u, bl) for bl in range(nb_)]
                first = ti == 0
                last = ti + 1 == len(EXT_TILES)
                if not first:
                    hup = halo_u.ap[:, (ti - 1) % 2, :, :]
                    pg.add("dve", lambda e, hup=hup: e.tensor_tensor(
                        out=hA.ap, in0=hup, in1=cw_pp.ap[:, 0, :].unsqueeze(2).to_broadcast([P, 44, 2]), op=ALU.mult),
                        reads=[halo_u, cw_pp], writes=[hA])
                    pg.add("dve", lambda e, hup=hup: e.tensor_tensor(
                        out=hB.ap, in0=hup[:, :, 1], in1=cw_pp.ap[:, 1, :], op=ALU.mult),
                        reads=[halo_u, cw_pp], writes=[hB])
                    pg.add("dve", lambda e: e.tensor_tensor(out=hA.ap[:, :, 0], in0=hA.ap[:, :, 0], in1=hB.ap, op=ALU.add),
                           reads=[hA, hB], writes=[hA])

                def S0(fi, gi):
                    fg, fl = fi // 2, fi % 2
                    k = ti * (NFC // 2) + fg
                    wu = Wup[k % 2]
                    if fl == 0:
                        load_group(k + 1)
                    bs = USETS[gi % 3]
                    c_lo = w_ - 2 if first else 0
                    for ag in range(2):
                        mm_group(bank(bs[ag])[:, c_lo:w_],
                                 [(wu.ap[:, kc, ag, fl * P:(fl + 1) * P], xn2T.ap[:, kc, c_lo:w_]) for kc in range(KC)],
                                 bank_u[bs[ag]], [wu] + xn_deps)

                def S1(fi, gi):
                    bs = USETS[gi % 3]
                    yb = yag[gi % 3]
                    for ag in range(2):
                        cidx = ag * NFC + fi
                        ups = bank(bs[ag])
                        if not last:
                            pg.add("act", lambda e, ups=ups, cidx=cidx, ti=ti, w_=w_: e.activation(
                                out=halo_u.ap[:, ti % 2, cidx, :], in_=ups[:, w_ - 2:w_], func=AF.Copy),
                                reads=[bank_u[bs[ag]]], writes=[(halo_u.u, (ti % 2, cidx))])
                        if first:
                            continue
                        pg.add("act", lambda e, ups=ups, yb=yb, cidx=cidx, ag=ag, w_=w_: e.activation(
                            out=yb.ap[:, ag, 0:w_], in_=ups[:, 0:w_], func=AF.Identity, scale=cw_pp.ap[:, 2, cidx:cidx + 1],
                            bias=cb_pp.ap[:, cidx:cidx + 1]),
                            reads=[bank_u[bs[ag]], cw_pp, cb_pp], writes=[(yb.u, ag)])
                    if not first:
                        hv = hA.ap.rearrange("p (a f) c -> p a f c", a=2)[:, :, fi, :]
                        pg.add("pool", lambda e, yb=yb, hv=hv: e.tensor_tensor(
                            out=yb.ap[:, :, 0:2], in0=yb.ap[:, :, 0:2], in1=hv, op=ALU.add),
                            reads=[hA, (yb.u, 0), (yb.u, 1)], writes=[(yb.u, 0), (yb.u, 1)])

                def S2(fi, gi):
                    if first:
                        return
                    bs = USETS[gi % 3]
                    yb = yag[gi % 3]
                    for ag in range(2):
                        cidx = ag * NFC + fi
                        ups = bank(bs[ag])
                        for (sh, jw) in ((1, 1), (2, 0)):
                            pg.add("dve", lambda e, ups=ups, yb=yb, cidx=cidx, sh=sh, jw=jw, ag=ag, w_=w_: e.scalar_tensor_tensor(
                                out=yb.ap[:, ag, sh:w_], in0=ups[:, 0:w_ - sh], scalar=cw_pp.ap[:, jw, cidx:cidx + 1],
                                in1=yb.ap[:, ag, sh:w_], op0=ALU.mult, op1=ALU.add),
                                reads=[bank_u[bs[ag]], cw_pp, (yb.u, ag)], writes=[(yb.u, ag)])

                def S3(fi, gi):
                    if first:
                        return
                    yb = yag[gi % 3]
                    gl = glb[gi % 2]
                    pg.add("act", lambda e, yb=yb, gl=gl, w_=w_: e.activation(out=gl.ap[:, 0:w_], in_=yb.ap[:, 1, 0:w_],
                                                                       func=AF.Gelu_apprx_tanh),
                           reads=[(yb.u, 1)], writes=[gl])
                    pg.add("pool", lambda e, yb=yb, gl=gl, fi=fi, w_=w_: e.tensor_tensor(
                        out=actb.ap[:, fi, 0:w_], in0=gl.ap[:, 0:w_], in1=yb.ap[:, 0, 0:w_], op=ALU.mult),
                        reads=[gl, (yb.u, 0)], writes=[(actb.u, fi)])

                for w in range(NFC + 2):
                    if w < NFC:
                        S0(w, gn + w)
                    if w == NFC - 1:
                        E1(ti + 1)
                    if 0 <= w - 1 < NFC:
                        S1(w - 1, gn + w - 1)
                    if 0 <= w - 2 < NFC:
                        S2(w - 2, gn + w - 2)
                        S3(w - 2, gn + w - 2)
                gn += NFC
                if first:
                    continue
                osets = [USETS[gn % 3], (6, 7)]
                SPLIT = 16

                def e3_mm(bl, hh, bs, f0, f1):
                    def f(e):
                        ins = None
                        for fc in range(f0, f1):
                            ins = e.matmul(bank(bs[hh]), lhsT=actb.ap[:, fc, bl * P:(bl + 1) * P],
                                           rhs=Wdn.ap[:, fc, hh * 512:(hh + 1) * 512], start=(fc == 0), stop=(fc == NFC - 1))
                        return ins
                    pg.add("pe", f, reads=[(actb.u, fc) for fc in range(f0, f1)] + Wdn_deps, writes=[bank_u[bs[hh]]])
                for bl in range(nb_):
                    eb = eb0 + bl
                    bs = osets[bl % 2]
                    tme = tmpe[0]
                    if bl == 0:
                        for hh in range(2):
                            e3_mm(bl, hh, bs, 0, SPLIT)
                        for hh in range(2):
                            e3_mm(bl, hh, bs, SPLIT, NFC)
                    else:
                        for hh in range(2):
                            e3_mm(bl, hh, bs, 0, NFC)
                    for hh in range(2):
                        c_ = 32 + 2 * eb + hh
                        pg.add("act", lambda e, b_=bs[hh], c_=c_, hh=hh, tme=tme: e.activation(
                            out=tme.ap[:, hh * 512:(hh + 1) * 512], in_=bank(b_), func=AF.Square,
                            accum_out=ssq.ap[:, c_:c_ + 1]),
                            reads=[bank_u[bs[hh]]], writes=[(ssq.u, c_), (tme.u, hh)])
                    cs = 66 + eb
                    pg.add("dve", lambda e, eb=eb, cs=cs: e.tensor_tensor(
                        out=ssq.ap[:, cs:cs + 1], in0=ssq.ap[:, 32 + 2 * eb:33 + 2 * eb],
                        in1=ssq.ap[:, 33 + 2 * eb:34 + 2 * eb], op=ALU.add),
                        reads=[(ssq.u, 32 + 2 * eb), (ssq.u, 33 + 2 * eb)], writes=[(ssq.u, cs)])
                    rms_rstd(cs, 1.0 / D)
                    for hh in range(2):
                        pg.add("dve", lambda e, b_=bs[hh], hh=hh, cs=cs, tme=tme: e.scalar_tensor_tensor(
                            out=tme.ap[:, hh * 512:(hh + 1) * 512], in0=bank(b_), scalar=rstd.ap[:, cs:cs + 1],
                            in1=wfpost_bc.ap[:, hh * 512:(hh + 1) * 512], op0=ALU.mult, op1=ALU.mult),
                            reads=[bank_u[bs[hh]], (rstd.u, cs), wfpost_bc], writes=[(tme.u, hh)])
                    pg.add("pool", lambda e, eb=eb, tme=tme: e.tensor_tensor(out=x1.ap[:, eb, :], in0=x1.ap[:, eb, :],
                                                                             in1=tme.ap, op=ALU.add),
                           reads=[tme, (x1.u, eb)], writes=[(x1.u, eb)])
                    pg.add("sp", lambda e, eb=eb: e.dma_start(out=out_d[(eb - 1) * P:eb * P, :], in_=x1.ap[:, eb, :]),
                           reads=[(x1.u, eb)], writes=[(out_u, "o%d" % eb)], dma="out%d" % (eb % 4))
            locals_ref[0] = locals()
            add_dumps("E")

        pg.add("sp", lambda e: e.nop(), reads=[out_u], name="final")
        nsem = pg.emit(nc, stack)
    return nc, dbg_d, nsem, pg.nops


def make_in_maps(inputs, cores=range(8)):
    x = np.asarray(inputs["x"], dtype=np.float32)
    g = lambda k: np.ascontiguousarray(np.asarray(inputs[k], dtype=np.float32)[0])
    shared = {
        "w_in": g("w_in"), "w_branch_a": g("w_branch_a"), "w_branch_b": g("w_branch_b"),
        "w_out": g("w_out"), "w_up": g("w_up"), "w_down": g("w_down"),
        "b_gate_ab": np.ascontiguousarray(np.concatenate([g("b_gate_a").reshape(8, P), g("b_gate_b").reshape(8, P)], 0)),
        "conv_w": np.ascontiguousarray(g("conv_w").reshape(3 * 44, P)),
        "conv_b": np.ascontiguousarray(g("conv_b").reshape(44, P)),
    }
    for nm in ("norm_mix_pre", "b_ml_i", "b_ml_f", "ml_head_norm", "b_fox_f", "norm_mix_post",
               "norm_ffn_pre", "norm_ffn_post"):
        shared[nm] = np.ascontiguousarray(g(nm).reshape(1, -1))
    maps = []
    for core in cores:
        b, hf = core // 2, core % 2
        km = np.zeros((P, NB), np.float32)
        if hf == 1:
            xa = np.ascontiguousarray(x[b])
        else:
            xa = np.concatenate([np.zeros((2048, D), np.float32), x[b, :2048]], 0)
            km[:, :16] = MASKNEG
        m = dict(shared)
        m["x"] = np.ascontiguousarray(xa)
        m["kmask"] = km
        maps.append(m)
    return maps


_CACHE = {}


def kernel(**inputs):
    if "nc" not in _CACHE:
        _CACHE["nc"] = build_program()[0]
    nc = _CACHE["nc"]
    maps = make_in_maps(inputs)
    res = run_bass_kernel_spmd(nc, maps, core_ids=list(range(8)))
    out = np.zeros((4, 4096, D), np.float32)
    for core in range(8):
        b, hf = core // 2, core % 2
        out[b, hf * 2048:(hf + 1) * 2048] = res.results[core]["out"]
    return out
```

```python
from contextlib import ExitStack

import numpy as np
import concourse.bass as bass
import concourse.mybir as mybir
from concourse.bass_utils import run_bass_kernel_spmd

F32 = mybir.dt.float32
BF16 = mybir.dt.bfloat16
AF = mybir.ActivationFunctionType
ALU = mybir.AluOpType

P = 128
D = 1024
KC = 8
NB = 32
NPRE = 15
NEXT = 17
TPRE = NPRE * P
TEXT = NEXT * P
D_IN = 8208
D_FF = 2816
NFC = 22
EPS = 1e-6
OFF_QM, OFF_KM, OFF_VM, OFF_IM, OFF_FM, OFF_OM = 0, 512, 1024, 2048, 2052, 2056
OFF_QF, OFF_KF, OFF_VF, OFF_FF, OFF_GA, OFF_GB = 3080, 4104, 5128, 6152, 6160, 7184
MASKNEG = -30000.0
EXT_TILES = [(0, 1), (1, 4), (5, 4), (9, 4), (13, 4)]
ALL = "__all__"
ARENA_KIB = 192


class Unit:
    __slots__ = ("name", "lo", "hi", "w", "r", "acc", "accw", "aliases")

    def __init__(self, name, lo=None, hi=None):
        self.name, self.lo, self.hi = name, lo, hi
        self.w, self.r, self.acc, self.accw, self.aliases = {}, {}, {}, {}, []


class Op:
    __slots__ = ("eng", "fn", "deps", "semkey", "order", "signals", "value", "dma", "name")


class Prog:
    ENGS = ("pe", "act", "dve", "pool", "sp")

    def __init__(self):
        self.eng_ops = {e: [] for e in self.ENGS}
        self.units = []
        self.dma_cnt = {}
        self.nops = 0

    def unit(self, name, lo=None, hi=None):
        u = Unit(name, lo, hi)
        if lo is not None:
            for o in self.units:
                if o.lo is not None and o.lo < hi and lo < o.hi:
                    o.aliases.append(u)
                    u.aliases.append(o)
        self.units.append(u)
        return u

    @staticmethod
    def _norm(lst):
        out = []
        for x in lst:
            if isinstance(x, Unit):
                out.append((x, ALL))
            elif isinstance(x, tuple):
                out.append(x)
            else:
                out.append((x.u, ALL))
        return out

    def add(self, eng, fn, reads=(), writes=(), dma=None, name=""):
        op = Op()
        op.eng, op.fn, op.name, op.dma = eng, fn, name, dma
        op.signals, op.value = False, None
        if dma is None:
            op.semkey = ("E", eng)
            op.order = len(self.eng_ops[eng])
        else:
            op.semkey = ("D", dma)
            self.dma_cnt[dma] = self.dma_cnt.get(dma, 0) + 1
            op.order = self.dma_cnt[dma]
        deps = {}

        def need(o):
            if o is None:
                return
            c = deps.get(o.semkey)
            if c is None or o.order > c.order:
                deps[o.semkey] = o

        reads, writes = self._norm(reads), self._norm(writes)
        for (u, sk) in reads:
            if sk is ALL:
                for o in u.w.values():
                    need(o)
            else:
                need(u.w.get(sk))
                need(u.w.get(ALL))
            for a in u.aliases:
                for o in a.accw.values():
                    need(o)
        for (u, sk) in writes:
            if sk is ALL:
                for o in u.w.values():
                    need(o)
                for d in u.r.values():
                    for o in d.values():
                        need(o)
            else:
                need(u.w.get(sk))
                need(u.w.get(ALL))
                for o in u.r.get(sk, {}).values():
                    need(o)
                for o in u.r.get(ALL, {}).values():
                    need(o)
            for a in u.aliases:
                for o in a.acc.values():
                    need(o)
        if eng == "pe" and dma is None:
            deps.pop(("E", "pe"), None)
        op.deps = []
        for sk_, o in deps.items():
            if o is op:
                continue
            o.signals = True
            val = None
            if sk_[0] == "D":
                val = 16 * (self.dma_cnt[sk_[1]] - (1 if sk_ == op.semkey else 0))
            op.deps.append((sk_, o, val))
        for (u, sk) in reads:
            u.r.setdefault(sk, {})[op.semkey] = op
            u.acc[op.semkey] = op
        for (u, sk) in writes:
            if sk is ALL:
                u.w = {ALL: op}
                u.r = {}
            else:
                u.w[sk] = op
                u.r[sk] = {}
            u.acc[op.semkey] = op
            u.accw[op.semkey] = op
        self.eng_ops[eng].append(op)
        self.nops += 1
        return op

    def emit(self, nc, stack):
        sems = {}

        def sem(key):
            if key not in sems:
                sems[key] = stack.enter_context(nc.semaphore("s_%s_%s" % (key[0], key[1])))
            return sems[key]

        for e in self.ENGS:
            n = 0
            for op in self.eng_ops[e]:
                if op.dma is None and op.signals:
                    n += 1
                    op.value = n
        engobj = {"pe": "tensor", "act": "scalar", "dve": "vector", "pool": "gpsimd", "sp": "sync"}
        with nc.Block() as block:
            for e in self.ENGS:
                ops = self.eng_ops[e]
                if not ops:
                    continue

                def body(eng, ops=ops):
                    waited = {}
                    for op in ops:
                        for (sk_, o, val) in op.deps:
                            v = val if sk_[0] == "D" else o.value
                            if waited.get(sk_, 0) >= v:
                                continue
                            eng.wait_ge(sem(sk_), v)
                            waited[sk_] = v
                        ins = op.fn(eng)
                        if op.dma is not None:
                            ins.then_inc(sem(op.semkey), 16)
                        elif op.signals:
                            ins.then_inc(sem(op.semkey), 1)

                getattr(block, engobj[e])(body)
        return len(sems)


def build_program(stop_after="E", dumps=()):
    nc = bass.Bass("TRN2", target_bir_lowering=False)
    pg = Prog()
    PH = ["A", "B0", "B", "C", "D1", "D2", "E"]
    nph = PH.index(stop_after)

    def dram(name, shape, kind="ExternalInput"):
        return nc.dram_tensor(name, list(shape), F32, kind=kind).ap()

    x_d = dram("x", [NB * P, D])
    kmask_d = dram("kmask", [P, NB])
    w_in_d = dram("w_in", [D, D_IN])
    w_a_d = dram("w_branch_a", [D, D])
    w_b_d = dram("w_branch_b", [D, D])
    w_out_d = dram("w_out", [D, D])
    w_up_d = dram("w_up", [D, 2 * D_FF])
    w_dn_d = dram("w_down", [D_FF, D])
    vec = {}
    for nm, n in (("norm_mix_pre", D), ("b_ml_i", 4), ("b_ml_f", 4), ("ml_head_norm", D),
                  ("b_fox_f", 8), ("norm_mix_post", D), ("norm_ffn_pre", D), ("norm_ffn_post", D)):
        vec[nm] = dram(nm, [1, n])
    bg_d = dram("b_gate_ab", [16, P])
    convw_d = dram("conv_w", [3 * 44, P])
    convb_d = dram("conv_b", [44, P])
    out_d = dram("out", [16 * P, D], kind="ExternalOutput")
    NG = NFC // 2
    wup_scr = nc.dram_tensor("wup_scr", [P, NG, KC * 2 * 256], BF16, kind="Internal").ap()
    dbg_d = {}

    stack = ExitStack()
    with stack:
        arena = stack.enter_context(nc.sbuf_tensor("arena", [P, ARENA_KIB * 512], BF16))
        psum = stack.enter_context(nc.psum_tensor("psum", [P, 4096], F32))

        class Buf:
            def __init__(self, name, lo, free, dt):
                n = int(np.prod(free))
                esz = 4 if dt == F32 else 2
                lo = (lo + 63) // 64 * 64
                self.lo, self.hi = lo, lo + n * esz
                assert self.hi <= ARENA_KIB * 1024, (name, self.hi)
                a = arena[:, lo // 2:self.hi // 2]
                if dt == F32:
                    a = a.bitcast(F32)
                if len(free) == 2:
                    a = a.rearrange("p (a b) -> p a b", a=free[0])
                elif len(free) == 3:
                    a = a.rearrange("p (a b c) -> p a b c", a=free[0], b=free[1])
                self.ap = a
                self.u = pg.unit(name, self.lo, self.hi)

        class Carver:
            def __init__(self, lo_kib, hi_kib):
                self.pos, self.hi = int(lo_kib * 1024), int(hi_kib * 1024)

            def __call__(self, name, free, dt):
                b = Buf(name, self.pos, free, dt)
                self.pos = b.hi
                assert self.pos <= self.hi, (name, self.pos, self.hi)
                return b

        class SB:
            def __init__(self, name, free, dt):
                self.t = stack.enter_context(nc.sbuf_tensor("sb_" + name, [P] + list(free), dt))
                self.ap = self.t[:]
                self.u = pg.unit(name)

        def bank(i, n=1):
            return psum[:, i * 512:(i + n) * 512]

        bank_u = [pg.unit("bank%d" % i) for i in range(8)]
        out_u = pg.unit("out_dram")

        def add_dumps(phase):
            for (ph, name, apf, shape) in dumps:
                if ph != phase:
                    continue
                ap, u = apf(locals_ref[0])
                dd = nc.dram_tensor(name, list(shape), ap.dtype, kind="ExternalOutput").ap()
                dbg_d[name] = dd
                pg.add("sp", lambda e, dd=dd, ap=ap: e.dma_start(out=dd, in_=ap), reads=[u], writes=[(out_u, name)],
                       dma="dbg_" + name)

        locals_ref = [None]

        ident_f = SB("ident_f", [P], F32)
        ident_b = SB("ident_b", [P], BF16)
        U_f = SB("U_f", [P], F32)
        M_b = SB("M_b", [P], BF16)
        ones_f = SB("ones_f", [P], F32)
        neghalf = SB("neghalf", [1], F32)
        ssq = SB("ssq", [96], F32)
        ms = SB("ms", [96], F32)
        rstd = SB("rstd", [96], F32)
        LF = SB("LF", [NB, 12], F32)
        LFh = SB("LFh", [NB, 12], BF16)
        LFl = SB("LFl", [NB, 12], BF16)
        es_t = SB("es_t", [NB, 4], F32)
        wk_t = SB("wk_t", [NB, 4], F32)
        a_t = SB("a_t", [NB, 4], F32)
        bias_all = SB("bias_all", [5, NB, 8], F32)
        kmask = SB("kmask", [NB], F32)
        bg_pp = SB("bg_pp", [16], F32)
        cw_pp = SB("cw_pp", [3, 44], F32)
        cb_pp = SB("cb_pp", [44], F32)
        halo_u = SB("halo_u", [2, 44, 2], F32)

        pg.add("pool", lambda e: e.memset(ones_f.ap, 1.0), writes=[ones_f])
        pg.add("pool", lambda e: e.memset(U_f.ap, 1.0), writes=[U_f])
        pg.add("pool", lambda e: e.memset(ident_f.ap, 1.0), writes=[ident_f])
        pg.add("pool", lambda e: e.memset(neghalf.ap, -0.5), writes=[neghalf])
        pg.add("pool", lambda e: e.memset(halo_u.ap, 0.0), writes=[halo_u])
        pg.add("pool", lambda e: e.affine_select(out=U_f.ap, in_=U_f.ap, pattern=[[1, P]],
                                                 compare_op=ALU.is_ge, fill=0.0, base=0,
                                                 channel_multiplier=-1),
               reads=[U_f], writes=[U_f])
        pg.add("pool", lambda e: e.affine_select(out=ident_f.ap, in_=ident_f.ap, pattern=[[-1, P]],
                                                 compare_op=ALU.is_equal, fill=0.0, base=0,
                                                 channel_multiplier=1),
               reads=[ident_f], writes=[ident_f])
        pg.add("dve", lambda e: e.tensor_copy(out=ident_b.ap, in_=ident_f.ap), reads=[ident_f], writes=[ident_b])
        pg.add("dve", lambda e: e.tensor_copy(out=M_b.ap, in_=U_f.ap), reads=[U_f], writes=[M_b])
        pg.add("sp", lambda e: e.dma_start(out=kmask.ap, in_=kmask_d[:, :]), writes=[kmask], dma="c_kmask")

        xnT_own = Carver(0, 34)("xnT_own", [KC, TEXT], BF16)
        xnT_pre = Carver(34, 64)("xnT_pre", [KC, TPRE], BF16)
        haT = Carver(64, 98)("haT", [KC, TEXT], BF16)
        hbT = Carver(98, 132)("hbT", [KC, TEXT], BF16)

        def xnT(tok0, n):
            if tok0 < TPRE:
                assert tok0 + n <= TPRE
                return xnT_pre, xnT_pre.ap[:, :, tok0:tok0 + n]
            return xnT_own, xnT_own.ap[:, :, tok0 - TPRE:tok0 - TPRE + n]

        def bcast_load(buf, name):
            pg.add("sp", lambda e: e.dma_start(out=buf.ap, in_=vec[name].partition_broadcast(P)),
                   writes=[buf], dma="c_" + name)

        def wload(buf_ap, buf_dep, src_ap, slot):
            pg.add("pool", lambda e: e.dma_start(out=buf_ap, in_=src_ap), writes=[buf_dep], dma=slot)

        def wsrc(w_d, c0, n):
            return w_d[:, c0:c0 + n].rearrange("(kc p) n -> p kc n", p=P)

        def rms_rstd(col, inv_n):
            pg.add("dve", lambda e: e.tensor_scalar(out=ms.ap[:, col:col + 1], in0=ssq.ap[:, col:col + 1],
                                                    scalar1=inv_n, scalar2=EPS, op0=ALU.mult, op1=ALU.add),
                   reads=[(ssq.u, col)], writes=[(ms.u, col)])
            pg.add("pool", lambda e: e.tensor_tensor(out=rstd.ap[:, col:col + 1], in0=ms.ap[:, col:col + 1],
                                                     in1=neghalf.ap, op=ALU.pow),
                   reads=[(ms.u, col), neghalf], writes=[(rstd.u, col)])

        def transpose8(src_ap, src_dep, bnk, dst_ap, dst_dep, n=KC, evac="act"):
            pb = bank(bnk).bitcast(BF16)

            def f(e):
                ins = None
                for kc in range(n):
                    ins = e.transpose(pb[:, kc * P:(kc + 1) * P], src_ap[:, kc * P:(kc + 1) * P], ident_b.ap)
                return ins
            pg.add("pe", f, reads=[src_dep, ident_b], writes=[bank_u[bnk]])
            pv = pb[:, 0:n * P]
            if n > 1:
                pv = pv.rearrange("p (a b) -> p a b", a=n)
            if evac == "act":
                pg.add("act", lambda e: e.activation(out=dst_ap, in_=pv, func=AF.Copy),
                       reads=[bank_u[bnk]], writes=[dst_dep])
            else:
                pg.add("dve", lambda e: e.tensor_copy(out=dst_ap, in_=pv), reads=[bank_u[bnk]], writes=[dst_dep])

        def mm_group(out_ap, pairs, bnk_dep, reads):
            def f(e):
                ins = None
                n = len(pairs)
                for i, (l, r) in enumerate(pairs):
                    ins = e.matmul(out_ap, lhsT=l, rhs=r, start=(i == 0), stop=(i == n - 1))
                return ins
            pg.add("pe", f, reads=reads, writes=[bnk_dep])

        cw = Carver(132, ARENA_KIB)
        wpre_bc = cw("wpre_bc", [D], F32)
        xin = [cw("xin%d" % i, [2, D], F32) for i in range(5)]
        xs = [cw("xs%d" % i, [D], BF16) for i in range(2)]
        jk = [cw("jk%d" % i, [D], BF16) for i in range(2)]
        Wg = cw("Wg", [KC, 16], BF16)
        bias16 = cw("bias16", [16], F32)
        bcast_load(wpre_bc, "norm_mix_pre")
        GB = 7
        if nph >= 1:
            wload(Wg.ap[:, :, 0:8], Wg.u, wsrc(w_in_d, OFF_IM, 8), "wg")
            wload(Wg.ap[:, :, 8:16], Wg.u, wsrc(w_in_d, OFF_FF, 8), "wg")
            for (nm, c0, n) in (("b_ml_i", 0, 4), ("b_ml_f", 4, 4), ("b_fox_f", 8, 8)):
                pg.add("sp", lambda e, nm=nm, c0=c0, n=n: e.dma_start(
                    out=bias16.ap[:, c0:c0 + n], in_=vec[nm].partition_broadcast(P)),
                    writes=[bias16], dma="c_bias16")

        def A_stats(b):
            if b >= NB:
                return
            g_ = b // 2
            xg = xin[g_ % 5]
            if b % 2 == 0:
                pg.add("sp", lambda e: e.dma_start(
                    out=xg.ap, in_=x_d[g_ * 2 * P:(g_ + 1) * 2 * P, :].rearrange("(k p) d -> p k d", p=P)),
                    writes=[xg], dma="xin%d" % (g_ % 5))
            pg.add("act", lambda e: e.activation(out=jk[b % 2].ap, in_=xg.ap[:, b % 2, :], func=AF.Square,
                                                 accum_out=ssq.ap[:, b:b + 1]),
                   reads=[xg], writes=[(ssq.u, b), jk[b % 2]])
            rms_rstd(b, 1.0 / D)

        def A_scale_T(b):
            xg, xsb = xin[(b // 2) % 5], xs[b % 2]
            pg.add("dve", lambda e: e.scalar_tensor_tensor(
                out=xsb.ap, in0=xg.ap[:, b % 2, :], scalar=rstd.ap[:, b:b + 1], in1=wpre_bc.ap, op0=ALU.mult, op1=ALU.mult),
                reads=[xg, (rstd.u, b), wpre_bc], writes=[xsb])
            pb = bank(b % 2).bitcast(BF16)

            def f(e):
                ins = None
                for kc in range(KC):
                    ins = e.transpose(pb[:, kc * P:(kc + 1) * P], xsb.ap[:, kc * P:(kc + 1) * P], ident_b.ap)
                return ins
            pg.add("pe", f, reads=[xsb, ident_b], writes=[bank_u[b % 2]])

        def A_evac(b):
            if b < 0:
                return
            buf, dst = xnT(b * P, P)
            pv = bank(b % 2).bitcast(BF16).rearrange("p (a b) -> p a b", a=KC)
            if b % 2 == 0:
                pg.add("act", lambda e: e.activation(out=dst, in_=pv, func=AF.Copy),
                       reads=[bank_u[b % 2]], writes=[(buf.u, b)])
            else:
                pg.add("dve", lambda e: e.tensor_copy(out=dst, in_=pv), reads=[bank_u[b % 2]], writes=[(buf.u, b)])
            if nph >= 1:
                mm_group(bank(GB)[:, b * 16:(b + 1) * 16],
                         [(dst[:, kc, :], Wg.ap[:, kc, :]) for kc in range(KC)],
                         (bank_u[GB], b), [(buf.u, b), Wg])

        for b_ in range(6):
            A_stats(b_)
        for b in range(NB):
            A_stats(b + 6)
            A_scale_T(b)
            A_evac(b - 1)
        A_evac(NB - 1)
        locals_ref[0] = locals()
        add_dumps("A")

        if nph >= 1:
            cw = Carver(160, ARENA_KIB)
            gates = cw("gates", [NB, 16], F32)
            cap8 = cw("cap8", [NB, 8], F32)
            e1 = cw("e1", [NB, 12], F32)
            spl = cw("spl", [NB, 12], F32)
            loc = cw("loc", [NB, 12], F32)
            tot = cw("tot", [NB, 12], F32)
            dbias = cw("dbias", [NB, 4], F32)
            wkarg = cw("wkarg", [NB, 4], F32)
            incl = cw("incl", [NB, 8], F32)
            carry = cw("carry", [NB, 8], F32)
            Gt = cw("Gt", [NB, 8], F32)
            negGm = cw("negGm", [NB, 8], F32)
            gps = bank(GB).rearrange("p (a b) -> p a b", a=NB)
            pg.add("dve", lambda e: e.tensor_tensor(out=gates.ap, in0=gps,
                                                    in1=bias16.ap.unsqueeze(1).to_broadcast([P, NB, 16]),
                                                    op=ALU.add),
                   reads=[bank_u[GB], bias16], writes=[gates])
            pg.add("act", lambda e: e.activation(out=cap8.ap, in_=gates.ap[:, :, 0:8], func=AF.Tanh, scale=1.0 / 15.0),
                   reads=[gates], writes=[cap8])
            pg.add("dve", lambda e: e.tensor_scalar(out=gates.ap[:, :, 0:8], in0=cap8.ap, scalar1=15.0, scalar2=None,
                                                    op0=ALU.mult),
                   reads=[cap8], writes=[gates])
            pg.add("act", lambda e: e.activation(out=e1.ap, in_=gates.ap[:, :, 4:16], func=AF.Exp, scale=-1.0),
                   reads=[gates], writes=[e1])
            pg.add("act", lambda e: e.activation(out=spl.ap, in_=e1.ap, func=AF.Ln, bias=1.0, scale=1.0),
                   reads=[e1], writes=[spl])
            pg.add("dve", lambda e: e.tensor_scalar(out=LF.ap, in0=spl.ap, scalar1=-1.0, scalar2=None, op0=ALU.mult),
                   reads=[spl], writes=[LF])
            pg.add("dve", lambda e: e.tensor_copy(out=LFh.ap, in_=LF.ap), reads=[LF], writes=[LFh])
            pg.add("dve", lambda e: e.tensor_tensor(out=LFl.ap, in0=LF.ap, in1=LFh.ap, op=ALU.subtract),
                   reads=[LF, LFh], writes=[LFl])
            LF2 = LF.ap.rearrange("p a b -> p (a b)")
            mm_group(bank(GB)[:, 0:384], [(U_f.ap, LF2)], bank_u[GB], [U_f, LF])
            pg.add("act", lambda e: e.activation(out=loc.ap.rearrange("p a b -> p (a b)"), in_=bank(GB)[:, 0:384],
                                                 func=AF.Copy),
                   reads=[bank_u[GB]], writes=[loc])
            mm_group(bank(GB)[:, 0:384], [(ones_f.ap, LF2)], bank_u[GB], [ones_f, LF])
            pg.add("act", lambda e: e.activation(out=tot.ap.rearrange("p a b -> p (a b)"), in_=bank(GB)[:, 0:384],
                                                 func=AF.Copy),
                   reads=[bank_u[GB]], writes=[tot])
            pg.add("dve", lambda e: e.tensor_tensor(out=dbias.ap, in0=gates.ap[:, :, 0:4], in1=loc.ap[:, :, 0:4],
                                                    op=ALU.subtract),
                   reads=[gates, loc], writes=[dbias])
            pg.add("act", lambda e: e.activation(out=es_t.ap, in_=dbias.ap, func=AF.Exp), reads=[dbias], writes=[es_t])
            pg.add("dve", lambda e: e.tensor_tensor(out=wkarg.ap, in0=dbias.ap, in1=tot.ap[:, :, 0:4], op=ALU.add),
                   reads=[dbias, tot], writes=[wkarg])
            pg.add("act", lambda e: e.activation(out=wk_t.ap, in_=wkarg.ap, func=AF.Exp), reads=[wkarg], writes=[wk_t])
            pg.add("act", lambda e: e.activation(out=a_t.ap, in_=tot.ap[:, :, 0:4], func=AF.Exp), reads=[tot], writes=[a_t])
            for h in range(8):
                pg.add("dve", lambda e, h=h: e.tensor_tensor_scan(
                    out=incl.ap[:, :, h], data0=ones_f.ap[:, 0:NB], data1=tot.ap[:, :, 4 + h], initial=0.0,
                    op0=ALU.mult, op1=ALU.add),
                    reads=[tot, ones_f], writes=[(incl.u, h)])
            pg.add("dve", lambda e: e.tensor_tensor(out=carry.ap, in0=incl.ap, in1=tot.ap[:, :, 4:12], op=ALU.subtract),
                   reads=[incl, tot], writes=[carry])
            pg.add("dve", lambda e: e.tensor_tensor(out=Gt.ap, in0=loc.ap[:, :, 4:12], in1=carry.ap, op=ALU.add),
                   reads=[loc, carry], writes=[Gt])
            pg.add("dve", lambda e: e.scalar_tensor_tensor(
                out=negGm.ap, in0=Gt.ap, scalar=-1.0, in1=kmask.ap.unsqueeze(2).to_broadcast([P, NB, 8]),
                op0=ALU.mult, op1=ALU.add),
                reads=[Gt, kmask], writes=[negGm])
            for ti, (eb0, nb_) in enumerate(EXT_TILES):
                r = NPRE + eb0 + (2 if nb_ == 4 else 0)
                pg.add("dve", lambda e, ti=ti, r=r: e.tensor_tensor(
                    out=bias_all.ap[:, ti, :, :], in0=negGm.ap,
                    in1=carry.ap[:, r:r + 1, :].to_broadcast([P, NB, 8]), op=ALU.add),
                    reads=[negGm, carry], writes=[(bias_all.u, ti)])
            locals_ref[0] = locals()
            add_dumps("B0")

        if nph >= 2:
            cw = Carver(98, ARENA_KIB)
            Wm = cw("Wm", [KC, 3072], BF16)
            halfw = cw("halfw", [D], F32)
            qkT = cw("qkT", [2, 4, 512], BF16)
            ksc = [cw("ksc%d" % i, [4, P], BF16) for i in range(2)]
            vext = [cw("vext%d" % i, [4, 257], BF16) for i in range(2)]
            Gw = [cw("Gw%d" % i, [D], F32) for i in range(2)]
            EB = [cw("EB0", [4, P], F32)] * 2
            EBM = [cw("EBM0", [4, P], F32)] * 2
            ST = [cw("ST%d" % i, [4, P], BF16) for i in range(2)]
            qs = [cw("qs%d" % i, [4, P], BF16) for i in range(2)]
            hablk = [cw("hablk%d" % i, [D], BF16) for i in range(2)]
            CT = cw("CT", [4, 257], F32)
            CTb = cw("CTb", [4, 257], BF16)
            sc = SB("sc", [2, 4, 8], F32)
            for i, (c0, n) in enumerate(((0, 512), (512, 512), (1024, 512), (1536, 512))):
                wload(Wm.ap[:, :, c0:c0 + n], (Wm.u, i), wsrc(w_in_d, c0, n), "wm%d" % i)
            for i in range(2):
                wload(Wm.ap[:, :, 2048 + i * 512:2048 + (i + 1) * 512], (Wm.u, 4 + i),
                      wsrc(w_in_d, OFF_OM + i * 512, 512), "wm%d" % (4 + i))
            bcast_load(halfw, "ml_head_norm")
            pg.add("dve", lambda e: e.tensor_scalar(out=halfw.ap, in0=halfw.ap, scalar1=0.5, scalar2=None, op0=ALU.mult),
                   reads=[halfw], writes=[halfw])
            pg.add("pool", lambda e: e.memset(CT.ap, 0.0), writes=[CT])
            pg.add("pool", lambda e: e.memset(CTb.ap, 0.0), writes=[CTb])
            for i in range(2):
                pg.add("pool", lambda e, i=i: e.memset(vext[i].ap[:, :, 256:257], 1.0), writes=[(vext[i].u, "ones")])
            PB = [0, 1, 2]
            pbi = [0]

            def nextbank():
                b_ = PB[pbi[0] % 3]
                pbi[0] += 1
                return b_
            SB0, SB1, RB0, RB1, TB = 3, 4, 5, 6, 7
            tile_of = {}
            for (eb0, nb_) in EXT_TILES:
                for j in range(nb_):
                    tile_of[NPRE + eb0 + j] = (eb0, nb_)

            def P_parts(c):
                own = c >= NPRE
                buf, xv = xnT(c * P, P)
                xdep = (buf.u, c)
                ks, ve = ksc[c % 2], vext[c % 2]

                def part_qk():
                    eb0, nb_ = tile_of[c]
                    if NPRE + eb0 != c:
                        return
                    w_ = nb_ * P
                    tb, tv = xnT(c * P, w_)
                    for qk in range(2):
                        for h in range(4):
                            bk = nextbank()
                            col = qk * 512 + h * P
                            mm_group(bank(bk)[:, 0:w_],
                                     [(Wm.ap[:, kc, col:col + P], tv[:, kc, :]) for kc in range(KC)],
                                     bank_u[bk], [(Wm.u, qk)] + [(tb.u, c + j) for j in range(nb_)])
                            sc_ = (P ** -0.5) if qk == 0 else 1.0
                            pg.add("act", lambda e, bk=bk, qk=qk, h=h, w_=w_, sc_=sc_: e.activation(
                                out=qkT.ap[:, qk, h, 0:w_], in_=bank(bk)[:, 0:w_], func=AF.Copy, scale=sc_),
                                reads=[bank_u[bk]], writes=[(qkT.u, (qk, h))])

                def part_k():
                    if own:
                        part_qk()
                    bk = nextbank()
                    if own:
                        eb0_, _ = tile_of[c]
                        lc_ = (c - NPRE - eb0_) * P
                        pbk = bank(bk).bitcast(BF16)

                        def f(e):
                            ins = None
                            for h in range(4):
                                ins = e.transpose(pbk[:, h * P:(h + 1) * P], qkT.ap[:, 1, h, lc_:lc_ + P], ident_b.ap)
                            return ins
                        pg.add("pe", f, reads=[(qkT.u, (1, h)) for h in range(4)] + [ident_b], writes=[bank_u[bk]])
                        ksrc = pbk
                    else:
                        mm_group(bank(bk), [(xv[:, kc, :], Wm.ap[:, kc, 512:1024]) for kc in range(KC)],
                                 bank_u[bk], [xdep, (Wm.u, 1)])
                        ksrc = bank(bk)
                    for h in range(4):
                        pg.add("dve", lambda e, h=h: e.tensor_scalar(
                            out=ks.ap[:, h, :], in0=ksrc[:, h * P:(h + 1) * P], scalar1=wk_t.ap[:, c, h:h + 1],
                            scalar2=None, op0=ALU.mult),
                            reads=[bank_u[bk], wk_t], writes=[(ks.u, h)])

                def part_v():
                    for hv in range(2):
                        bk = nextbank()
                        mm_group(bank(bk), [(xv[:, kc, :], Wm.ap[:, kc, 1024 + hv * 512:1536 + hv * 512]) for kc in range(KC)],
                                 bank_u[bk], [xdep, (Wm.u, 2 + hv)])
                        pg.add("act", lambda e, bk=bk, hv=hv: e.activation(
                            out=ve.ap[:, 2 * hv:2 * hv + 2, 0:256], in_=bank(bk).rearrange("p (a b) -> p a b", a=2),
                            func=AF.Copy),
                            reads=[bank_u[bk]], writes=[(ve.u, hv)])

                def part_o():
                    if not own:
                        return
                    gw = Gw[c % 2]
                    for ho in range(2):
                        bk = nextbank()
                        mm_group(bank(bk), [(xv[:, kc, :], Wm.ap[:, kc, 2048 + ho * 512:2560 + ho * 512]) for kc in range(KC)],
                                 bank_u[bk], [xdep, (Wm.u, 4 + ho)])
                        sl = slice(ho * 512, (ho + 1) * 512)
                        pg.add("act", lambda e, bk=bk, sl=sl: e.activation(out=gw.ap[:, sl], in_=bank(bk), func=AF.Tanh, scale=0.5),
                               reads=[bank_u[bk]], writes=[(gw.u, ho)])
                        pg.add("dve", lambda e, sl=sl: e.scalar_tensor_tensor(
                            out=gw.ap[:, sl], in0=gw.ap[:, sl], scalar=1.0, in1=halfw.ap[:, sl], op0=ALU.add, op1=ALU.mult),
                            reads=[(gw.u, ho), halfw], writes=[(gw.u, ho)])
                return [part_k, part_v, part_o]

            def I_stage(c):
                if c >= NB or c < NPRE:
                    return
                eb0, nb_ = tile_of[c]
                lc = (c - NPRE - eb0) * P
                eb_, ebm_, st_, qs_ = EB[c % 2], EBM[c % 2], ST[c % 2], qs[c % 2]
                for h in range(4):
                    mm_group(bank(SB0)[:, h * P:(h + 1) * P],
                             [(LFh.ap[:, c, h:h + 1].to_broadcast([P, P]), M_b.ap),
                              (LFl.ap[:, c, h:h + 1].to_broadcast([P, P]), M_b.ap)],
                             (bank_u[SB0], h), [LFh, LFl, M_b])
                for h in range(4):
                    mm_group(bank(SB1)[:, h * P:(h + 1) * P],
                             [(qkT.ap[:, 1, h, lc:lc + P], qkT.ap[:, 0, h, lc:lc + P])],
                             (bank_u[SB1], h), [(qkT.u, (0, h)), (qkT.u, (1, h))])
                for h in range(4):
                    pg.add("act", lambda e, h=h: e.activation(out=eb_.ap[:, h, :], in_=bank(SB0)[:, h * P:(h + 1) * P],
                                                              func=AF.Exp),
                           reads=[bank_u[SB0]], writes=[(eb_.u, h)])
                for h in range(4):
                    pg.add("dve", lambda e, h=h: e.tensor_tensor(out=ebm_.ap[:, h, :], in0=eb_.ap[:, h, :], in1=U_f.ap,
                                                                 op=ALU.mult),
                           reads=[(eb_.u, h), U_f], writes=[(ebm_.u, h)])
                    pg.add("dve", lambda e, h=h: e.tensor_tensor(
                        out=qs_.ap[:, h, :], in0=qkT.ap[:, 0, h, lc:lc + P], in1=eb_.ap[:, h, :], op=ALU.mult),
                        reads=[(eb_.u, h), (qkT.u, (0, h))], writes=[(qs_.u, h)])
                for h in range(4):
                    pg.add("dve", lambda e, h=h: e.scalar_tensor_tensor(
                        out=st_.ap[:, h, :], in0=bank(SB1)[:, h * P:(h + 1) * P], scalar=es_t.ap[:, c, h:h + 1],
                        in1=ebm_.ap[:, h, :], op0=ALU.mult, op1=ALU.mult),
                        reads=[bank_u[SB1], es_t, (ebm_.u, h)], writes=[(st_.u, h)])

            def R_num(c, h):
                ve = vext[c % 2]
                st_, qs_ = ST[c % 2], qs[c % 2]
                gw, hb_ = Gw[c % 2], hablk[c % 2]
                rb = RB0 if h % 2 == 0 else RB1
                nps = bank(rb)[:, 0:257]
                mm_group(nps, [(st_.ap[:, h, :], ve.ap[:, h, :]), (qs_.ap[:, h, :], CTb.ap[:, h, :])],
                         bank_u[rb], [(st_.u, h), (ve.u, h // 2), (ve.u, "ones"), (qs_.u, h), (CTb.u, h)])
                s0 = sc.ap[:, c % 2, h, :]
                su = (sc.u, (c % 2, h))
                pg.add("act", lambda e: e.activation(out=hb_.ap[:, h * 256:(h + 1) * 256], in_=nps[:, 0:256], func=AF.Square,
                                                     accum_out=s0[:, 0:1]),
                       reads=[bank_u[rb]], writes=[su, (hb_.u, h)])
                pg.add("act", lambda e: e.activation(out=s0[:, 1:2], in_=nps[:, 256:257], func=AF.Square),
                       reads=[bank_u[rb]], writes=[su])
                pg.add("dve", lambda e: e.tensor_scalar(out=s0[:, 2:3], in0=s0[:, 1:2], scalar1=1.0, scalar2=EPS,
                                                        op0=ALU.max, op1=ALU.mult),
                       reads=[su], writes=[su])
                pg.add("dve", lambda e: e.scalar_tensor_tensor(out=s0[:, 3:4], in0=s0[:, 0:1], scalar=1.0 / 256.0,
                                                               in1=s0[:, 2:3], op0=ALU.mult, op1=ALU.add),
                       reads=[su], writes=[su])
                pg.add("pool", lambda e: e.tensor_tensor(out=s0[:, 4:5], in0=s0[:, 3:4], in1=neghalf.ap, op=ALU.pow),
                       reads=[su, neghalf], writes=[su])
                pg.add("dve", lambda e: e.scalar_tensor_tensor(
                    out=hb_.ap[:, h * 256:(h + 1) * 256], in0=nps[:, 0:256], scalar=s0[:, 4:5],
                    in1=gw.ap[:, h * 256:(h + 1) * 256], op0=ALU.mult, op1=ALU.mult),
                    reads=[bank_u[rb], su, (gw.u, h // 2)], writes=[(hb_.u, h)])

            def R_upd(c, h):
                ks, ve = ksc[c % 2], vext[c % 2]
                rb = RB0 if h % 2 == 0 else RB1
                ups = bank(rb)[:, 0:257]
                mm_group(ups, [(ks.ap[:, h, :], ve.ap[:, h, :])], bank_u[rb],
                         [(ks.u, h), (ve.u, h // 2), (ve.u, "ones")])
                pg.add("dve", lambda e: e.scalar_tensor_tensor(
                    out=CT.ap[:, h, :], in0=CT.ap[:, h, :], scalar=a_t.ap[:, c, h:h + 1], in1=ups,
                    op0=ALU.mult, op1=ALU.add),
                    reads=[bank_u[rb], a_t, (CT.u, h)], writes=[(CT.u, h)])
                pg.add("act", lambda e: e.activation(out=CTb.ap[:, h, :], in_=CT.ap[:, h, :], func=AF.Copy),
                       reads=[(CT.u, h)], writes=[(CTb.u, h)])

            def T_stage(c):
                if c < NPRE or c >= NB:
                    return
                ec = c - NPRE
                hb_ = hablk[c % 2]
                transpose8(hb_.ap, hb_.u, TB, haT.ap[:, :, ec * P:(ec + 1) * P], (haT.u, ec))

            scr_u = pg.unit("wup_scr")
            scr_v = wup_scr.rearrange("p g (kc ag n) -> p g kc ag n", kc=KC, ag=2)
            conv_jobs = []
            for g_ in range(NG):
                def cj(g_=g_):
                    for ag_ in range(2):
                        pg.add("pool", lambda e, g_=g_, ag_=ag_: e.dma_start(
                            out=scr_v[:, g_, :, ag_, :], in_=wsrc(w_up_d, ag_ * D_FF + g_ * 256, 256)),
                            writes=[(scr_u, g_)], dma="scr%d" % (g_ % 4))
                conv_jobs.append(cj)
            for p_ in P_parts(0):
                p_()
            I_stage(0)
            for c in range(NB):
                own = c >= NPRE
                nxt = P_parts(c + 1) if c + 1 < NB else [lambda: None] * 3
                if own:
                    R_num(c, 0)
                    R_num(c, 1)
                    nxt[0]()
                    I_stage(c + 1)
                    R_num(c, 2)
                    R_num(c, 3)
                    nxt[1]()
                    R_upd(c, 0)
                    R_upd(c, 1)
                    nxt[2]()
                    R_upd(c, 2)
                    R_upd(c, 3)
                else:
                    R_upd(c, 0)
                    R_upd(c, 1)
                    nxt[0]()
                    I_stage(c + 1)
                    R_upd(c, 2)
                    R_upd(c, 3)
                    nxt[1]()
                    nxt[2]()
                T_stage(c - 1)
                if c % 2 == 0 and c // 2 < NG:
                    conv_jobs[c // 2]()
            T_stage(NB - 1)
            locals_ref[0] = locals()
            add_dumps("B")

        if nph >= 3:
            cw = Carver(132, ARENA_KIB)
            kT = [cw("kT%d" % i, [NB * P], BF16) for i in range(2)]
            vxp = cw("vxp", [NB, 2, 129], BF16)
            qT = [cw("qT%d" % i, [TEXT], BF16) for i in range(2)]
            Wf = [cw("Wf%d" % i, [KC, 256], BF16) for i in range(2)]
            Wfv = cw("Wfv", [KC, 256], BF16)
            pT = [cw("pT%d" % i, [512], BF16) for i in range(6)]
            fsc = cw("fsc", [4, 2], F32)
            pg.add("pool", lambda e: e.memset(vxp.ap[:, :, :, 128:129], 1.0), writes=[(vxp.u, "ones")])
            LG = [0, 1, 7]
            ACC = [2, 3, 4, 5]
            PBK = [6, 0, 1, 7]
            pbc = [0]

            def nextb():
                b_ = PBK[pbc[0] % 4]
                pbc[0] += 1
                return b_
            def evac(out_ap, in_ap, reads, writes):
                pg.add("dve", lambda e: e.tensor_copy(out=out_ap, in_=in_ap), reads=reads, writes=writes)
            WIN_TILES = [(0, 512), (512, 512), (1024, 512), (1536, 384)] + \
                        [(TPRE + eb0 * P, nb_ * P) for (eb0, nb_) in EXT_TILES]
            hbtok = [cw("hbtk%d" % i, [P], BF16) for i in range(4)]

            def proj_groups(h):
                s_ = h % 2
                wf = Wf[s_]
                out = []

                def ld():
                    for i, off in enumerate((OFF_QF, OFF_KF)):
                        wload(wf.ap[:, :, i * P:(i + 1) * P], wf.u, wsrc(w_in_d, off + h * P, P), "wf%d" % s_)
                out.append(ld)
                gidx = 0
                for (t0, w_) in WIN_TILES:
                    for which in ((1, 0) if t0 >= TPRE else (1,)):
                        gbk = PBK[gidx % 4] if h == 0 else 6
                        gidx += 1
                        for piece in range(4):
                            def pc(t0=t0, w_=w_, which=which, piece=piece, gbk=gbk):
                                tb, tv = xnT(t0, w_)
                                blks = list(range(t0 // P, (t0 + w_) // P))
                                bk = gbk

                                def f(e):
                                    ins = None
                                    for kc in (2 * piece, 2 * piece + 1):
                                        ins = e.matmul(bank(bk)[:, 0:w_], lhsT=wf.ap[:, kc, which * P:(which + 1) * P],
                                                       rhs=tv[:, kc, :], start=(kc == 0), stop=(kc == KC - 1))
                                    return ins
                                pg.add("pe", f, reads=[wf] + [(tb.u, j) for j in blks], writes=[bank_u[bk]])
                                if piece == 3:
                                    if which == 1:
                                        evac(kT[s_].ap[:, t0:t0 + w_], bank(bk)[:, 0:w_], [bank_u[bk]],
                                             [(kT[s_].u, j) for j in blks])
                                    else:
                                        evac(qT[s_].ap[:, t0 - TPRE:t0 - TPRE + w_], bank(bk)[:, 0:w_], [bank_u[bk]],
                                             [(qT[s_].u, j) for j in blks])
                            pc.closes = (piece == 3)
                            out.append(pc)
                return out

            def vpair_load(p_):
                if p_ >= 4:
                    return
                wload(Wfv.ap, Wfv.u, wsrc(w_in_d, OFF_VF + p_ * 2 * P, 2 * P), "wfv")

            def vpair_proj(p_):
                for g in range(NB // 2):
                    bk = nextb()
                    for jj in range(2):
                        j = g * 2 + jj
                        tb, tv = xnT(j * P, P)
                        mm_group(bank(bk)[:, jj * 256:(jj + 1) * 256],
                                 [(tv[:, kc, :], Wfv.ap[:, kc, :]) for kc in range(KC)],
                                 (bank_u[bk], jj), [Wfv, (tb.u, j)])
                    evac(vxp.ap[:, g * 2:(g + 1) * 2, :, 0:P],
                         bank(bk).rearrange("p (a b c) -> p a b c", a=2, b=2),
                         [bank_u[bk]], [(vxp.u, g)])
            vpair_load(0)
            pgs = {h_: proj_groups(h_) for h_ in range(8)}
            lds = {h_: pgs[h_].pop(0) for h_ in range(8)}
            lds[0]()
            for f_ in pgs[0]:
                f_()
            lds[1]()
            gstep = [0]
            for h in range(8):
                s_ = h % 2
                pend = pgs[h + 1] if h + 1 < 8 else []
                if h + 2 < 8:
                    lds[h + 2]()
                if h % 2 == 0:
                    vpair_proj(h // 2)
                    vpair_load(h // 2 + 1)
                npend = len(pend)
                HOLD = 0
                steps = []
                for ti, (eb0, nb_) in enumerate(EXT_TILES):
                    b0 = NPRE + eb0
                    for j in range(b0 + nb_):
                        steps.append((ti, eb0, nb_, b0, j))
                nst = len(steps)

                def qk(n):
                    ti, eb0, nb_, b0, j = steps[n]
                    w_ = nb_ * P
                    c0 = max(0, j - b0) * P
                    lgb = LG[(gstep[0] + n) % 3]
                    mm_group(bank(lgb)[:, 0:w_ - c0],
                             [(kT[s_].ap[:, j * P:(j + 1) * P], qT[s_].ap[:, eb0 * P + c0:eb0 * P + w_])],
                             bank_u[lgb], [(kT[s_].u, j)] + [(qT[s_].u, b0 + i) for i in range(nb_)])
                qk(0)
                if nst > 1:
                    qk(1)
                deferred = []
                grp_open = [False]
                for n in range(nst):
                    ti, eb0, nb_, b0, j = steps[n]
                    w_ = nb_ * P
                    c0 = max(0, j - b0) * P
                    if n + 2 < nst:
                        qk(n + 2)
                    want = (max(0, n + 1 - HOLD) * npend + (nst - HOLD) - 1) // (nst - HOLD)
                    while npend - len(pend) < want and pend:
                        f_ = pend.pop(0)
                        f_()
                        grp_open[0] = not getattr(f_, "closes", True)
                    lgb = LG[(gstep[0] + n) % 3]
                    pt = pT[(gstep[0] + n) % 6]
                    pg.add("act", lambda e, pt=pt, lgb=lgb, c0=c0, w_=w_, ti=ti, j=j, h=h: e.activation(
                        out=pt.ap[:, c0:w_], in_=bank(lgb)[:, 0:w_ - c0], func=AF.Exp, scale=float(P) ** -0.5,
                        bias=bias_all.ap[:, ti, j, h:h + 1]),
                        reads=[bank_u[lgb], (bias_all.u, ti)], writes=[pt])
                    if j >= b0:
                        pg.add("pool", lambda e, pt=pt, c0=c0: e.tensor_tensor(
                            out=pt.ap[:, c0:c0 + P], in0=pt.ap[:, c0:c0 + P], in1=M_b.ap, op=ALU.mult),
                            reads=[pt, M_b], writes=[pt])
                    act_i = [i for i in range(nb_) if i * P >= c0]

                    def pv(e, pt=pt, j=j, b0=b0, act_i=act_i, hp=h % 2):
                        ins = None
                        for i in act_i:
                            ins = e.matmul(bank(ACC[i])[:, 0:129], lhsT=pt.ap[:, i * P:(i + 1) * P], rhs=vxp.ap[:, j, hp, :],
                                           start=(j == 0), stop=(j == b0 + i))
                        return ins
                    pg.add("pe", pv, reads=[pt, (vxp.u, j // 2), (vxp.u, "ones")],
                           writes=[bank_u[ACC[i]] for i in act_i])
                    while deferred and deferred[0][0] <= n and not grp_open[0]:
                        deferred.pop(0)[1]()
                    if j >= b0:
                        i = j - b0
                        ab = ACC[i]
                        hbk = hbtok[(gstep[0] + n) % 4]
                        pg.add("dve", lambda e, ab=ab, i=i: e.tensor_scalar(
                            out=fsc.ap[:, i, 0:1], in0=bank(ab)[:, 128:129], scalar1=1e-30, scalar2=None, op0=ALU.max),
                            reads=[bank_u[ab]], writes=[(fsc.u, i)])
                        pg.add("dve", lambda e, i=i: e.reciprocal(out=fsc.ap[:, i, 1:2], in_=fsc.ap[:, i, 0:1]),
                               reads=[(fsc.u, i)], writes=[(fsc.u, i)])
                        pg.add("dve", lambda e, ab=ab, i=i, hbk=hbk: e.tensor_scalar(
                            out=hbk.ap, in0=bank(ab)[:, 0:P], scalar1=fsc.ap[:, i, 1:2], scalar2=None, op0=ALU.mult),
                            reads=[bank_u[ab], (fsc.u, i)], writes=[hbk])

                        def tr(hbk=hbk, h=h, col=(eb0 + i) * P, key=(h, eb0 + i)):
                            tbk = 6
                            transpose8(hbk.ap, hbk.u, tbk, hbT.ap[:, h, col:col + P], (hbT.u, key), n=1, evac="dve")
                        deferred.append((n + 2, tr))
                for f_ in pend:
                    f_()
                for (_, f_) in deferred:
                    f_()
                gstep[0] += nst
            locals_ref[0] = locals()
            add_dumps("C")

        if nph >= 4:
            cw = Carver(132, 190)
            mergedT = cw("mergedT", [KC, TEXT], BF16)
            Wd = [cw("Wd%d" % i, [4, KC, P], BF16) for i in range(2)]
            cw = Carver(34, 64)
            sga = cw("sga", [512], F32)
            sgb = cw("sgb", [512], F32)
            m1 = cw("m1", [512], F32)
            m2 = cw("m2", [512], F32)
            stg = cw("stg", [5, P], F32)
            for i, (src, n) in enumerate(((bg_d, 16), (convw_d[0:44, :], 44), (convw_d[44:88, :], 44),
                                          (convw_d[88:132, :], 44), (convb_d, 44))):
                pg.add("sp", lambda e, i=i, src=src, n=n: e.dma_start(out=stg.ap[0:n, i, :], in_=src),
                       writes=[(stg.u, i)], dma="c_stg")
            for i, (dst, n) in enumerate(((bg_pp.ap, 16), (cw_pp.ap[:, 0, :], 44), (cw_pp.ap[:, 1, :], 44),
                                          (cw_pp.ap[:, 2, :], 44), (cb_pp.ap, 44))):
                pg.add("pe", lambda e, i=i, n=n: e.transpose(bank(7)[:, i * 64:i * 64 + n], stg.ap[0:n, i, :],
                                                             ident_f.ap[0:n, 0:n]),
                       reads=[(stg.u, i), ident_f], writes=[(bank_u[7], i)])
                pg.add("dve", lambda e, i=i, n=n, dst=dst: e.tensor_copy(out=dst, in_=bank(7)[:, i * 64:i * 64 + n]),
                       reads=[bank_u[7]], writes=[bg_pp if i == 0 else (cb_pp if i == 4 else (cw_pp.u, i))])

            dstep = 0

            wstg = cw("wstg", [4, KC, P], F32)

            def wd_load(cc_):
                if cc_ >= KC:
                    return
                wd_ = Wd[cc_ % 2]
                for i_, (wdr, off) in enumerate(((w_a_d, cc_ * P), (w_b_d, cc_ * P), (w_in_d, OFF_GA + cc_ * P),
                                                 (w_in_d, OFF_GB + cc_ * P))):
                    pg.add("sp", lambda e, i_=i_, wdr=wdr, off=off: e.dma_start(out=wstg.ap[:, i_, :, :], in_=wsrc(wdr, off, P)),
                           writes=[(wstg.u, i_)], dma="wstg%d" % i_)
                    eng_ = "pool" if i_ % 2 == 0 else "dve"
                    pg.add(eng_, lambda e, i_=i_: e.tensor_copy(out=wd_.ap[:, i_, :, :], in_=wstg.ap[:, i_, :, :]),
                           reads=[(wstg.u, i_)], writes=[(wd_.u, i_)])
            wd_load(0)
            for cc in range(KC):
                wd = Wd[cc % 2]
                wd_load(cc + 1)
                for (eb0, nb_) in EXT_TILES:
                    w_ = nb_ * P
                    t0 = eb0 * P
                    bs = [0, 1, 2, 3] if dstep % 2 == 0 else [4, 5, 6, 7]
                    dstep += 1
                    srcs = (haT, hbT, xnT_own, xnT_own)
                    for i in range(4):
                        src = srcs[i]
                        if src is xnT_own:
                            rd = [(src.u, NPRE + eb0 + j) for j in range(nb_)]
                        elif src is haT:
                            rd = [(src.u, eb0 + j) for j in range(nb_)]
                        else:
                            rd = [src]
                        mm_group(bank(bs[i])[:, 0:w_],
                                 [(wd.ap[:, i, kc, :], src.ap[:, kc, t0:t0 + w_]) for kc in range(KC)],
                                 bank_u[bs[i]], [(wd.u, i)] + rd)
                    pg.add("act", lambda e, bs=bs, w_=w_, cc=cc: e.activation(
                        out=sga.ap[:, 0:w_], in_=bank(bs[2])[:, 0:w_], func=AF.Sigmoid, bias=bg_pp.ap[:, cc:cc + 1]),
                        reads=[bank_u[bs[2]], bg_pp], writes=[sga])
                    pg.add("act", lambda e, bs=bs, w_=w_, cc=cc: e.activation(
                        out=sgb.ap[:, 0:w_], in_=bank(bs[3])[:, 0:w_], func=AF.Sigmoid, bias=bg_pp.ap[:, 8 + cc:9 + cc]),
                        reads=[bank_u[bs[3]], bg_pp], writes=[sgb])
                    pg.add("dve", lambda e, bs=bs, w_=w_: e.tensor_tensor(out=m1.ap[:, 0:w_], in0=bank(bs[0])[:, 0:w_],
                                                                          in1=sga.ap[:, 0:w_], op=ALU.mult),
                           reads=[bank_u[bs[0]], sga], writes=[m1])
                    pg.add("dve", lambda e, bs=bs, w_=w_: e.tensor_tensor(out=m2.ap[:, 0:w_], in0=bank(bs[1])[:, 0:w_],
                                                                          in1=sgb.ap[:, 0:w_], op=ALU.mult),
                           reads=[bank_u[bs[1]], sgb], writes=[m2])
                    pg.add("pool", lambda e, w_=w_, cc=cc, t0=t0: e.tensor_tensor(
                        out=mergedT.ap[:, cc, t0:t0 + w_], in0=m1.ap[:, 0:w_], in1=m2.ap[:, 0:w_], op=ALU.add),
                        reads=[m1, m2], writes=[(mergedT.u, (cc, eb0))])
            locals_ref[0] = locals()
            add_dumps("D1")

        if nph >= 5:
            x1 = Carver(0, 68)("x1", [NEXT, D], F32)
            cw = Carver(68, 132)
            wpost_bc = cw("wpost_bc", [D], F32)
            tmpd = [cw("tmpd%d" % i, [D], F32) for i in range(2)]
            xr = [cw("xr%d" % i, [2, D], F32) for i in range(3)]
            sqd = cw("sqd", [D], BF16)
            cw = Carver(166, 190)
            Wout = cw("Wout", [KC, D], BF16)
            bcast_load(wpost_bc, "norm_mix_post")
            for i in range(2):
                wload(Wout.ap[:, :, i * 512:(i + 1) * 512], (Wout.u, i), wsrc(w_out_d, i * 512, 512), "wout%d" % i)
            for eb in range(NEXT):
                bs = [0, 1] if eb % 2 == 0 else [2, 3]
                for hh in range(2):
                    mm_group(bank(bs[hh]),
                             [(mergedT.ap[:, kc, eb * P:(eb + 1) * P], Wout.ap[:, kc, hh * 512:(hh + 1) * 512])
                              for kc in range(KC)],
                             bank_u[bs[hh]], [mergedT, (Wout.u, hh)])
                    c_ = 32 + 2 * eb + hh
                    pg.add("act", lambda e, b_=bs[hh], c_=c_, hh=hh, tm=tmpd[eb % 2]: e.activation(
                        out=tm.ap[:, hh * 512:(hh + 1) * 512], in_=bank(b_), func=AF.Square, accum_out=ssq.ap[:, c_:c_ + 1]),
                        reads=[bank_u[bs[hh]]], writes=[(ssq.u, c_), (tmpd[eb % 2].u, hh)])
                cs = 66 + eb
                pg.add("dve", lambda e, eb=eb, cs=cs: e.tensor_tensor(
                    out=ssq.ap[:, cs:cs + 1], in0=ssq.ap[:, 32 + 2 * eb:33 + 2 * eb], in1=ssq.ap[:, 33 + 2 * eb:34 + 2 * eb],
                    op=ALU.add),
                    reads=[(ssq.u, 32 + 2 * eb), (ssq.u, 33 + 2 * eb)], writes=[(ssq.u, cs)])
                rms_rstd(cs, 1.0 / D)
                grp = (eb + 1) // 2
                xrg, tm = xr[grp % 3], tmpd[eb % 2]
                kk = (eb + 1) % 2
                if eb == 0:
                    pg.add("sp", lambda e, xrg=xrg: e.dma_start(out=xrg.ap[:, 1, :], in_=x_d[NPRE * P:(NPRE + 1) * P, :]),
                           writes=[xrg], dma="xr%d" % (grp % 3))
                elif kk == 0:
                    pg.add("sp", lambda e, xrg=xrg, eb=eb: e.dma_start(
                        out=xrg.ap, in_=x_d[(NPRE + eb) * P:(NPRE + eb + 2) * P, :].rearrange("(k p) d -> p k d", p=P)),
                        writes=[xrg], dma="xr%d" % (grp % 3))
                xrb_ap = xrg.ap[:, kk, :]
                for hh in range(2):
                    pg.add("dve", lambda e, b_=bs[hh], hh=hh, tm=tm, cs=cs: e.scalar_tensor_tensor(
                        out=tm.ap[:, hh * 512:(hh + 1) * 512], in0=bank(b_), scalar=rstd.ap[:, cs:cs + 1],
                        in1=wpost_bc.ap[:, hh * 512:(hh + 1) * 512], op0=ALU.mult, op1=ALU.mult),
                        reads=[bank_u[bs[hh]], (rstd.u, cs), wpost_bc], writes=[(tm.u, hh)])
                pg.add("pool", lambda e, tm=tm, xrb_ap=xrb_ap, eb=eb: e.tensor_tensor(
                    out=x1.ap[:, eb, :], in0=tm.ap, in1=xrb_ap, op=ALU.add),
                    reads=[tm, xrg], writes=[(x1.u, eb)])
                pg.add("act", lambda e, eb=eb: e.activation(out=sqd.ap, in_=x1.ap[:, eb, :], func=AF.Square,
                                                            accum_out=ssq.ap[:, eb:eb + 1]),
                       reads=[(x1.u, eb)], writes=[(ssq.u, eb), sqd])
                rms_rstd(eb, 1.0 / D)
            locals_ref[0] = locals()
            add_dumps("D2")

        if nph >= 6:
            cw = Carver(68, ARENA_KIB)
            Wdn = cw("Wdn", [NFC, D], BF16)
            actb = cw("actb", [NFC, 512], BF16)
            xn2T = cw("xn2T", [KC, 512], BF16)
            Wup = [cw("Wup%d" % i, [KC, 2, 256], BF16) for i in range(2)]
            yag = [cw("yag%d" % i, [2, 512], F32) for i in range(3)]
            glb = [cw("glb%d" % i, [512], F32) for i in range(2)]
            wfpre_bc = cw("wfpre_bc", [D], F32)
            wfpost_bc = cw("wfpost_bc", [D], F32)
            xs2 = [cw("xs2%d" % i, [D], BF16) for i in range(2)]
            tmpe = [cw("tmpe%d" % i, [D], F32) for i in range(1)]
            bcast_load(wfpre_bc, "norm_ffn_pre")
            bcast_load(wfpost_bc, "norm_ffn_post")
            wdn_src = w_dn_d.rearrange("(fc p) n -> p fc n", p=P)
            for i, (f0, f1) in enumerate(((0, 6), (6, 12), (12, 18), (18, 22))):
                wload(Wdn.ap[:, f0:f1, :], (Wdn.u, i), wdn_src[:, f0:f1, :], "wdn%d" % i)
            Wdn_deps = [(Wdn.u, i) for i in range(4)]
            USETS = [(0, 1), (2, 3), (4, 5)]
            OSETS = [(6, 7), (0, 1)]
            TBKS = [6, 7]
            sqj = cw("sqj", [D], BF16)
            hA = SB("hA", [44, 2], F32)
            hB = SB("hB", [44], F32)
            GROUPS = [(ti_, fg_) for ti_ in range(len(EXT_TILES)) for fg_ in range(NFC // 2)]

            def load_group(k):
                if k >= len(GROUPS):
                    return
                fg_ = GROUPS[k][1]
                wu_ = Wup[k % 2]
                pg.add("sp", lambda e, wu_=wu_, fg_=fg_: e.dma_start(
                    out=wu_.ap.rearrange("p a b c -> p (a b c)"), in_=wup_scr[:, fg_, :]),
                    reads=[(scr_u, fg_)], writes=[wu_], dma="wup%d" % (k % 2))
            load_group(0)
            gn = 0

            def E1(ti_):
                if ti_ >= len(EXT_TILES):
                    return
                eb0_, nbt_ = EXT_TILES[ti_]
                for bl in range(nbt_):
                    eb = eb0_ + bl
                    xsb = xs2[eb % 2]
                    pg.add("dve", lambda e, eb=eb, xsb=xsb: e.scalar_tensor_tensor(
                        out=xsb.ap, in0=x1.ap[:, eb, :], scalar=rstd.ap[:, eb:eb + 1], in1=wfpre_bc.ap,
                        op0=ALU.mult, op1=ALU.mult),
                        reads=[(x1.u, eb), (rstd.u, eb), wfpre_bc], writes=[xsb])
                    transpose8(xsb.ap, xsb.u, TBKS[eb % 2], xn2T.ap[:, :, bl * P:(bl + 1) * P], (xn2T.u, bl), evac="act")
            E1(0)
            ocnt = 0
            for ti, (eb0, nb_) in enumerate(EXT_TILES):
                w_ = nb_ * P
                xn_deps = [(xn2T.u, bl) for bl in range(nb_)]
                first = ti == 0
                last = ti + 1 == len(EXT_TILES)
                if not first:
                    hup = halo_u.ap[:, (ti - 1) % 2, :, :]
                    pg.add("dve", lambda e, hup=hup: e.tensor_tensor(
                        out=hA.ap, in0=hup, in1=cw_pp.ap[:, 0, :].unsqueeze(2).to_broadcast([P, 44, 2]), op=ALU.mult),
                        reads=[halo_u, cw_pp], writes=[hA])
                    pg.add("dve", lambda e, hup=hup: e.tensor_tensor(
                        out=hB.ap, in0=hup[:, :, 1], in1=cw_pp.ap[:, 1, :], op=ALU.mult),
                        reads=[halo_u, cw_pp], writes=[hB])
                    pg.add("dve", lambda e: e.tensor_tensor(out=hA.ap[:, :, 0], in0=hA.ap[:, :, 0], in1=hB.ap, op=ALU.add),
                           reads=[hA, hB], writes=[hA])

                def S0(fi, gi):
                    fg, fl = fi // 2, fi % 2
                    k = ti * (NFC // 2) + fg
                    wu = Wup[k % 2]
                    if fl == 0:
                        load_group(k + 1)
                    bs = USETS[gi % 3]
                    c_lo = w_ - 2 if first else 0
                    for ag in range(2):
                        mm_group(bank(bs[ag])[:, c_lo:w_],
                                 [(wu.ap[:, kc, ag, fl * P:(fl + 1) * P], xn2T.ap[:, kc, c_lo:w_]) for kc in range(KC)],
                                 bank_u[bs[ag]], [wu] + xn_deps)

                def S1(fi, gi):
                    bs = USETS[gi % 3]
                    yb = yag[gi % 3]
                    for ag in range(2):
                        cidx = ag * NFC + fi
                        ups = bank(bs[ag])
                        if not last:
                            pg.add("act", lambda e, ups=ups, cidx=cidx, ti=ti, w_=w_: e.activation(
                                out=halo_u.ap[:, ti % 2, cidx, :], in_=ups[:, w_ - 2:w_], func=AF.Copy),
                                reads=[bank_u[bs[ag]]], writes=[(halo_u.u, (ti % 2, cidx))])
                        if first:
                            continue
                        pg.add("act", lambda e, ups=ups, yb=yb, cidx=cidx, ag=ag, w_=w_: e.activation(
                            out=yb.ap[:, ag, 0:w_], in_=ups[:, 0:w_], func=AF.Identity, scale=cw_pp.ap[:, 2, cidx:cidx + 1],
                            bias=cb_pp.ap[:, cidx:cidx + 1]),
                            reads=[bank_u[bs[ag]], cw_pp, cb_pp], writes=[(yb.u, ag)])
                    if not first:
                        hv = hA.ap.rearrange("p (a f) c -> p a f c", a=2)[:, :, fi, :]
                        pg.add("pool", lambda e, yb=yb, hv=hv: e.tensor_tensor(
                            out=yb.ap[:, :, 0:2], in0=yb.ap[:, :, 0:2], in1=hv, op=ALU.add),
                            reads=[hA, (yb.u, 0), (yb.u, 1)], writes=[(yb.u, 0), (yb.u, 1)])

                def S2(fi, gi):
                    if first:
                        return
                    bs = USETS[gi % 3]
                    yb = yag[gi % 3]
                    for ag in range(2):
                        cidx = ag * NFC + fi
                        ups = bank(bs[ag])
                        for (sh, jw) in ((1, 1), (2, 0)):
                            pg.add("dve", lambda e, ups=ups, yb=yb, cidx=cidx, sh=sh, jw=jw, ag=ag, w_=w_: e.scalar_tensor_tensor(
                                out=yb.ap[:, ag, sh:w_], in0=ups[:, 0:w_ - sh], scalar=cw_pp.ap[:, jw, cidx:cidx + 1],
                                in1=yb.ap[:, ag, sh:w_], op0=ALU.mult, op1=ALU.add),
                                reads=[bank_u[bs[ag]], cw_pp, (yb.u, ag)], writes=[(yb.u, ag)])

                def S3(fi, gi):
                    if first:
                        return
                    yb = yag[gi % 3]
                    gl = glb[gi % 2]
                    pg.add("act", lambda e, yb=yb, gl=gl, w_=w_: e.activation(out=gl.ap[:, 0:w_], in_=yb.ap[:, 1, 0:w_],
                                                                       func=AF.Gelu_apprx_tanh),
                           reads=[(yb.u, 1)], writes=[gl])
                    pg.add("pool", lambda e, yb=yb, gl=gl, fi=fi, w_=w_: e.tensor_tensor(
                        out=actb.ap[:, fi, 0:w_], in0=gl.ap[:, 0:w_], in1=yb.ap[:, 0, 0:w_], op=ALU.mult),
                        reads=[gl, (yb.u, 0)], writes=[(actb.u, fi)])

                for w in range(NFC + 2):
                    if w < NFC:
                        S0(w, gn + w)
                    if w == NFC - 1:
                        E1(ti + 1)
                    if 0 <= w - 1 < NFC:
                        S1(w - 1, gn + w - 1)
                    if 0 <= w - 2 < NFC:
                        S2(w - 2, gn + w - 2)
                        S3(w - 2, gn + w - 2)
                gn += NFC
                if first:
                    continue
                osets = [USETS[gn % 3], (6, 7)]
                SPLIT = 16

                def e3_mm(bl, hh, bs, f0, f1):
                    def f(e):
                        ins = None
                        for fc in range(f0, f1):
                            ins = e.matmul(bank(bs[hh]), lhsT=actb.ap[:, fc, bl * P:(bl + 1) * P],
                                           rhs=Wdn.ap[:, fc, hh * 512:(hh + 1) * 512], start=(fc == 0), stop=(fc == NFC - 1))
                        return ins
                    pg.add("pe", f, reads=[(actb.u, fc) for fc in range(f0, f1)] + Wdn_deps, writes=[bank_u[bs[hh]]])
                for bl in range(nb_):
                    eb = eb0 + bl
                    bs = osets[bl % 2]
                    tme = tmpe[0]
                    if bl == 0:
                        for hh in range(2):
                            e3_mm(bl, hh, bs, 0, SPLIT)
                        for hh in range(2):
                            e3_mm(bl, hh, bs, SPLIT, NFC)
                    else:
                        for hh in range(2):
                            e3_mm(bl, hh, bs, 0, NFC)
                    for hh in range(2):
                        c_ = 32 + 2 * eb + hh
                        pg.add("act", lambda e, b_=bs[hh], c_=c_, hh=hh, tme=tme: e.activation(
                            out=tme.ap[:, hh * 512:(hh + 1) * 512], in_=bank(b_), func=AF.Square,
                            accum_out=ssq.ap[:, c_:c_ + 1]),
                            reads=[bank_u[bs[hh]]], writes=[(ssq.u, c_), (tme.u, hh)])
                    cs = 66 + eb
                    pg.add("dve", lambda e, eb=eb, cs=cs: e.tensor_tensor(
                        out=ssq.ap[:, cs:cs + 1], in0=ssq.ap[:, 32 + 2 * eb:33 + 2 * eb],
                        in1=ssq.ap[:, 33 + 2 * eb:34 + 2 * eb], op=ALU.add),
                        reads=[(ssq.u, 32 + 2 * eb), (ssq.u, 33 + 2 * eb)], writes=[(ssq.u, cs)])
                    rms_rstd(cs, 1.0 / D)
                    for hh in range(2):
                        pg.add("dve", lambda e, b_=bs[hh], hh=hh, cs=cs, tme=tme: e.scalar_tensor_tensor(
                            out=tme.ap[:, hh * 512:(hh + 1) * 512], in0=bank(b_), scalar=rstd.ap[:, cs:cs + 1],
                            in1=wfpost_bc.ap[:, hh * 512:(hh + 1) * 512], op0=ALU.mult, op1=ALU.mult),
                            reads=[bank_u[bs[hh]], (rstd.u, cs), wfpost_bc], writes=[(tme.u, hh)])
                    pg.add("pool", lambda e, eb=eb, tme=tme: e.tensor_tensor(out=x1.ap[:, eb, :], in0=x1.ap[:, eb, :],
                                                                             in1=tme.ap, op=ALU.add),
                           reads=[tme, (x1.u, eb)], writes=[(x1.u, eb)])
                    pg.add("sp", lambda e, eb=eb: e.dma_start(out=out_d[(eb - 1) * P:eb * P, :], in_=x1.ap[:, eb, :]),
                           reads=[(x1.u, eb)], writes=[(out_u, "o%d" % eb)], dma="out%d" % (eb % 4))
            locals_ref[0] = locals()
            add_dumps("E")

        pg.add("sp", lambda e: e.nop(), reads=[out_u], name="final")
        nsem = pg.emit(nc, stack)
    return nc, dbg_d, nsem, pg.nops


def make_in_maps(inputs, cores=range(8)):
    x = np.asarray(inputs["x"], dtype=np.float32)
    g = lambda k: np.ascontiguousarray(np.asarray(inputs[k], dtype=np.float32)[0])
    shared = {
        "w_in": g("w_in"), "w_branch_a": g("w_branch_a"), "w_branch_b": g("w_branch_b"),
        "w_out": g("w_out"), "w_up": g("w_up"), "w_down": g("w_down"),
        "b_gate_ab": np.ascontiguousarray(np.concatenate([g("b_gate_a").reshape(8, P), g("b_gate_b").reshape(8, P)], 0)),
        "conv_w": np.ascontiguousarray(g("conv_w").reshape(3 * 44, P)),
        "conv_b": np.ascontiguousarray(g("conv_b").reshape(44, P)),
    }
    for nm in ("norm_mix_pre", "b_ml_i", "b_ml_f", "ml_head_norm", "b_fox_f", "norm_mix_post",
               "norm_ffn_pre", "norm_ffn_post"):
        shared[nm] = np.ascontiguousarray(g(nm).reshape(1, -1))
    maps = []
    for core in cores:
        b, hf = core // 2, core % 2
        km = np.zeros((P, NB), np.float32)
        if hf == 1:
            xa = np.ascontiguousarray(x[b])
        else:
            xa = np.concatenate([np.zeros((2048, D), np.float32), x[b, :2048]], 0)
            km[:, :16] = MASKNEG
        m = dict(shared)
        m["x"] = np.ascontiguousarray(xa)
        m["kmask"] = km
        maps.append(m)
    return maps


_CACHE = {}


def kernel(**inputs):
    if "nc" not in _CACHE:
        _CACHE["nc"] = build_program()[0]
    nc = _CACHE["nc"]
    maps = make_in_maps(inputs)
    res = run_bass_kernel_spmd(nc, maps, core_ids=list(range(8)))
    out = np.zeros((4, 4096, D), np.float32)
    for core in range(8):
        b, hf = core // 2, core % 2
        out[b, hf * 2048:(hf + 1) * 2048] = res.results[core]["out"]
    return out
```
